# Optimizing a Trainium2 kernel written in Bass

```python
import math
import jax, jax.numpy as jnp
from jax import lax
import numpy as np

D_MODEL = 1024
BATCH = 8
SEQ = 4096
DEPTH = 4

N_MIXERS = 3
REL_BUCKETS = 32
REL_EXACT = REL_BUCKETS // 2
REL_MAX_DIST = 128
ATTN_HEADS = 16
HEAD_DIM = D_MODEL // ATTN_HEADS
NSA_KV_HEADS = 4
NSA_QPG = ATTN_HEADS // NSA_KV_HEADS
CMP_STRIDE = 16
CMP_LEN = 2 * CMP_STRIDE
CMP_HIDDEN = 2 * HEAD_DIM
SEL_BLOCK = 64
SEL_COUNT = 16
NSA_WINDOW = 512
NSA_QBLOCK = 64
NSA_IN = ATTN_HEADS * HEAD_DIM + 6 * NSA_KV_HEADS * HEAD_DIM + 3 * ATTN_HEADS
HGRN_HEADS = 8
HGRN_KDIM = D_MODEL // HGRN_HEADS
HGRN_VDIM = D_MODEL // HGRN_HEADS
HGRN_CHUNK = 64
SWA_KV_HEADS = 2
SWA_QPG = ATTN_HEADS // SWA_KV_HEADS
SWA_WINDOW = 128
SWA_BLOCK = SWA_WINDOW
SWA_IN = ATTN_HEADS * HEAD_DIM + 2 * SWA_KV_HEADS * HEAD_DIM
D_FF = 2816
LN_EPS = 1e-5
RMS_EPS = 1e-6
NEG_INF = -1e30
N_NSA = len(range(0, DEPTH, N_MIXERS))
N_HGRN = len(range(1, DEPTH, N_MIXERS))
N_SWA = len(range(2, DEPTH, N_MIXERS))

kernel_name = 'hybrid_nsa_hgrn2_swa_macaron_deepnorm'

F32 = jnp.float32


def _layer_norm(x, g, b):
    xf = x.astype(F32)
    mu = jnp.mean(xf, axis=-1, keepdims=True)
    xc = xf - mu
    var = jnp.mean(xc * xc, axis=-1, keepdims=True)
    return (xc * lax.rsqrt(var + LN_EPS) * g.astype(F32) + b.astype(F32)).astype(x.dtype)


def _swiglu(x, w_gate, w_up, w_down):
    return (jax.nn.silu(x @ w_gate) * (x @ w_up)) @ w_down


def _t5_bucket(dist):
    n = jnp.maximum(dist, 0)
    nf = jnp.maximum(n, 1).astype(F32)
    large = REL_EXACT + (jnp.log(nf / REL_EXACT) / math.log(REL_MAX_DIST / REL_EXACT)
                         * (REL_BUCKETS - REL_EXACT)).astype(jnp.int32)
    return jnp.where(n < REL_EXACT, n, jnp.minimum(large, REL_BUCKETS - 1))


def _rel_bias(table, dist):
    return jnp.moveaxis(table[_t5_bucket(dist)], -1, 0).astype(F32)


def _masked_softmax(logits, valid):
    p = jax.nn.softmax(jnp.where(valid, logits.astype(F32), NEG_INF), axis=-1)
    return jnp.where(valid, p, 0.0)


def _lower_bounds(lb_param):
    sm = jax.nn.softmax(lb_param.astype(F32), axis=0)
    return jnp.cumsum(sm, axis=0) - sm[0]


def _nsa(x, w_in, w_out, cmp_pos, cmp_w1, cmp_w2, rel_bias):
    B, T, _ = x.shape
    dt = x.dtype
    G, Q, Dh = NSA_KV_HEADS, NSA_QPG, HEAD_DIM
    n_cmp = T // CMP_STRIDE - 1
    n_blk = T // SEL_BLOCK
    n_sel = min(SEL_COUNT, n_blk)
    n_qb = T // NSA_QBLOCK
    scale = Dh ** -0.5
    kvw = G * Dh
    h = x @ w_in
    sizes = [ATTN_HEADS * Dh] + [kvw] * 6
    q, k_c, v_c, k_s, v_s, k_w, v_w, gate = jnp.split(h, np.cumsum(sizes).tolist(), axis=-1)

    def heads(z):
        return z.reshape(B, T, G, Dh)

    def compress(z, pos, w1, w2):
        ch = heads(z).reshape(B, T // CMP_STRIDE, CMP_STRIDE, G, Dh)
        blk = jnp.concatenate([ch[:, :-1], ch[:, 1:]], axis=2) + pos[:, None, :]
        hid = jax.nn.gelu(jnp.einsum('bnlgd,lde->bnge', blk, w1))
        return jnp.einsum('bnge,ed->bgnd', hid, w2)

    kc = compress(k_c, cmp_pos[0], cmp_w1[0], cmp_w2[0])
    vc = compress(v_c, cmp_pos[1], cmp_w1[1], cmp_w2[1])
    cmp_end = CMP_STRIDE * jnp.arange(n_cmp, dtype=jnp.int32) + CMP_LEN - 1
    ci = np.arange(n_cmp)[:, None]
    bj = np.arange(n_blk)[None, :]
    overlap = jnp.asarray(((CMP_STRIDE * ci < SEL_BLOCK * (bj + 1)) &
                           (CMP_STRIDE * ci + CMP_LEN > SEL_BLOCK * bj)).astype(np.float32))

    def sel_blocks(z):
        return heads(z).reshape(B, n_blk, SEL_BLOCK, G, Dh).transpose(0, 3, 1, 2, 4)

    ks, vs = sel_blocks(k_s), sel_blocks(v_s)

    def win_keys(z):
        return jnp.pad(heads(z).transpose(0, 2, 1, 3), ((0, 0), (0, 0), (NSA_WINDOW, 0), (0, 0)))

    kw, vw = win_keys(k_w), win_keys(v_w)
    q_blocks = q.reshape(B, n_qb, NSA_QBLOCK, G, Q, Dh).transpose(1, 0, 3, 4, 2, 5)
    g_blocks = jax.nn.sigmoid(gate.astype(F32)).reshape(B, n_qb, NSA_QBLOCK, G, Q, 3).transpose(1, 0, 3, 4, 2, 5)
    bi = jnp.arange(B)[:, None, None, None]
    gi = jnp.arange(G)[None, :, None, None]
    tab_g = rel_bias.reshape(REL_BUCKETS, G, Q).transpose(1, 0, 2)
    blk_ids = jnp.arange(n_blk, dtype=jnp.int32)
    tok_in_blk = jnp.arange(SEL_BLOCK, dtype=jnp.int32)
    win_off = jnp.arange(NSA_WINDOW + NSA_QBLOCK, dtype=jnp.int32)

    def block(args):
        qb, gb, qi = args
        t = qi * NSA_QBLOCK + jnp.arange(NSA_QBLOCK, dtype=jnp.int32)
        d_c = t[:, None] - cmp_end[None, :]
        l_c = (jnp.einsum('bgqtd,bgnd->bgqtn', qb, kc).astype(F32) * scale
               + _rel_bias(rel_bias, d_c).reshape(G, Q, NSA_QBLOCK, n_cmp))
        p_c = _masked_softmax(l_c, d_c >= 0)
        o_c = jnp.einsum('bgqtn,bgnd->bgqtd', p_c.astype(dt), vc)
        imp = jnp.einsum('bgqtn,nj->bgtj', p_c, overlap)
        cur = (t // SEL_BLOCK)[:, None]
        forced = (blk_ids == 0) | (blk_ids == cur) | (blk_ids == cur - 1)
        imp = jnp.where(blk_ids > cur, -1e9, jnp.where(forced, 1e9, imp))
        _, idx = lax.top_k(imp, n_sel)
        k_g = ks[bi, gi, idx].reshape(B, G, NSA_QBLOCK, n_sel * SEL_BLOCK, Dh)
        v_g = vs[bi, gi, idx].reshape(B, G, NSA_QBLOCK, n_sel * SEL_BLOCK, Dh)
        pos = (idx[..., None] * SEL_BLOCK + tok_in_blk).reshape(B, G, NSA_QBLOCK, n_sel * SEL_BLOCK)
        d_s = t[:, None] - pos
        b_s = tab_g[gi, _t5_bucket(d_s)].transpose(0, 1, 4, 2, 3).astype(F32)
        l_s = jnp.einsum('bgqtd,bgtsd->bgqts', qb, k_g).astype(F32) * scale + b_s
        p_s = _masked_softmax(l_s, (d_s >= 0)[:, :, None])
        o_s = jnp.einsum('bgqts,bgtsd->bgqtd', p_s.astype(dt), v_g)
        start = qi * NSA_QBLOCK
        k_wb = lax.dynamic_slice_in_dim(kw, start, NSA_WINDOW + NSA_QBLOCK, axis=2)
        v_wb = lax.dynamic_slice_in_dim(vw, start, NSA_WINDOW + NSA_QBLOCK, axis=2)
        s_abs = start - NSA_WINDOW + win_off
        d_w = t[:, None] - s_abs[None, :]
        valid_w = (d_w >= 0) & (d_w < NSA_WINDOW) & (s_abs >= 0)[None, :]
        l_w = (jnp.einsum('bgqtd,bgsd->bgqts', qb, k_wb).astype(F32) * scale
               + _rel_bias(rel_bias, d_w).reshape(G, Q, NSA_QBLOCK, NSA_WINDOW + NSA_QBLOCK))
        p_w = _masked_softmax(l_w, valid_w)
        o_w = jnp.einsum('bgqts,bgsd->bgqtd', p_w.astype(dt), v_wb)
        return (gb[..., 0:1] * o_c + gb[..., 1:2] * o_s + gb[..., 2:3] * o_w).astype(dt)

    o = lax.map(block, (q_blocks, g_blocks, jnp.arange(n_qb, dtype=jnp.int32)))
    o = o.transpose(1, 0, 4, 2, 3, 5).reshape(B, T, D_MODEL)
    return o @ w_out


def _hgrn2(x, w_in, w_out, norm_gain, lb):
    B, T, _ = x.shape
    H, K, V, C = HGRN_HEADS, HGRN_KDIM, HGRN_VDIM, HGRN_CHUNK
    n_ch = T // C
    h = (x @ w_in).astype(F32)
    zq, zf, zi, zg = jnp.split(h, 4, axis=-1)
    lbf = lb.astype(F32)
    q = jax.nn.silu(zq)
    log_f = jnp.logaddexp(jnp.log(lbf), jnp.log1p(-lbf) + jax.nn.log_sigmoid(zf))
    k = (1.0 - lbf) * jax.nn.sigmoid(-zf)

    def chunks(z, d):
        return z.reshape(B, n_ch, C, H, d).transpose(1, 0, 3, 2, 4)

    causal = jnp.tril(jnp.ones((C, C), dtype=bool))[None, None, :, :, None]

    def step(S, inp):
        qc, kc, vc, gc = inp
        bcum = jnp.cumsum(gc, axis=2)
        o_inter = jnp.einsum('bhck,bhkv->bhcv', qc * jnp.exp(bcum), S)
        diff = jnp.where(causal, bcum[:, :, :, None, :] - bcum[:, :, None, :, :], -jnp.inf)
        a = jnp.einsum('bhtk,bhsk,bhtsk->bhts', qc, kc, jnp.exp(diff))
        o_intra = jnp.einsum('bhts,bhsv->bhtv', a, vc)
        b_last = bcum[:, :, -1]
        S_new = (jnp.exp(b_last)[..., None] * S
                 + jnp.einsum('bhsk,bhsv->bhkv', kc * jnp.exp(b_last[:, :, None] - bcum), vc))
        return S_new, o_inter + o_intra

    S0 = jnp.zeros((B, H, K, V), F32)
    _, o = lax.scan(step, S0, (chunks(q, K), chunks(k, K), chunks(zi, V), chunks(log_f, K)))
    o = o.transpose(1, 0, 3, 2, 4).reshape(B, T, H, V)
    o = o * lax.rsqrt(jnp.mean(o * o, axis=-1, keepdims=True) + RMS_EPS) * norm_gain.astype(F32)
    o = o * jax.nn.silu(zg.reshape(B, T, H, V))
    return o.reshape(B, T, D_MODEL).astype(x.dtype) @ w_out


def _swa(x, w_in, w_out, sinks, rel_bias):
    B, T, _ = x.shape
    nb = T // SWA_BLOCK
    Kv, Q, Dh, L = SWA_KV_HEADS, SWA_QPG, HEAD_DIM, SWA_BLOCK
    scale = Dh ** -0.5
    h = x @ w_in
    q, k, v = jnp.split(h, [ATTN_HEADS * Dh, ATTN_HEADS * Dh + Kv * Dh], axis=-1)
    q = q.reshape(B, nb, L, Kv, Q, Dh)

    def band(z):
        z = jnp.pad(z.reshape(B, T, Kv, Dh), ((0, 0), (L, 0), (0, 0), (0, 0))).reshape(B, nb + 1, L, Kv, Dh)
        return jnp.concatenate([z[:, :-1], z[:, 1:]], axis=2)

    kb, vb = band(k), band(v)
    tl = jnp.arange(L, dtype=jnp.int32)
    sl = jnp.arange(2 * L, dtype=jnp.int32)
    dist = tl[:, None] + L - sl[None, :]
    bias = _rel_bias(rel_bias, dist).reshape(Kv, Q, L, 2 * L)
    abs_s = jnp.arange(nb, dtype=jnp.int32)[:, None] * L - L + sl[None, :]
    valid = ((dist >= 0) & (dist < SWA_WINDOW))[None] & (abs_s >= 0)[:, None, :]
    logits = jnp.einsum('bntgqd,bnsgd->bngqts', q, kb).astype(F32) * scale + bias
    logits = jnp.where(valid[None, :, None, None], logits, NEG_INF)
    sink = sinks.astype(F32).reshape(Kv, Q)[None, None, :, :, None, None]
    m = jnp.maximum(jnp.max(logits, axis=-1, keepdims=True), sink)
    e = jnp.exp(logits - m)
    p = e / (jnp.sum(e, axis=-1, keepdims=True) + jnp.exp(sink - m))
    o = jnp.einsum('bngqts,bnsgd->bntgqd', p.astype(x.dtype), vb).reshape(B, T, D_MODEL)
    return o @ w_out


def setup_inputs(seed: int = 0) -> dict:
    key = jax.random.key(seed)
    ks = jax.random.split(key, 22)
    beta = (8.0 * DEPTH) ** -0.25
    D = D_MODEL

    def nrm(k, shape, scale):
        return jax.random.normal(k, shape, F32) * scale

    return {
        'x': nrm(ks[0], (BATCH, SEQ, D), 1.0),
        'rel_bias': nrm(ks[1], (REL_BUCKETS, ATTN_HEADS), 0.5),
        'ln_gain': 1.0 + nrm(ks[2], (DEPTH, 3, D), 0.02),
        'ln_bias': nrm(ks[3], (DEPTH, 3, D), 0.02),
        'ffn1_w_gate': nrm(ks[4], (DEPTH, D, D_FF), D ** -0.5),
        'ffn1_w_up': nrm(ks[5], (DEPTH, D, D_FF), D ** -0.5),
        'ffn1_w_down': nrm(ks[6], (DEPTH, D_FF, D), beta * D_FF ** -0.5),
        'ffn2_w_gate': nrm(ks[7], (DEPTH, D, D_FF), D ** -0.5),
        'ffn2_w_up': nrm(ks[8], (DEPTH, D, D_FF), D ** -0.5),
        'ffn2_w_down': nrm(ks[9], (DEPTH, D_FF, D), beta * D_FF ** -0.5),
        'nsa_w_in': nrm(ks[10], (N_NSA, D, NSA_IN), D ** -0.5),
        'nsa_w_out': nrm(ks[11], (N_NSA, D, D), beta * D ** -0.5),
        'nsa_cmp_pos': nrm(ks[12], (N_NSA, 2, CMP_LEN, HEAD_DIM), 0.1),
        'nsa_cmp_w1': nrm(ks[13], (N_NSA, 2, CMP_LEN, HEAD_DIM, CMP_HIDDEN), (CMP_LEN * HEAD_DIM) ** -0.5),
        'nsa_cmp_w2': nrm(ks[14], (N_NSA, 2, CMP_HIDDEN, HEAD_DIM), CMP_HIDDEN ** -0.5),
        'hgrn_w_in': nrm(ks[15], (N_HGRN, D, 4 * D), D ** -0.5),
        'hgrn_w_out': nrm(ks[16], (N_HGRN, D, D), beta * D ** -0.5),
        'hgrn_norm_gain': 1.0 + nrm(ks[17], (N_HGRN, HGRN_VDIM), 0.02),
        'hgrn_lb': nrm(ks[18], (DEPTH, D), 1.0),
        'swa_w_in': nrm(ks[19], (N_SWA, D, SWA_IN), D ** -0.5),
        'swa_w_out': nrm(ks[20], (N_SWA, D, D), beta * D ** -0.5),
        'swa_sinks': nrm(ks[21], (N_SWA, ATTN_HEADS), 0.5),
    }


def reference(x, rel_bias, ln_gain, ln_bias, ffn1_w_gate, ffn1_w_up, ffn1_w_down,
              ffn2_w_gate, ffn2_w_up, ffn2_w_down, nsa_w_in, nsa_w_out, nsa_cmp_pos,
              nsa_cmp_w1, nsa_cmp_w2, hgrn_w_in, hgrn_w_out, hgrn_norm_gain, hgrn_lb,
              swa_w_in, swa_w_out, swa_sinks):
    alpha = (2.0 * DEPTH) ** 0.25
    lbs = _lower_bounds(hgrn_lb)
    for i in range(DEPTH):
        x = _layer_norm(alpha * x + 0.5 * _swiglu(x, ffn1_w_gate[i], ffn1_w_up[i], ffn1_w_down[i]),
                        ln_gain[i, 0], ln_bias[i, 0])
        kind, slot = i % N_MIXERS, i // N_MIXERS
        if kind == 0:
            y = _nsa(x, nsa_w_in[slot], nsa_w_out[slot], nsa_cmp_pos[slot], nsa_cmp_w1[slot],
                     nsa_cmp_w2[slot], rel_bias)
        elif kind == 1:
            y = _hgrn2(x, hgrn_w_in[slot], hgrn_w_out[slot], hgrn_norm_gain[slot], lbs[i])
        else:
            y = _swa(x, swa_w_in[slot], swa_w_out[slot], swa_sinks[slot], rel_bias)
        x = _layer_norm(alpha * x + y, ln_gain[i, 1], ln_bias[i, 1])
        x = _layer_norm(alpha * x + 0.5 * _swiglu(x, ffn2_w_gate[i], ffn2_w_up[i], ffn2_w_down[i]),
                        ln_gain[i, 2], ln_bias[i, 2])
    return x
```

```python
import math
import numpy as np
import ml_dtypes
from contextlib import ExitStack
import concourse.bass as bass
import concourse.mybir as mybir
from concourse.bass_utils import run_bass_kernel_spmd

F32 = mybir.dt.float32
BF16 = mybir.dt.bfloat16
AF = mybir.ActivationFunctionType
ALU = mybir.AluOpType
AX = mybir.AxisListType

D = 1024
DFF = 2816
DEPTH = 4
NCH = D // 128
NFC = DFF // 128
ALPHA = (2.0 * DEPTH) ** 0.25
LN_EPS = 1e-5
NDS = 24
NEG = -30000.0
EMBED_WAITS = True
import os
EMBED_ENG = set(os.environ.get('EMBED_ENG', 'pe,dve,act,pool').split(','))


class Buf:
    __slots__ = ("name", "w", "r")

    def __init__(self, name):
        self.name = name
        self.w = None
        self.r = {}


class Tile:
    def __init__(self, t, name):
        self.t = t
        self.b = Buf(name)
        self.subs = {}

    def __getitem__(self, idx):
        return self.t[idx]

    def sub(self, key):
        if key not in self.subs:
            self.subs[key] = Buf(f"{self.b.name}.{key}")
        return self.subs[key]


def _bufs(xs):
    out = []
    for x in xs:
        if x is None:
            continue
        if isinstance(x, Buf):
            out.append(x)
        else:
            out.append(x.b)
            out.extend(x.subs.values())
    return out


class Prog:
    def __init__(self, nc, stack):
        self.nc = nc
        self.stacks = [stack]
        self.E = {"pe": nc.tensor, "dve": nc.vector, "act": nc.scalar, "pool": nc.gpsimd, "sp": nc.sync}
        self.sem = {k: stack.enter_context(nc.semaphore("s_" + k)) for k in self.E}
        self.cnt = {k: 0 for k in self.E}
        self.known = {k: {} for k in self.E}
        self.dsems = [stack.enter_context(nc.semaphore(f"dq{i}")) for i in range(NDS)]
        self.dcnt = [0] * NDS
        self.dnext = 0
        self.dnext2 = 0
        self.nwait = {}
        self.pe_serial = False
        self.nins = 0
        self.uid = 0

    def push(self):
        st = ExitStack()
        st.__enter__()
        self.stacks.append(st)

    def pop(self):
        st = self.stacks.pop()
        st.__exit__(None, None, None)

    def sb(self, shape, dt, name):
        self.uid += 1
        nm = f"{name}_{self.uid}"
        t = self.stacks[-1].enter_context(self.nc.sbuf_tensor(nm, list(shape), dt))
        return Tile(t, nm)

    def ps(self, shape, dt, name):
        self.uid += 1
        nm = f"{name}_{self.uid}"
        t = self.stacks[-1].enter_context(self.nc.psum_tensor(nm, list(shape), dt))
        return Tile(t, nm)

    def _wait(self, e, tok):
        sem, v, key = tok
        if self.known[e].get(key, 0) >= v:
            return
        self.E[e].wait_ge(sem, v)
        self.known[e][key] = v
        self.nins += 1
        self.nwait[e] = self.nwait.get(e, 0) + 1

    def _need(self, e, reads, writes):
        reads = _bufs(reads)
        writes = _bufs(writes)
        need = {}

        def add(tok):
            sem, v, key = tok
            if self.known[e].get(key, 0) >= v:
                return
            if key not in need or need[key][1] < v:
                need[key] = tok

        for b in reads:
            if b.w is not None:
                add(b.w)
        for b in writes:
            if b.w is not None and not (e == "pe" and b.w[2] == "pe" and not self.pe_serial):
                add(b.w)
            for k, t in b.r.items():
                if not (e == "pe" and k == "pe"):
                    add(t)
        return reads, writes, list(need.values())

    def _issue(self, e, fn, need):
        for tok in need[:-1]:
            self._wait(e, tok)
        ins = fn()
        if need:
            sem, v, key = need[-1]
            if EMBED_WAITS:
                ins._wait_ge(sem, v)
                self.known[e][key] = v
            else:
                raise RuntimeError
        return ins

    def _post(self, tok, reads, writes):
        key = tok[2]
        for b in reads:
            b.r[key] = tok
        for b in writes:
            b.w = tok
            b.r = {}

    def op(self, e, fn, reads, writes):
        reads, writes, need = self._need(e, reads, writes)
        if e not in EMBED_ENG:
            for tok in need:
                self._wait(e, tok)
            need = []
        ins = self._issue(e, fn, need)
        self.cnt[e] += 1
        ins.then_inc(self.sem[e], 1)
        self._post((self.sem[e], self.cnt[e], e), reads, writes)
        self.nins += 1
        return ins

    def dma(self, q, out, in_, reads, writes, **kw):
        if q == "sp":
            i = self.dnext
            self.dnext = (i + 1) % 16
        else:
            i = 16 + self.dnext2
            self.dnext2 = (self.dnext2 + 1) % (NDS - 16)
        key = ("d", i)
        if self.dcnt[i] > 0:
            self._wait(q, (self.dsems[i], self.dcnt[i], key))
        reads, writes, need = self._need(q, reads, writes)
        for tok in need:
            self._wait(q, tok)
        ins = self.E[q].dma_start(out=out, in_=in_, **kw)
        self.dcnt[i] += 16
        ins.then_inc(self.dsems[i], 16)
        self._post((self.dsems[i], self.dcnt[i], key), reads, writes)
        self.nins += 1

    def barrier(self, engines=None):
        for e in (engines or self.E):
            for f in self.E:
                if f != e and self.cnt[f] > 0:
                    self._wait(e, (self.sem[f], self.cnt[f], f))
            for i in range(NDS):
                if self.dcnt[i] > 0:
                    self._wait(e, (self.dsems[i], self.dcnt[i], ("d", i)))

    def mm(self, out, lhsT, rhs, start, stop, reads, writes, skip=False):
        if skip:
            return self.op("pe", lambda: self.nc.tensor.matmul(out, lhsT, rhs, start=start, stop=stop,
                                                               skip_group_check=True), reads, writes)
        return self.op("pe", lambda: self.nc.tensor.matmul(out, lhsT, rhs, start=start, stop=stop), reads, writes)

    def tr(self, out, in_, ident, reads, writes):
        return self.op("pe", lambda: self.nc.tensor.transpose(out, in_, ident), reads, writes)

    def act(self, out, in_, func, reads, writes, **kw):
        return self.op("act", lambda: self.nc.scalar.activation(out, in_, func, **kw), reads, writes)

    def v(self, e, name, reads, writes, *a, **kw):
        eng = self.E[e]
        return self.op(e, lambda: getattr(eng, name)(*a, **kw), reads, writes)


class Ctx:
    pass


class Tile4:
    def __init__(self, tile):
        self.t = tile
        self.b = tile.b
        self.subs = tile.subs

    def __getitem__(self, idx):
        v = self.t[:].rearrange("p (m n) -> p m n", m=4)
        return v[idx]


def setup_common(p):
    nc = p.nc
    C = Ctx()
    C.identf = p.sb([128, 128], F32, "identf")
    C.ident = p.sb([128, 128], BF16, "ident")
    p.v("pool", "memset", [], [C.identf], C.identf[:], 0.0)
    p.op("pool", lambda: nc.gpsimd.affine_select(out=C.identf[:], in_=C.identf[:], pattern=[[-1, 128]],
                                                 compare_op=ALU.not_equal, fill=1.0, base=0, channel_multiplier=1),
         [C.identf], [C.identf])
    p.v("dve", "tensor_copy", [C.identf], [C.ident], C.ident[:], C.identf[:])
    C.G = p.sb([128, D], F32, "lnG")
    C.Bt = p.sb([128, D], F32, "lnB")
    C.st6 = [p.sb([128, 2, 6], F32, f"st6{i}") for i in range(2)]
    C.mv = [p.sb([128, 4], F32, f"mv{i}") for i in range(2)]
    C.ysb = [p.sb([128, D], F32, f"ysb{i}") for i in range(2)]
    C.epi = 0
    return C


def load_ln(p, C, g_d, b_d):
    p.dma("sp", C.G[:], g_d.rearrange("(o n) -> o n", o=1).to_broadcast([128, D]), [], [C.G])
    p.dma("sp", C.Bt[:], b_d.rearrange("(o n) -> o n", o=1).to_broadcast([128, D]), [], [C.Bt])


def ln_epilogue(p, C, xs, py0, py1, yscale, out_rows):
    k = C.epi % 2
    C.epi += 1
    ysb, st6, mv = C.ysb[k], C.st6[k], C.mv[k]
    p.act(ysb[:, 0:512], py0[:], AF.Copy, [py0], [ysb], scale=yscale)
    p.act(ysb[:, 512:1024], py1[:], AF.Copy, [py1], [ysb], scale=yscale)
    p.v("dve", "scalar_tensor_tensor", [xs, ysb], [ysb], out=ysb[:], in0=xs[:], scalar=ALPHA, in1=ysb[:],
        op0=ALU.mult, op1=ALU.add)
    for c in range(2):
        p.v("dve", "bn_stats", [ysb], [st6], st6[:, c, :], ysb[:, c * 512:(c + 1) * 512])
    p.v("dve", "bn_aggr", [st6], [mv], mv[:, 0:2], st6[:])
    p.v("dve", "tensor_scalar", [mv], [mv], mv[:, 3:4], mv[:, 1:2], LN_EPS, None, ALU.add)
    p.act(mv[:, 3:4], mv[:, 3:4], AF.Sqrt, [mv], [mv])
    p.v("dve", "reciprocal", [mv], [mv], mv[:, 2:3], mv[:, 3:4])
    p.v("dve", "tensor_scalar", [ysb, mv], [ysb], ysb[:], ysb[:], mv[:, 0:1], mv[:, 2:3], ALU.subtract, ALU.mult)
    p.v("pool", "tensor_tensor", [ysb, C.G], [ysb], ysb[:], ysb[:], C.G[:], ALU.mult)
    p.v("pool", "tensor_tensor", [ysb, C.Bt], [ysb], ysb[:], ysb[:], C.Bt[:], ALU.add)
    p.dma("sp", out_rows, ysb[:], [ysb], [])


def ffn_phase(p, C, T, x_in, x_out, wg_d, wu_d, wd_d, g_d, b_d):
    p.push()
    Wg = p.sb([128, NCH, DFF], BF16, "Wg")
    Wu = p.sb([128, NCH, DFF], BF16, "Wu")
    Wd = p.sb([128, NFC, D], BF16, "Wd")
    for c in range(NCH):
        p.dma("pool", Wg[:, c, :], wg_d[c * 128:(c + 1) * 128, :], [], [Wg.sub(c)])
    for c in range(NCH):
        p.dma("pool", Wu[:, c, :], wu_d[c * 128:(c + 1) * 128, :], [], [Wu.sub(c)])
    for f in range(NFC):
        p.dma("pool", Wd[:, f, :], wd_d[f * 128:(f + 1) * 128, :], [], [Wd.sub(f)])
    load_ln(p, C, g_d, b_d)
    xs = [[p.sb([128, D], F32, f"xs{a}{j}") for j in range(2)] for a in range(2)]
    xb = [p.sb([128, D], BF16, f"xb{j}") for j in range(2)]
    xT = [p.sb([128, NCH, 256], BF16, f"xT{a}") for a in range(2)]
    h = p.sb([128, NFC, 256], BF16, "h")
    sg = [p.sb([128, 256], F32, f"sg{a}") for a in range(2)]
    ptr = [p.ps([128, NCH, 128], BF16, f"ptr{a}") for a in range(2)]
    pgu = [p.ps([128, 2, 256], F32, f"pgu{a}") for a in range(2)]
    py = [p.ps([128, 512], F32, f"py{a}") for a in range(4)]
    NT = T // 256

    def prep(t):
        a = t % 2
        for j in range(2):
            r0 = t * 256 + j * 128
            p.dma("sp", xs[a][j][:], x_in[r0:r0 + 128, :], [], [xs[a][j]])
            p.act(xb[j][:], xs[a][j][:], AF.Copy, [xs[a][j]], [xb[j]])
            for c in range(NCH):
                p.tr(ptr[j][:, c, :], xb[j][:, c * 128:(c + 1) * 128], C.ident[:], [xb[j], C.ident], [ptr[j]])
            p.v("dve", "tensor_copy", [ptr[j]], [xT[a]], xT[a][:, :, j * 128:(j + 1) * 128], ptr[j][:])

    prep(0)
    for t in range(NT):
        a = t % 2
        for f in range(NFC):
            k = f % 2
            for c in range(NCH):
                p.mm(pgu[k][:, 0, :], Wg[:, c, f * 128:(f + 1) * 128], xT[a][:, c, :], c == 0, c == NCH - 1,
                     [Wg.sub(c), xT[a]], [pgu[k]])
            for c in range(NCH):
                p.mm(pgu[k][:, 1, :], Wu[:, c, f * 128:(f + 1) * 128], xT[a][:, c, :], c == 0, c == NCH - 1,
                     [Wu.sub(c), xT[a]], [pgu[k]])
            p.act(sg[k][:], pgu[k][:, 0, :], AF.Silu, [pgu[k]], [sg[k]])
            p.v("dve", "tensor_tensor", [sg[k], pgu[k]], [h], h[:, f, :], sg[k][:], pgu[k][:, 1, :], ALU.mult)
        if t + 1 < NT:
            prep(t + 1)
        for j in range(2):
            for hf in range(2):
                pb = py[j * 2 + hf]
                for f in range(NFC):
                    p.mm(pb[:], h[:, f, j * 128:(j + 1) * 128], Wd[:, f, hf * 512:(hf + 1) * 512], f == 0,
                         f == NFC - 1, [h, Wd.sub(f)], [pb])
            r0 = t * 256 + j * 128
            ln_epilogue(p, C, xs[a][j], py[j * 2], py[j * 2 + 1], 0.5, x_out[r0:r0 + 128, :])
    p.barrier()
    p.pop()


def _t5_bucket_np(dist):
    n = np.maximum(dist, 0)
    nf = np.maximum(n, 1).astype(np.float32)
    large = 16 + (np.log(nf / np.float32(16)) / np.float32(math.log(128 / 16)) * np.float32(16)).astype(np.int32)
    return np.where(n < 16, n, np.minimum(large, 31))


SWA_HPERM = [g * 8 + 2 * m + par for g in range(2) for par in range(2) for m in range(4)]


def swa_consts(rel_bias, sinks):
    s = np.arange(128)[:, None]
    t = np.arange(128)[None, :]
    out = np.empty((2, 128, 16, 128), np.float32)
    for jj in range(2):
        d = t - s + 128 * jj
        valid = (d >= 0) & (d < 128)
        tab = rel_bias[_t5_bucket_np(d)]
        tab = np.where(valid[:, :, None], tab, np.float32(NEG))
        out[jj] = np.transpose(tab[:, :, SWA_HPERM], (0, 2, 1))
    return out, np.ascontiguousarray(sinks[SWA_HPERM])


def load_x_tile(p, C, x_rows, xs, xb, ptr, xT_dst):
    p.dma("sp", xs[:], x_rows, [], [xs])
    p.act(xb[:], xs[:], AF.Copy, [xs], [xb])
    for c in range(NCH):
        p.tr(ptr[:, c, :], xb[:, c * 128:(c + 1) * 128], C.ident[:], [xb, C.ident], [ptr])
    p.v("dve", "tensor_copy", [ptr], [xT_dst], xT_dst[:], ptr[:])


def out_proj_ln(p, C, o_tok, ptr, oT, Wo, py0, py1, xs, out_rows):
    for c in range(NCH):
        p.tr(ptr[:, c, :], o_tok[:, c * 128:(c + 1) * 128], C.ident[:], [o_tok, C.ident], [ptr])
    p.v("dve", "tensor_copy", [ptr], [oT], oT[:], ptr[:])
    for hf, pb in enumerate((py0, py1)):
        for c in range(NCH):
            p.mm(pb[:], oT[:, c, :], Wo[:, c, hf * 512:(hf + 1) * 512], c == 0, c == NCH - 1, [oT, Wo], [pb])
    ln_epilogue(p, C, xs, py0, py1, 1.0, out_rows)


def load_w_chunks(p, W, w_d, col0, ncols, nsplit=2):
    src = w_d.rearrange("(c p) n -> p c n", p=128)
    step = NCH // nsplit
    for s in range(nsplit):
        p.dma("pool", W[:, s * step:(s + 1) * step, 0:ncols], src[:, s * step:(s + 1) * step, col0:col0 + ncols], [], [W])


def swa_phase(p, C, T, x_in, x_out, w_in_d, w_out_d, bias_d, sinks_d, g_d, b_d):
    nc = p.nc
    p.push()
    Wq = p.sb([128, NCH, 1024], BF16, "Wq")
    Wk2 = p.sb([128, NCH, 2, 2, 64], BF16, "Wk2")
    Wv = p.sb([128, NCH, 128], BF16, "Wv")
    Wo = p.sb([128, NCH, 1024], BF16, "Wo")
    load_w_chunks(p, Wq, w_in_d, 0, 1024, 4)
    src = w_in_d.rearrange("(c p) n -> p c n", p=128)
    for g in range(2):
        for dup in range(2):
            p.dma("pool", Wk2[:, :, g, dup, :], src[:, :, 1024 + g * 64:1024 + (g + 1) * 64], [], [Wk2])
    load_w_chunks(p, Wv, w_in_d, 1152, 128, 1)
    load_w_chunks(p, Wo, w_out_d, 0, 1024, 4)
    load_ln(p, C, g_d, b_d)
    BT = p.sb([128, 2, 16, 128], F32, "BT")
    for jj in range(2):
        p.dma("sp", BT[:, jj, :, :], bias_d[jj], [], [BT])
    ES = p.sb([128, 16], F32, "ES")
    p.dma("sp", ES[:], sinks_d.rearrange("(o n) -> o n", o=1).to_broadcast([128, 16]), [], [ES])
    p.act(ES[:], ES[:], AF.Exp, [ES], [ES])

    xs = [p.sb([128, D], F32, f"xs{a}") for a in range(2)]
    xb = [p.sb([128, D], BF16, f"xb{a}") for a in range(2)]
    xT = [p.sb([128, NCH, 128], BF16, f"xT{a}") for a in range(2)]
    qT = [p.sb([128, 8, 128], BF16, f"qT{a}") for a in range(2)]
    kbuf = [p.sb([128, 2, 128], BF16, f"kb{a}") for a in range(2)]
    vbuf = [p.sb([128, 2, 65], BF16, f"vb{a}") for a in range(2)]
    for a in range(2):
        p.v("dve", "memset", [], [vbuf[a]], vbuf[a][:, :, 64:65], 1.0)
    sc = [p.sb([128, 4, 128], F32, f"sc{a}") for a in range(2)]
    e = [p.sb([128, 4, 128], BF16, f"e{a}") for a in range(4)]
    den = [p.sb([128, 8], F32, f"den{a}") for a in range(2)]
    o_tok = [p.sb([128, D], BF16, f"otok{a}") for a in range(2)]
    oT = p.sb([128, NCH, 128], BF16, "oT")
    ptr = [p.ps([128, NCH, 128], BF16, f"ptr{a}") for a in range(1)]
    pq = [p.ps([128, 512], F32, f"pq{a}") for a in range(2)]
    pkv = p.ps([128, 512], F32, "pkv")
    ps = [p.ps([128, 4, 128], F32, f"ps{a}") for a in range(2)]
    po = [p.ps([128, 512], F32, f"po{a}") for a in range(2)]
    ps = ps + [Tile4(pq[0]), Tile4(pq[1])]
    NT = T // 128
    nsc = 0
    for i in range(NT):
        a = i % 2
        load_x_tile(p, C, x_in[i * 128:(i + 1) * 128, :], xs[a], xb[a], ptr[0], xT[a])
        for m in range(8):
            pb = pq[m // 4]
            for c in range(NCH):
                p.mm(pb[:, (m % 4) * 128:(m % 4 + 1) * 128], Wq[:, c, m * 128:(m + 1) * 128], xT[a][:, c, :], c == 0,
                     c == NCH - 1, [Wq, xT[a]], [pb])
        for g in range(2):
            for c in range(NCH):
                p.mm(pkv[:, g * 128:(g + 1) * 128], Wk2[:, c, g, :, :], xT[a][:, c, :], c == 0, c == NCH - 1,
                     [Wk2, xT[a]], [pkv])
        for c in range(NCH):
            p.mm(pkv[:, 256:384], xT[a][:, c, :], Wv[:, c, :], c == 0, c == NCH - 1, [Wv, xT[a]], [pkv])
        p.act(qT[a][:, 0:4, :], pq[0][:], AF.Copy, [pq[0]], [qT[a]])
        p.act(qT[a][:, 4:8, :], pq[1][:], AF.Copy, [pq[1]], [qT[a]])
        p.v("dve", "tensor_copy", [pkv], [kbuf[a]], kbuf[a][:], pkv[:, 0:256])
        p.v("dve", "tensor_copy", [pkv], [vbuf[a]], vbuf[a][:, :, 0:64], pkv[:, 256:384])
        kts = ([i - 1] if i > 0 else []) + [i]
        ot4 = o_tok[a][:].rearrange("p (m r d) -> p m r d", m=8, r=2, d=64)
        items = [(g, par, kt, kt == kts[0], kt == kts[-1]) for g in range(2) for par in range(2) for kt in kts]
        LA = 2

        def stage_a(n):
            g, par, kt, first, last = items[n]
            b = g * 2 + par
            jj = i - kt
            k = (nsc + n) % 4
            p.mm(ps[k][:], kbuf[kt % 2][par * 64:(par + 1) * 64, g, :],
                 qT[a][par * 64:(par + 1) * 64, g * 4:(g + 1) * 4, :], True, True, [kbuf[kt % 2], qT[a]], [ps[k]])
            p.v("dve", "scalar_tensor_tensor", [ps[k], BT], [sc[k % 2]], out=sc[k % 2][:], in0=ps[k][:], scalar=0.125,
                in1=BT[:, jj, b * 4:(b + 1) * 4, :], op0=ALU.mult, op1=ALU.add)
            p.act(e[k][:], sc[k % 2][:], AF.Exp, [sc[k % 2]], [e[k]])

        def stage_c(n):
            g, par, kt, first, last = items[n]
            b = g * 2 + par
            k = (nsc + n) % 4
            pob = po[b % 2]
            pov = pob[:, 0:260].rearrange("p (m d) -> p m d", m=4, d=65)
            for m in range(4):
                p.mm(pov[:, m, :], e[k][:, m, :], vbuf[kt % 2][:, g, :], first and m == 0, last,
                     [e[k], vbuf[kt % 2]], [pob], skip=True)
            if last:
                dn = den[b % 2]
                p.v("dve", "tensor_tensor", [pob, ES], [dn], dn[:, 0:4], pov[:, :, 64], ES[:, b * 4:(b + 1) * 4], ALU.add)
                p.v("dve", "reciprocal", [dn], [dn], dn[:, 4:8], dn[:, 0:4])
                p.v("dve", "tensor_tensor", [pob, dn], [o_tok[a]], ot4[:, g * 4:(g + 1) * 4, par, :], pov[:, :, 0:64],
                    dn[:, 4:8].unsqueeze(2).to_broadcast([128, 4, 64]), ALU.mult)

        NI = len(items)
        for n in range(NI + LA):
            if n < NI:
                stage_a(n)
            if n - LA >= 0:
                stage_c(n - LA)
        nsc += NI
        out_proj_ln(p, C, o_tok[a], ptr[0], oT, Wo, pq[0], pq[1], xs[a], x_out[i * 128:(i + 1) * 128, :])
    p.barrier()
    p.pop()


def hgrn_consts():
    s = np.arange(128)[:, None]
    t = np.arange(128)[None, :]
    same = (s // 64) == (t // 64)
    U = (same & (s <= t)).astype(np.float32)
    Emid = (same & ((s % 64) <= 31)).astype(np.float32)
    urhs = np.zeros((128, 132), np.float32)
    urhs[:, 0:128] = U - Emid
    for c in range(2):
        urhs[:, 128 + c] = ((s[:, 0] // 64 == c) & ((s[:, 0] % 64) <= 31)).astype(np.float32)
        urhs[:, 130 + c] = (s[:, 0] // 64 == c).astype(np.float32)
    mneg = -(U - Emid)
    return np.concatenate([urhs, mneg, U], axis=1).astype(np.float32)


def hgrn_phase(p, C, T, x_in, x_out, layer, w_in_d, w_out_d, gain_d, lb_d, hc_d, g_d, b_d):
    nc = p.nc
    p.push()
    W = [p.sb([128, NCH, 1024], BF16, f"Wh{j}") for j in range(4)]
    Wo = p.sb([128, NCH, 1024], BF16, "Wo")
    for j in range(4):
        load_w_chunks(p, W[j], w_in_d, j * 1024, 1024, 4)
    load_w_chunks(p, Wo, w_out_d, 0, 1024, 4)
    load_ln(p, C, g_d, b_d)
    HC = p.sb([128, 388], F32, "HC")
    p.dma("sp", HC[:], hc_d, [], [HC])
    Urhs, Mneg, Mbd = HC[:, 0:132], HC[:, 132:260], HC[:, 260:388]
    Gn = p.sb([128, 128], F32, "Gn")
    p.dma("sp", Gn[:], gain_d.rearrange("(o n) -> o n", o=1).to_broadcast([128, 128]), [], [Gn])
    LBb = p.sb([128, D], F32, "LBb")
    OMLb = p.sb([128, D], F32, "OMLb")
    lbT = p.sb([128, 16], F32, "lbT")
    p.push()
    L4 = p.sb([128, 4, D], F32, "L4")
    for j in range(4):
        p.dma("sp", L4[:, j, :], lb_d[j].rearrange("(o n) -> o n", o=1).to_broadcast([128, D]), [], [L4])
    p.act(L4[:], L4[:], AF.Exp, [L4], [L4])
    p.v("dve", "tensor_tensor", [L4], [OMLb], OMLb[:], L4[:, 0, :], L4[:, 1, :], ALU.add)
    p.v("dve", "tensor_tensor", [L4, OMLb], [OMLb], OMLb[:], OMLb[:], L4[:, 2, :], ALU.add)
    p.v("dve", "tensor_tensor", [L4, OMLb], [OMLb], OMLb[:], OMLb[:], L4[:, 3, :], ALU.add)
    p.v("dve", "reciprocal", [OMLb], [OMLb], OMLb[:], OMLb[:])
    p.v("dve", "memset", [], [LBb], LBb[:], 0.0)
    for j in range(1, layer + 1):
        p.v("dve", "tensor_tensor", [L4, LBb], [LBb], LBb[:], LBb[:], L4[:, j, :], ALU.add)
    p.v("dve", "tensor_tensor", [LBb, OMLb], [LBb], LBb[:], LBb[:], OMLb[:], ALU.mult)
    p.v("dve", "tensor_scalar", [LBb], [OMLb], OMLb[:], LBb[:], -1.0, 1.0, ALU.mult, ALU.add)
    plb = p.ps([128, 8, 128], F32, "plb")
    for h in range(8):
        p.op("pe", lambda h=h: nc.tensor.transpose(plb[:, h, :], LBb[:, h * 128:(h + 1) * 128], C.identf[:]), [LBb, C.identf], [plb])
    p.v("dve", "tensor_copy", [plb], [lbT], lbT[:, 0:8], plb[:, :, 0])
    p.v("dve", "tensor_scalar", [lbT], [lbT], lbT[:, 8:16], lbT[:, 0:8], -1.0, 1.0, ALU.mult, ALU.add)
    p.barrier()
    p.pop()

    xs = [p.sb([128, D], F32, f"xs{a}") for a in range(2)]
    xb = [p.sb([128, D], BF16, f"xb{a}") for a in range(2)]
    xT = [p.sb([128, NCH, 128], BF16, f"xT{a}") for a in range(2)]
    qT = p.sb([128, 8, 128], F32, "qT")
    smT = p.sb([128, 8, 128], F32, "smT")
    fs = p.sb([128, D], F32, "fs")
    logf = p.sb([128, D], F32, "logf")
    kk = p.sb([128, D], F32, "kk")
    vb = p.sb([128, D], BF16, "vb")
    GG = p.sb([128, D], F32, "GG")
    eD = [p.sb([128, 128], F32, f"eD{a}") for a in range(2)]
    eDn = [p.sb([128, 128], F32, f"eDn{a}") for a in range(2)]
    eDp = [p.sb([128, 128], F32, f"eDp{a}") for a in range(2)]
    ex = [p.sb([128, 8], F32, f"ex{a}") for a in range(2)]
    qz = [p.sb([128, 2, 128], BF16, f"qz{a}") for a in range(2)]
    kz = [p.sb([128, 2, 128], BF16, f"kz{a}") for a in range(2)]
    for a in range(2):
        p.v("dve", "memset", [], [qz[a]], qz[a][:], 0.0)
        p.v("dve", "memset", [], [kz[a]], kz[a][:], 0.0)
    kTt = [p.sb([128, 128], BF16, f"kTt{a}") for a in range(2)]
    aT = [p.sb([128, 128], BF16, f"aT{a}") for a in range(2)]
    Sbf = [p.sb([128, 128], BF16, f"Sbf{a}") for a in range(2)]
    T1 = p.sb([128, 128], F32, "T1")
    S = p.sb([128, 8, 128], F32, "S")
    p.v("dve", "memset", [], [S], S[:], 0.0)
    sq = p.sb([128, 4, 128], F32, "sq")
    t1 = p.sb([128, 4, 128], F32, "t1")
    ss = p.sb([128, 16], F32, "ss")
    o_tok = [p.sb([128, D], BF16, f"otok{a}") for a in range(2)]
    oT = p.sb([128, NCH, 128], BF16, "oT")
    ptr = p.ps([128, NCH, 128], BF16, "ptr")
    PA = [p.ps([128, 512], F32, f"PA{a}") for a in range(2)]
    PB = [p.ps([128, 512], F32, f"PB{a}") for a in range(2)]
    Dk = p.ps([128, 512], F32, "Dk")
    Mi = p.ps([128, 512], F32, "Mi")
    po = p.ps([128, 4, 128], F32, "po")
    NT = T // 128
    hcnt = 0
    for i in range(NT):
        a = i % 2
        load_x_tile(p, C, x_in[i * 128:(i + 1) * 128, :], xs[a], xb[a], ptr, xT[a])

        def proj_fm(P2, Wm):
            for h in range(8):
                pb = P2[h // 4]
                for c in range(NCH):
                    p.mm(pb[:, (h % 4) * 128:(h % 4 + 1) * 128], Wm[:, c, h * 128:(h + 1) * 128], xT[a][:, c, :],
                         c == 0, c == NCH - 1, [Wm, xT[a]], [pb])

        def proj_tm(P2, Wm):
            for hf in range(2):
                for c in range(NCH):
                    p.mm(P2[hf][:], xT[a][:, c, :], Wm[:, c, hf * 512:(hf + 1) * 512], c == 0, c == NCH - 1,
                         [Wm, xT[a]], [P2[hf]])

        proj_fm(PA, W[0])
        proj_fm(PB, W[1])
        for hf in range(2):
            p.act(qT[:, hf * 4:(hf + 1) * 4, :], PA[hf][:], AF.Silu, [PA[hf]], [qT])
            p.act(smT[:, hf * 4:(hf + 1) * 4, :], PB[hf][:], AF.Sigmoid, [PB[hf]], [smT], scale=-1.0)
        proj_tm(PA, W[1])
        proj_tm(PB, W[2])
        for hf in range(2):
            p.act(fs[:, hf * 512:(hf + 1) * 512], PA[hf][:], AF.Sigmoid, [PA[hf]], [fs])
            p.v("dve", "tensor_copy", [PB[hf]], [vb], vb[:, hf * 512:(hf + 1) * 512], PB[hf][:])
        proj_tm(PA, W[3])
        p.v("dve", "tensor_tensor", [fs, OMLb], [fs], fs[:], fs[:], OMLb[:], ALU.mult)
        p.v("dve", "tensor_tensor", [fs, LBb], [fs], fs[:], fs[:], LBb[:], ALU.add)
        p.act(logf[:], fs[:], AF.Ln, [fs], [logf])
        p.v("pool", "tensor_scalar", [fs], [kk], kk[:], fs[:], -1.0, 1.0, ALU.mult, ALU.add)
        for hf in range(2):
            p.act(GG[:, hf * 512:(hf + 1) * 512], PA[hf][:], AF.Silu, [PA[hf]], [GG])
        p.v("pool", "tensor_tensor", [GG, Gn], [GG], GG[:].rearrange("p (h v) -> p h v", h=8), GG[:].rearrange("p (h v) -> p h v", h=8),
            Gn[:].unsqueeze(1).to_broadcast([128, 8, 128]), ALU.mult)
        XB = [Dk, PA[0]]
        YB = [Mi, PA[1]]

        def hs1(h):
            k = h % 2
            hs = slice(h * 128, (h + 1) * 128)
            X = XB[k]
            p.mm(X[:, 0:132], logf[:, hs], Urhs, True, True, [logf, HC], [X])
            p.mm(X[:, 256:384], Mneg, logf[:, hs], True, True, [logf, HC], [X])
            p.act(eD[k][:], X[:, 0:128], AF.Exp, [X], [eD[k]])
            p.act(eDn[k][:], X[:, 0:128], AF.Exp, [X], [eDn[k]], scale=-1.0)
            p.act(ex[k][:, 0:4], X[:, 128:132], AF.Exp, [X], [ex[k]])
            p.act(ex[k][:, 4:5], X[:, 63:64], AF.Exp, [X], [ex[k]])
            p.act(ex[k][:, 5:6], X[:, 127:128], AF.Exp, [X], [ex[k]])
            p.act(eDp[k][:], X[:, 256:384], AF.Exp, [X], [eDp[k]])
            for c in range(2):
                cs = slice(c * 64, (c + 1) * 64)
                p.v("dve", "tensor_tensor", [qT, eD[k]], [qz[k]], qz[k][:, c, cs], qT[:, h, cs], eD[k][:, cs], ALU.mult)
                p.v("dve", "tensor_tensor", [kk, eDp[k]], [kz[k]], kz[k][cs, c, :], kk[cs, hs], eDp[k][cs, :], ALU.mult)
            p.v("dve", "scalar_tensor_tensor", [smT, lbT, eDn[k]], [kTt[k]], out=kTt[k][:], in0=smT[:, h, :],
                scalar=lbT[:, 8 + h:9 + h], in1=eDn[k][:], op0=ALU.mult, op1=ALU.mult)

        def hs2(h):
            k = h % 2
            Y = YB[k]
            for c in range(2):
                cs = slice(c * 64, (c + 1) * 64)
                p.mm(Y[:, c * 64:(c + 1) * 64], kTt[k][:], qz[k][:, c, cs], True, True, [kTt[k], qz[k]], [Y])
            p.v("dve", "tensor_tensor", [Y, HC], [aT[k]], aT[k][:], Y[:, 0:128], Mbd, ALU.mult)

        def hs3(h):
            k = h % 2
            hh = h % 4
            hs = slice(h * 128, (h + 1) * 128)
            Y = YB[k]
            for c in range(2):
                wsl = slice(128 + c * 128, 256 + c * 128)
                p.mm(Y[:, wsl], kz[k][:, c, :], vb[:, hs], True, True, [kz[k], vb], [Y])
                sb_ = Sbf[c]
                p.v("act", "mul", [S, ex[k]], [sb_], sb_[:], S[:, h, :], ex[k][:, c:c + 1])
                p.mm(po[:, hh, :], qz[k][:, c, :], sb_[:], c == 0, False, [qz[k], sb_], [po], skip=True)
                p.v("dve", "tensor_scalar", [S, ex[k]], [T1], T1[:], S[:, h, :], ex[k][:, 2 + c:3 + c], None, ALU.mult)
                p.v("dve", "scalar_tensor_tensor", [Y, ex[k], T1], [S], out=S[:, h, :], in0=Y[:, wsl],
                    scalar=ex[k][:, 4 + c:5 + c], in1=T1[:], op0=ALU.mult, op1=ALU.add)
            p.mm(po[:, hh, :], aT[k][:], vb[:, hs], False, True, [aT[k], vb], [po], skip=True)
            if hh == 3:
                g4 = h // 4
                p.act(sq[:], po[:], AF.Square, [po], [sq])
                p.v("dve", "reduce_sum", [sq], [ss], ss[:, 0:4], sq[:], AX.X)
                p.v("dve", "tensor_scalar", [ss], [ss], ss[:, 4:8], ss[:, 0:4], 1.0 / 128, 1e-6, ALU.mult, ALU.add)
                p.act(ss[:, 4:8], ss[:, 4:8], AF.Sqrt, [ss], [ss])
                p.v("dve", "reciprocal", [ss], [ss], ss[:, 8:12], ss[:, 4:8])
                p.v("dve", "tensor_tensor", [po, ss], [t1], t1[:], po[:], ss[:, 8:12].unsqueeze(2).to_broadcast([128, 4, 128]), ALU.mult)
                p.v("dve", "tensor_tensor", [t1, GG], [o_tok[a]], o_tok[a][:, g4 * 512:(g4 + 1) * 512],
                    t1[:].rearrange("p h v -> p (h v)"), GG[:, g4 * 512:(g4 + 1) * 512], ALU.mult)

        hs1(0)
        for h in range(8):
            if h + 1 < 8:
                hs1(h + 1)
            hs2(h)
            hs3(h)
        out_proj_ln(p, C, o_tok[a], ptr, oT, Wo, PB[0], PB[1], xs[a], x_out[i * 128:(i + 1) * 128, :])
    p.barrier()
    p.pop()


def nsa_consts(rel_bias):
    s = np.arange(128)[:, None]
    t = np.arange(128)[None, :]
    bw = np.empty((2, 128, 16, 128), np.float32)
    for jj in range(2):
        d = t - s + 128 * jj
        tab = rel_bias[_t5_bucket_np(d)]
        if jj == 0:
            tab = np.where((d >= 0)[:, :, None], tab, np.float32(NEG))
        bw[jj] = np.transpose(tab, (0, 2, 1))
    tq = np.arange(128)[:, None]
    mq = 14 - np.arange(15)[None, :]
    dc = tq + 16 * mq - 127
    gc = rel_bias[_t5_bucket_np(dc)]
    gc = np.where((dc >= 0)[:, :, None], gc, np.float32(NEG))
    gc = np.ascontiguousarray(np.transpose(gc, (0, 2, 1)))
    ch = np.ascontiguousarray(rel_bias[31])
    mask4 = np.where(s > t, np.float32(0.0), np.float32(NEG)).astype(np.float32)
    return bw, gc, ch, mask4


def nsa_eexp(T):
    return (np.arange(T)[None, :] // 64 == np.arange(64)[:, None]).astype(np.float32)


def nsa_phase(p, C, T, x_in, x_out, w_in_d, w_out_d, pos_d, w1_d, w2_d, bw_d, gc_d, ch_d, m4_d, ee_d, g_d, b_d):
    nc = p.nc
    SKIP = set(os.environ.get("NSA_SKIP", "").split(","))
    p.push()
    NT = T // 128
    srcw = w_in_d.rearrange("(c p) n -> p c n", p=128)
    Wqp = p.sb([128, NCH, 8, 2, 64], BF16, "Wqp")
    for j in range(8):
        A = (j // 4) * 8 + j % 4
        for hf, hd in enumerate((A, A + 4)):
            p.dma("pool", Wqp[:, :, j, hf, :], srcw[:, :, hd * 64:(hd + 1) * 64], [], [Wqp])
    Wr = p.sb([128, NCH, 1584], BF16, "Wr")
    for sgm in range(4):
        p.dma("pool", Wr[:, sgm * 2:(sgm + 1) * 2, :], srcw[:, sgm * 2:(sgm + 1) * 2, 1024:2608], [], [Wr])
    OKC, OVC, OKS, OVS, OKW, OVW, OGT = 0, 256, 512, 768, 1024, 1280, 1536
    Wo = p.sb([128, NCH, 1024], BF16, "Wo")
    load_w_chunks(p, Wo, w_out_d, 0, 1024, 4)
    w1 = [p.sb([128, 32, 128], BF16, f"w1_{kv}") for kv in range(2)]
    w2k = p.sb([128, 2, 64], BF16, "w2k")
    w2v = p.sb([128, 64], BF16, "w2v")
    posT = p.sb([128, 2, 32], BF16, "posT")
    for kv in range(2):
        for hf in range(2):
            p.dma("pool", w1[kv][hf * 64:(hf + 1) * 64, :, :], w1_d[kv].rearrange("l d e -> d l e"), [], [w1[kv]])
            p.dma("pool", posT[hf * 64:(hf + 1) * 64, kv, :], pos_d[kv].rearrange("l d -> d l"), [], [posT],
                  allow_slow_non_contiguous=True)
    for dup in range(2):
        p.dma("pool", w2k[:, dup, :], w2_d[0], [], [w2k])
    p.dma("pool", w2v[:], w2_d[1], [], [w2v])
    load_ln(p, C, g_d, b_d)
    BW = p.sb([128, 2, 16, 128], F32, "BW")
    for jj in range(2):
        p.dma("sp", BW[:, jj, :, :], bw_d[jj], [], [BW])
    G8 = p.sb([128, 16, 15], F32, "G8")
    p.dma("sp", G8[:], gc_d, [], [G8])
    CHb = p.sb([128, 16], F32, "CHb")
    p.dma("sp", CHb[:], ch_d.rearrange("(o n) -> o n", o=1).to_broadcast([128, 16]), [], [CHb])
    M4 = p.sb([128, 128], F32, "M4")
    p.dma("sp", M4[:], m4_d, [], [M4])
    for jj in range(2):
        p.v("dve", "tensor_tensor", [BW, CHb], [BW], BW[:, jj, :, :], BW[:, jj, :, :],
            CHb[:].unsqueeze(2).to_broadcast([128, 16, 128]), ALU.subtract)
    p.v("dve", "tensor_tensor", [G8, CHb], [G8], G8[:], G8[:], CHb[:].unsqueeze(2).to_broadcast([128, 16, 15]), ALU.subtract)
    p.v("dve", "tensor_scalar", [G8], [G8], G8[:], G8[:], 8.0, None, ALU.mult)

    KSE = [p.sb([128, T], BF16, f"KSE{g}") for g in range(4)]
    for g in range(4):
        oh = 1 - g % 2
        p.dma("pool", KSE[g][oh * 64:(oh + 1) * 64, :], ee_d, [], [KSE[g]])
    VS = p.sb([128, NT, 4, 65], BF16, "VS")
    KW = p.sb([128, 5, 2, 128], BF16, "KW")
    VW = p.sb([128, 5, 4, 65], BF16, "VW")
    KC = p.sb([128, 2, 256], BF16, "KC")
    VC = p.sb([128, 2, 4, 64], BF16, "VC")
    p.v("pool", "memset", [], [VS], VS[:, :, :, 64:65], 1.0)
    p.v("pool", "memset", [], [VW], VW[:, :, :, 64:65], 1.0)
    p.v("pool", "memset", [], [VC], VC[:], 0.0)
    p.v("pool", "memset", [], [KC], KC[:], 0.0)
    rawr = [p.sb([64, 16, 9, 4], BF16, f"rawr{kv}") for kv in range(2)]
    for kv in range(2):
        p.v("pool", "memset", [], [rawr[kv]], rawr[kv][:], 0.0)
    cb = p.sb([128, 2], F32, "cb")
    hid = [p.sb([128, 32], BF16, f"hid{kv}") for kv in range(2)]
    for kv in range(2):
        p.v("pool", "memset", [], [hid[kv]], hid[kv][:], 0.0)
    vtmp = p.sb([8, 4, 64], BF16, "vtmp")
    xs = p.sb([128, D], F32, "xs")
    xb = p.sb([128, D], BF16, "xb")
    xT = p.sb([128, NCH, 128], BF16, "xT")
    QM = [p.sb([128, 4, 128], BF16, f"QM{g}") for g in range(4)]
    for g in range(4):
        p.v("pool", "memset", [], [QM[g]], QM[g][:], 0.0)
    gt = p.sb([128, 48], F32, "gt")
    gt3 = gt[:].rearrange("p (h b) -> p h b", b=3)
    sc = [p.sb([128, 4, 128], F32, f"sc{a}") for a in range(2)]
    e = [p.sb([128, 4, 128], BF16, f"e{a}") for a in range(4)]
    MTs = p.sb([128, 2, 128], BF16, "MTs")
    ef = p.sb([128, 4, 256], F32, "ef")
    eb = p.sb([128, 4, 256], BF16, "eb")
    p.v("pool", "memset", [], [eb], eb[:], 0.0)
    ebT = p.sb([128, 8, 128], BF16, "ebT")
    rs = p.sb([128, 16], F32, "rs")
    p.v("pool", "memset", [], [rs], rs[:], 0.0)
    fac = [p.sb([128, 8], F32, f"fac{a}") for a in range(2)]
    Pg = p.sb([128, 256], F32, "Pg")
    impb = p.sb([128, 4, 64], F32, "impb")
    impw = p.sb([128, 64], F32, "impw")
    mx = p.sb([128, 16], F32, "mx")
    selm = p.sb([128, 4, 64], BF16, "selm")
    oacc = p.sb([128, 16, 64], F32, "oacc")
    otmp = [p.sb([128, 4, 64], F32, f"otmp{a}") for a in range(2)]
    o_tok = p.sb([128, D], BF16, "otok")
    oT = p.sb([128, NCH, 128], BF16, "oT")
    ptr = p.ps([128, NCH, 128], BF16, "ptr")
    PA = [p.ps([128, 512], F32, f"PA{a}") for a in range(2)]
    PB = p.ps([128, 512], F32, "PB")
    psb = [p.ps([128, 512], F32, f"ps{a}") for a in range(2)]
    pacc = [p.ps([128, 512], F32, f"pacc{a}") for a in range(2)]
    cnt = {"ps": 0, "acc": 0, "sc": 0}
    print("nsa sbuf remaining:", nc.sbuf_bytes_remaining)

    p.pe_serial = True
    for kv in range(2):
        for l in range(32):
            p.mm(PB[:, kv:kv + 1], w1[kv][0:64, l, :], posT[0:64, kv, l:l + 1], l == 0, l == 31, [w1[kv], posT], [PB])
    p.pe_serial = False
    p.v("dve", "tensor_copy", [PB], [cb], cb[:], PB[:, 0:2])

    def next_ps():
        k = cnt["ps"] % 2
        cnt["ps"] += 1
        return k

    def next_acc():
        k = cnt["acc"] % 2
        cnt["acc"] += 1
        return pacc[k], k

    def finish_branch(pob, k, g, br, first):
        pov = pob[:, 0:260].rearrange("p (m d) -> p m d", m=4, d=65)
        fc = fac[k]
        p.v("dve", "reciprocal", [pob], [fc], fc[:, 0:4], pov[:, :, 64])
        p.v("dve", "tensor_tensor", [fc, gt], [fc], fc[:, 4:8], fc[:, 0:4], gt3[:, 4 * g:4 * g + 4, br], ALU.mult)
        dst = oacc[:, 4 * g:4 * g + 4, :] if first else otmp[k][:]
        p.v("dve", "tensor_tensor", [pob, fc], [oacc if first else otmp[k]], dst, pov[:, :, 0:64],
            fc[:, 4:8].unsqueeze(2).to_broadcast([128, 4, 64]), ALU.mult)
        if not first:
            p.v("pool", "tensor_tensor", [oacc, otmp[k]], [oacc], oacc[:, 4 * g:4 * g + 4, :], oacc[:, 4 * g:4 * g + 4, :],
                otmp[k][:], ALU.add)

    for i in range(NT):
        tsl = slice(i * 128, (i + 1) * 128)
        load_x_tile(p, C, x_in[tsl, :], xs, xb, ptr, xT)
        for j in range(8):
            pb = PA[j // 4]
            for c in range(NCH):
                p.mm(pb[:, (j % 4) * 128:(j % 4 + 1) * 128], Wqp[:, c, j, :, :], xT[:, c, :], c == 0, c == NCH - 1,
                     [Wqp, xT], [pb])
        for hf in range(2):
            for par in range(2):
                rs_ = slice(par * 64, par * 64 + 64)
                p.act(QM[2 * hf + par][rs_, :, :], PA[hf][rs_, :].rearrange("p (m n) -> p m n", m=4), AF.Copy, [PA[hf]],
                      [QM[2 * hf + par]])
        def fm(pt, off):
            for pr in range(2):
                for c in range(NCH):
                    p.mm(pt[:, pr * 128:(pr + 1) * 128], Wr[:, c, off + pr * 128:off + (pr + 1) * 128], xT[:, c, :],
                         c == 0, c == NCH - 1, [Wr, xT], [pt])
        for kv, off in ((0, OKC), (1, OVC)):
            pt = PB if kv == 0 else psb[0]
            p.v("dve", "tensor_copy", [rawr[kv]], [rawr[kv]], rawr[kv][:, :, 0, :], rawr[kv][:, :, 8, :])
            for g in range(4):
                for c in range(NCH):
                    p.mm(pt[0:64, g * 128:(g + 1) * 128], Wr[:, c, off + g * 64:off + (g + 1) * 64], xT[:, c, :],
                         c == 0, c == NCH - 1, [Wr, xT], [pt])
            p.v("dve", "tensor_copy", [pt], [rawr[kv]], rawr[kv][:, :, 1:9, :],
                pt[0:64, :].rearrange("p (g k b) -> p b k g", g=4, k=8, b=16))
        fm(psb[1], OKS)
        for g in range(4):
            rs_ = slice((g % 2) * 64, (g % 2) * 64 + 64)
            p.act(KSE[g][rs_, tsl], psb[1][rs_, (g // 2) * 128:(g // 2 + 1) * 128], AF.Copy, [psb[1]], [KSE[g]])
        fm(PB, OKW)
        p.act(KW[:, i % 5, :, :], PB[:, 0:256], AF.Copy, [PB], [KW])
        for c in range(NCH):
            p.mm(PA[0][:, 0:256], xT[:, c, :], Wr[:, c, OVS:OVS + 256], c == 0, c == NCH - 1, [Wr, xT], [PA[0]])
        p.v("dve", "tensor_copy", [PA[0]], [VS], VS[:, i, :, 0:64], PA[0][:, 0:256])
        for c in range(NCH):
            p.mm(PA[1][:, 0:256], xT[:, c, :], Wr[:, c, OVW:OVW + 256], c == 0, c == NCH - 1, [Wr, xT], [PA[1]])
        p.v("dve", "tensor_copy", [PA[1]], [VW], VW[:, i % 5, :, 0:64], PA[1][:, 0:256])
        for c in range(NCH):
            p.mm(PA[0][:, 256:304], xT[:, c, :], Wr[:, c, OGT:OGT + 48], c == 0, c == NCH - 1, [Wr, xT], [PA[0]])
        p.act(gt[:], PA[0][:, 256:304], AF.Sigmoid, [PA[0]], [gt])
        j0 = 1 if i == 0 else 0
        n_lo = 8 * i - 1 + j0
        n_hi = 8 * i + 6
        cn = n_hi - n_lo + 1
        if "cmp" not in SKIP:
            for l in range(32):
                a_, b_ = divmod(l, 16)
                for kv in range(2):
                    p.mm(PB[:, kv * 32:(kv + 1) * 32], w1[kv][0:64, l, :],
                         rawr[kv][:, b_, a_:a_ + 8, :].rearrange("p k g -> p (k g)"), l == 0 and kv == 0, l == 31,
                         [w1[kv], rawr[kv]], [PB], skip=True)
            for kv in range(2):
                p.act(hid[kv][:], PB[:, kv * 32:(kv + 1) * 32], AF.Gelu_apprx_tanh, [PB, cb], [hid[kv]], bias=cb[:, kv:kv + 1])
        p.mm(PB[:, 64:96], w2k[:], hid[0][:], True, True, [w2k, hid[0]], [PB])
        pk2 = PB[:, 64:96].rearrange("p (j g) -> p g j", g=4)
        for hf in range(2):
            hs = slice(hf * 64, hf * 64 + 64)
            p.v("dve", "tensor_copy", [PB], [KC], KC[hs, :, n_lo:n_hi + 1], pk2[hs, hf::2, j0:8])
        pv2 = PB[:, 128:384].rearrange("p (g d) -> p g d", g=4)
        p.pe_serial = True
        for g in range(4):
            p.mm(pv2[0:cn, g, :], hid[1][:, j0 * 4 + g:32:4], w2v[:], True, True, [w2v, hid[1]], [PB])
        p.pe_serial = False
        p.v("dve", "tensor_copy", [PB], [vtmp], vtmp[0:cn, :, :], pv2[0:cn, :, :])
        na = max(0, min(n_hi, 127) - n_lo + 1) if n_lo < 128 else 0
        if na > 0:
            p.dma("sp", VC[n_lo:n_lo + na, 0, :, :], vtmp[0:na, :, :], [vtmp], [VC])
        if cn - na > 0:
            st0 = n_lo + na - 128
            p.dma("sp", VC[st0:st0 + cn - na, 1, :, :], vtmp[na:cn, :, :], [vtmp], [VC])
        nv = 8 * i + 7
        nb0 = max(0, nv - 15)
        q0 = 15 - (nv - nb0)
        ntl = 1 if nv <= 128 else 2
        sel_on = i >= 8
        if sel_on:
            p.v("pool", "memset", [], [impb], impb[:], -1e9)
        for g in range(4 if "cattn" not in SKIP else 0):
            hs = slice((g % 2) * 64, (g % 2) * 64 + 64)
            pr = g // 2
            p.v("pool", "memset", [], [rs], rs[:, 0:4], 0.0)
            for hp in range(2):
                k = next_ps()
                Sc = psb[k][:].rearrange("p (m n) -> p m n", m=2)
                for m2 in range(2):
                    m = hp * 2 + m2
                    h = 4 * g + m
                    p.mm(Sc[:, m2, 0:nv], QM[g][hs, m, :], KC[hs, pr, 0:nv], True, True, [QM[g], KC], [psb[k]])
                    p.v("dve", "tensor_tensor", [psb[k], G8], [psb[k]], Sc[:, m2, nb0:nv], Sc[:, m2, nb0:nv], G8[:, h, q0:15], ALU.add)
                    p.act(ef[:, m, 0:nv], Sc[:, m2, 0:nv], AF.Exp, [psb[k]], [ef, rs], scale=0.125, accum_out=rs[:, m:m + 1])
            p.v("dve", "tensor_scalar", [rs], [rs], rs[:, 8:12], rs[:, 0:4], 1e-30, None, ALU.add)
            p.v("dve", "reciprocal", [rs], [rs], rs[:, 4:8], rs[:, 8:12])
            p.v("pool", "tensor_copy", [ef], [eb], eb[:, :, 0:nv], ef[:, :, 0:nv])
            if sel_on:
                p.v("dve", "tensor_scalar", [ef, rs], [Pg], Pg[:, 0:nv], ef[:, 0, 0:nv], rs[:, 4:5], None, ALU.mult)
                for m in range(1, 4):
                    p.v("dve", "scalar_tensor_tensor", [ef, rs, Pg], [Pg], out=Pg[:, 0:nv], in0=ef[:, m, 0:nv],
                        scalar=rs[:, 4 + m:5 + m], in1=Pg[:, 0:nv], op0=ALU.mult, op1=ALU.add)
                nj = 2 * i
                P4 = Pg[:, 0:4 * nj].rearrange("p (j r) -> p j r", r=4)
                p.v("dve", "reduce_sum", [Pg], [impb], impb[:, g, 0:nj], P4, AX.X)
                p.v("dve", "tensor_tensor", [Pg, impb], [impb], impb[:, g, 1:nj], impb[:, g, 1:nj], P4[:, 0:nj - 1, 3], ALU.add)
            for nt in range(ntl):
                for m in range(4):
                    p.tr(ptr[:, nt * 4 + m, :], eb[:, m, nt * 128:(nt + 1) * 128], C.ident[:], [eb, C.ident], [ptr])
            p.v("dve", "tensor_copy", [ptr], [ebT], ebT[:, 0:4 * ntl, :], ptr[:, 0:4 * ntl, :])
            pob, ka = next_acc()
            pov = pob[:, 0:260].rearrange("p (m d) -> p m d", m=4, d=65)
            for m in range(4):
                for nt in range(ntl):
                    p.mm(pov[:, m, 0:64], ebT[:, nt * 4 + m, :], VC[:, nt, g, :], m == 0 and nt == 0, nt == ntl - 1,
                         [ebT, VC], [pob], skip=True)
            fc = fac[ka]
            p.v("dve", "tensor_tensor", [rs, gt], [fc], fc[:, 4:8], rs[:, 4:8], gt3[:, 4 * g:4 * g + 4, 0], ALU.mult)
            p.v("dve", "tensor_tensor", [pob, fc], [oacc], oacc[:, 4 * g:4 * g + 4, :], pov[:, :, 0:64],
                fc[:, 4:8].unsqueeze(2).to_broadcast([128, 4, 64]), ALU.mult)
        if sel_on:
            p.v("pool", "memset", [impb], [impb], impb[:, :, 0:1], 1e9)
            p.v("pool", "memset", [impb], [impb], impb[0:64, :, 2 * i - 1:2 * i], 2e9)
            p.v("pool", "memset", [impb], [impb], impb[0:64, :, 2 * i:2 * i + 1], 3e9)
            p.v("pool", "memset", [impb], [impb], impb[0:64, :, 2 * i + 1:2 * i + 2], -1e9)
            p.v("pool", "memset", [impb], [impb], impb[64:128, :, 2 * i:2 * i + 1], 2e9)
            p.v("pool", "memset", [impb], [impb], impb[64:128, :, 2 * i + 1:2 * i + 2], 3e9)
            for g in range(4):
                p.v("dve", "max", [impb], [mx], mx[:, 0:8], impb[:, g, :])
                p.v("dve", "match_replace", [mx, impb], [impw], out=impw[:], in_to_replace=mx[:, 0:8], in_values=impb[:, g, :],
                    imm_value=-3e9)
                p.v("dve", "max", [impw], [mx], mx[:, 8:16], impw[:])
                p.v("dve", "tensor_scalar", [impb, mx], [selm], selm[:, g ^ 1, :], impb[:, g, :], mx[:, 15:16], None, ALU.is_ge)
            for j in range(2):
                p.tr(ptr[:, j, :], selm[:, 2 * j:2 * j + 2, :].rearrange("p g b -> p (g b)"), C.ident[:], [selm, C.ident], [ptr])
            p.v("dve", "tensor_copy", [ptr], [MTs], MTs[:], ptr[:, 0:2, :])
            for g in range(4):
                oh = 1 - g % 2
                rs_ = slice(oh * 64, oh * 64 + 64)
                p.v("pool", "tensor_scalar", [MTs], [QM[g]], QM[g][rs_, :, :],
                    MTs[rs_, g // 2, :].unsqueeze(1).to_broadcast([64, 4, 128]), 1.0, 30000.0, ALU.subtract, ALU.mult)
        items = []
        for g in range(4):
            for br in (1, 2):
                if ("sel" in SKIP and br == 1) or ("win" in SKIP and br == 2):
                    continue
                kts = list(range(0, i + 1)) if br == 1 else list(range(max(0, i - 4), i + 1))
                for idx, kt in enumerate(kts):
                    items.append((g, br, kt, idx == 0, idx == len(kts) - 1))
        SBK = [psb[0], psb[1], PA[0], PA[1]]
        NB = 4
        LA = int(os.environ.get("NSA_LA", "2"))

        def kv_of(g, br, kt):
            hs = slice((g % 2) * 64, (g % 2) * 64 + 64)
            pr = g // 2
            if br == 1:
                return KSE[g][:, kt * 128:(kt + 1) * 128], VS[:, kt, g, :], KSE[g], VS
            return KW[hs, kt % 5, pr, :], VW[:, kt % 5, g, :], KW, VW

        def stage_a(n):
            g, br, kt, first, last = items[n]
            k = n % NB
            hs = slice((g % 2) * 64, (g % 2) * 64 + 64)
            pr = g // 2
            kk_, vv_, kbuf, vbuf = kv_of(g, br, kt)
            S4 = SBK[k][:].rearrange("p (m n) -> p m n", m=4)
            if br == 1:
                p.mm(S4, kk_, QM[g][:, :, :], True, True, [kbuf, QM[g]], [SBK[k]])
            else:
                p.mm(S4, kk_, QM[g][hs, :, :], True, True, [kbuf, QM[g]], [SBK[k]])
            jj = i - kt
            if jj <= 1 or (br == 2 and jj == 4):
                q = cnt["sc"] % 2
                cnt["sc"] += 1
                in1 = BW[:, jj, 4 * g:4 * g + 4, :] if jj <= 1 else M4[:].unsqueeze(1).to_broadcast([128, 4, 128])
                p.v("dve", "scalar_tensor_tensor", [SBK[k], BW, M4], [sc[q]], out=sc[q][:], in0=S4, scalar=0.125,
                    in1=in1, op0=ALU.mult, op1=ALU.add)
                p.act(e[k][:], sc[q][:], AF.Exp, [sc[q]], [e[k]])
            elif "exp" not in SKIP:
                p.act(e[k][:], S4, AF.Exp, [SBK[k]], [e[k]], scale=0.125)

        def stage_c(n):
            g, br, kt, first, last = items[n]
            k = n % NB
            kk_, vv_, kbuf, vbuf = kv_of(g, br, kt)
            qacc = (g * 2 + br) % 2
            pob = pacc[qacc]
            pov = pob[:, 0:260].rearrange("p (m d) -> p m d", m=4, d=65)
            src = e[k]
            for m in range(4 if "pv" not in SKIP else 1):
                p.mm(pov[:, m, :], src[:, m, :], vv_, first and m == 0, last, [src, vbuf], [pob], skip=True)
            if last:
                finish_branch(pob, qacc, g, br, False)

        NI = len(items)
        for n in range(NI + LA):
            if n < NI:
                stage_a(n)
            if n - LA >= 0:
                stage_c(n - LA)
        p.act(o_tok[:], oacc[:].rearrange("p h d -> p (h d)"), AF.Copy, [oacc], [o_tok])
        out_proj_ln(p, C, o_tok, ptr, oT, Wo, PA[0], PA[1], xs, x_out[tsl, :])
    p.barrier()
    p.pop()


def build(T, plan, extra_inputs):
    nc = bass.Bass("TRN2", target_bir_lowering=False)
    dr = {}

    def din(name, shape):
        dr[name] = nc.dram_tensor(name, list(shape), F32, kind="ExternalInput").ap()
        return dr[name]

    x = din("x", [T, D])
    for name, shape in extra_inputs:
        din(name, shape)
    y = nc.dram_tensor("y", [T, D], F32, kind="ExternalOutput").ap()
    bufs = [nc.dram_tensor(f"act{i}", [T, D], F32, kind="Internal").ap() for i in range(2)]
    with ExitStack() as st:
        p = Prog(nc, st)
        C = setup_common(p)
        cur = x
        for i, ph in enumerate(plan):
            dst = y if i == len(plan) - 1 else bufs[i % 2]
            if ph["kind"] == "ffn":
                L, w = ph["layer"], ph["which"]
                ffn_phase(p, C, T, cur, dst, dr[f"ffn{w}_w_gate"][L], dr[f"ffn{w}_w_up"][L], dr[f"ffn{w}_w_down"][L],
                          dr["ln_gain"][L, 0 if w == 1 else 2], dr["ln_bias"][L, 0 if w == 1 else 2])
            elif ph["kind"] == "swa":
                L = ph["layer"]
                swa_phase(p, C, T, cur, dst, dr["swa_w_in"][0], dr["swa_w_out"][0], dr["swa_bias"], dr["swa_sinks_p"],
                          dr["ln_gain"][L, 1], dr["ln_bias"][L, 1])
            elif ph["kind"] == "hgrn":
                L = ph["layer"]
                hgrn_phase(p, C, T, cur, dst, L, dr["hgrn_w_in"][0], dr["hgrn_w_out"][0], dr["hgrn_norm_gain"][0], dr["hgrn_lb"],
                           dr["hgrn_c"], dr["ln_gain"][L, 1], dr["ln_bias"][L, 1])
            elif ph["kind"] == "nsa":
                L, sl = ph["layer"], ph["slot"]
                nsa_phase(p, C, T, cur, dst, dr["nsa_w_in"][sl], dr["nsa_w_out"][sl], dr["nsa_cmp_pos"][sl], dr["nsa_cmp_w1"][sl],
                          dr["nsa_cmp_w2"][sl], dr["nsa_bw"], dr["nsa_gc"], dr["nsa_ch"], dr["nsa_m4"], dr["nsa_ee"],
                          dr["ln_gain"][L, 1], dr["ln_bias"][L, 1])
            elif ph["kind"] == "dummy_dve":
                for _ in range(ph["n"]):
                    p.v("dve", "memset", [], [C.mv[0]], C.mv[0][:, 0:1], 0.0)
                p.barrier()
            elif ph["kind"] == "dummy":
                for _ in range(ph["n"]):
                    p.dma("sp", dst, cur, [], [])
                p.barrier()
            else:
                raise ValueError(ph)
            cur = dst
        p.barrier()
        print("instructions:", p.nins, {k: p.cnt[k] for k in p.cnt}, "waits:", p.nwait)
    return nc


SEQ = 4096
BATCH = 8
_NC_CACHE = {}


def full_plan():
    plan = []
    for L in range(DEPTH):
        plan.append({"kind": "ffn", "layer": L, "which": 1})
        kind = L % 3
        if kind == 0:
            plan.append({"kind": "nsa", "layer": L, "slot": L // 3})
        elif kind == 1:
            plan.append({"kind": "hgrn", "layer": L})
        else:
            plan.append({"kind": "swa", "layer": L})
        plan.append({"kind": "ffn", "layer": L, "which": 2})
    return plan


def kernel(**inputs):
    inp = {k: np.ascontiguousarray(np.asarray(v, dtype=np.float32)) for k, v in inputs.items()}
    x = inp.pop("x")
    B, T, _ = x.shape
    swa_bias, swa_sp = swa_consts(inp["rel_bias"], inp["swa_sinks"][0])
    bw, gc, ch, m4 = nsa_consts(inp["rel_bias"])
    shared = dict(inp)
    shared.pop("rel_bias")
    shared.pop("swa_sinks")
    shared.update({"swa_bias": swa_bias, "swa_sinks_p": swa_sp, "nsa_bw": bw, "nsa_gc": gc, "nsa_ch": ch, "nsa_m4": m4, "nsa_ee": nsa_eexp(T),
                   "hgrn_c": hgrn_consts()})
    extra = [(k, v.shape) for k, v in shared.items()]
    key = (T,)
    if key not in _NC_CACHE:
        _NC_CACHE[key] = build(T, full_plan(), extra)
    nc = _NC_CACHE[key]
    in_maps = [dict(shared, x=np.ascontiguousarray(x[b])) for b in range(B)]
    res = run_bass_kernel_spmd(nc, in_maps, core_ids=list(range(B)))
    return np.stack([np.asarray(r["y"], dtype=np.float32) for r in res.results], axis=0)
```

```python
import math
import numpy as np
import ml_dtypes
from contextlib import ExitStack
import concourse.bass as bass
import concourse.mybir as mybir
from concourse.bass_utils import run_bass_kernel_spmd

F32 = mybir.dt.float32
BF16 = mybir.dt.bfloat16
AF = mybir.ActivationFunctionType
ALU = mybir.AluOpType
AX = mybir.AxisListType

D = 1024
DFF = 2816
DEPTH = 4
NCH = D // 128
NFC = DFF // 128
ALPHA = (2.0 * DEPTH) ** 0.25
LN_EPS = 1e-5
NDS = 24
NEG = -30000.0
EMBED_WAITS = True
import os
EMBED_ENG = set(os.environ.get('EMBED_ENG', 'pe,dve,act,pool').split(','))


class Buf:
    __slots__ = ("name", "w", "r")

    def __init__(self, name):
        self.name = name
        self.w = None
        self.r = {}


class Tile:
    def __init__(self, t, name):
        self.t = t
        self.b = Buf(name)
        self.subs = {}

    def __getitem__(self, idx):
        return self.t[idx]

    def sub(self, key):
        if key not in self.subs:
            self.subs[key] = Buf(f"{self.b.name}.{key}")
        return self.subs[key]


def _bufs(xs):
    out = []
    for x in xs:
        if x is None:
            continue
        if isinstance(x, Buf):
            out.append(x)
        else:
            out.append(x.b)
            out.extend(x.subs.values())
    return out


class Prog:
    def __init__(self, nc, stack):
        self.nc = nc
        self.stacks = [stack]
        self.E = {"pe": nc.tensor, "dve": nc.vector, "act": nc.scalar, "pool": nc.gpsimd, "sp": nc.sync}
        self.sem = {k: stack.enter_context(nc.semaphore("s_" + k)) for k in self.E}
        self.cnt = {k: 0 for k in self.E}
        self.known = {k: {} for k in self.E}
        self.dsems = [stack.enter_context(nc.semaphore(f"dq{i}")) for i in range(NDS)]
        self.dcnt = [0] * NDS
        self.dnext = 0
        self.dnext2 = 0
        self.nwait = {}
        self.pe_serial = False
        self.nins = 0
        self.uid = 0

    def push(self):
        st = ExitStack()
        st.__enter__()
        self.stacks.append(st)

    def pop(self):
        st = self.stacks.pop()
        st.__exit__(None, None, None)

    def sb(self, shape, dt, name):
        self.uid += 1
        nm = f"{name}_{self.uid}"
        t = self.stacks[-1].enter_context(self.nc.sbuf_tensor(nm, list(shape), dt))
        return Tile(t, nm)

    def ps(self, shape, dt, name):
        self.uid += 1
        nm = f"{name}_{self.uid}"
        t = self.stacks[-1].enter_context(self.nc.psum_tensor(nm, list(shape), dt))
        return Tile(t, nm)

    def _wait(self, e, tok):
        sem, v, key = tok
        if self.known[e].get(key, 0) >= v:
            return
        self.E[e].wait_ge(sem, v)
        self.known[e][key] = v
        self.nins += 1
        self.nwait[e] = self.nwait.get(e, 0) + 1

    def _need(self, e, reads, writes):
        reads = _bufs(reads)
        writes = _bufs(writes)
        need = {}

        def add(tok):
            sem, v, key = tok
            if self.known[e].get(key, 0) >= v:
                return
            if key not in need or need[key][1] < v:
                need[key] = tok

        for b in reads:
            if b.w is not None:
                add(b.w)
        for b in writes:
            if b.w is not None and not (e == "pe" and b.w[2] == "pe" and not self.pe_serial):
                add(b.w)
            for k, t in b.r.items():
                if not (e == "pe" and k == "pe"):
                    add(t)
        return reads, writes, list(need.values())

    def _issue(self, e, fn, need):
        for tok in need[:-1]:
            self._wait(e, tok)
        ins = fn()
        if need:
            sem, v, key = need[-1]
            if EMBED_WAITS:
                ins._wait_ge(sem, v)
                self.known[e][key] = v
            else:
                raise RuntimeError
        return ins

    def _post(self, tok, reads, writes):
        key = tok[2]
        for b in reads:
            b.r[key] = tok
        for b in writes:
            b.w = tok
            b.r = {}

    def op(self, e, fn, reads, writes):
        reads, writes, need = self._need(e, reads, writes)
        if e not in EMBED_ENG:
            for tok in need:
                self._wait(e, tok)
            need = []
        ins = self._issue(e, fn, need)
        self.cnt[e] += 1
        ins.then_inc(self.sem[e], 1)
        self._post((self.sem[e], self.cnt[e], e), reads, writes)
        self.nins += 1
        return ins

    def dma(self, q, out, in_, reads, writes, **kw):
        if q == "sp":
            i = self.dnext
            self.dnext = (i + 1) % 16
        else:
            i = 16 + self.dnext2
            self.dnext2 = (self.dnext2 + 1) % (NDS - 16)
        key = ("d", i)
        if self.dcnt[i] > 0:
            self._wait(q, (self.dsems[i], self.dcnt[i], key))
        reads, writes, need = self._need(q, reads, writes)
        for tok in need:
            self._wait(q, tok)
        ins = self.E[q].dma_start(out=out, in_=in_, **kw)
        self.dcnt[i] += 16
        ins.then_inc(self.dsems[i], 16)
        self._post((self.dsems[i], self.dcnt[i], key), reads, writes)
        self.nins += 1

    def barrier(self, engines=None):
        for e in (engines or self.E):
            for f in self.E:
                if f != e and self.cnt[f] > 0:
                    self._wait(e, (self.sem[f], self.cnt[f], f))
            for i in range(NDS):
                if self.dcnt[i] > 0:
                    self._wait(e, (self.dsems[i], self.dcnt[i], ("d", i)))

    def mm(self, out, lhsT, rhs, start, stop, reads, writes, skip=False):
        if skip:
            return self.op("pe", lambda: self.nc.tensor.matmul(out, lhsT, rhs, start=start, stop=stop,
                                                               skip_group_check=True), reads, writes)
        return self.op("pe", lambda: self.nc.tensor.matmul(out, lhsT, rhs, start=start, stop=stop), reads, writes)

    def tr(self, out, in_, ident, reads, writes):
        return self.op("pe", lambda: self.nc.tensor.transpose(out, in_, ident), reads, writes)

    def act(self, out, in_, func, reads, writes, **kw):
        return self.op("act", lambda: self.nc.scalar.activation(out, in_, func, **kw), reads, writes)

    def v(self, e, name, reads, writes, *a, **kw):
        eng = self.E[e]
        return self.op(e, lambda: getattr(eng, name)(*a, **kw), reads, writes)


class Ctx:
    pass


class Tile4:
    def __init__(self, tile):
        self.t = tile
        self.b = tile.b
        self.subs = tile.subs

    def __getitem__(self, idx):
        v = self.t[:].rearrange("p (m n) -> p m n", m=4)
        return v[idx]


def setup_common(p):
    nc = p.nc
    C = Ctx()
    C.identf = p.sb([128, 128], F32, "identf")
    C.ident = p.sb([128, 128], BF16, "ident")
    p.v("pool", "memset", [], [C.identf], C.identf[:], 0.0)
    p.op("pool", lambda: nc.gpsimd.affine_select(out=C.identf[:], in_=C.identf[:], pattern=[[-1, 128]],
                                                 compare_op=ALU.not_equal, fill=1.0, base=0, channel_multiplier=1),
         [C.identf], [C.identf])
    p.v("dve", "tensor_copy", [C.identf], [C.ident], C.ident[:], C.identf[:])
    C.G = p.sb([128, D], F32, "lnG")
    C.Bt = p.sb([128, D], F32, "lnB")
    C.st6 = [p.sb([128, 2, 6], F32, f"st6{i}") for i in range(2)]
    C.mv = [p.sb([128, 4], F32, f"mv{i}") for i in range(2)]
    C.ysb = None
    C.epi = 0
    return C


def load_ln(p, C, g_d, b_d, nysb=2):
    C.ysb = [p.sb([128, D], F32, f"ysb{i}") for i in range(nysb)]
    p.dma("sp", C.G[:], g_d.rearrange("(o n) -> o n", o=1).to_broadcast([128, D]), [], [C.G])
    p.dma("sp", C.Bt[:], b_d.rearrange("(o n) -> o n", o=1).to_broadcast([128, D]), [], [C.Bt])


def ln_epilogue(p, C, xs, py0, py1, yscale, out_rows):
    k = C.epi % 2
    C.epi += 1
    ysb, st6, mv = C.ysb[k % len(C.ysb)], C.st6[k], C.mv[k]
    p.act(ysb[:, 0:512], py0[:], AF.Copy, [py0], [ysb], scale=yscale)
    p.act(ysb[:, 512:1024], py1[:], AF.Copy, [py1], [ysb], scale=yscale)
    p.v("dve", "scalar_tensor_tensor", [xs, ysb], [ysb], out=ysb[:], in0=xs[:], scalar=ALPHA, in1=ysb[:],
        op0=ALU.mult, op1=ALU.add)
    for c in range(2):
        p.v("dve", "bn_stats", [ysb], [st6], st6[:, c, :], ysb[:, c * 512:(c + 1) * 512])
    p.v("dve", "bn_aggr", [st6], [mv], mv[:, 0:2], st6[:])
    p.v("dve", "tensor_scalar", [mv], [mv], mv[:, 3:4], mv[:, 1:2], LN_EPS, None, ALU.add)
    p.act(mv[:, 3:4], mv[:, 3:4], AF.Sqrt, [mv], [mv])
    p.v("dve", "reciprocal", [mv], [mv], mv[:, 2:3], mv[:, 3:4])
    p.v("dve", "tensor_scalar", [ysb, mv], [ysb], ysb[:], ysb[:], mv[:, 0:1], mv[:, 2:3], ALU.subtract, ALU.mult)
    p.v("pool", "tensor_tensor", [ysb, C.G], [ysb], ysb[:], ysb[:], C.G[:], ALU.mult)
    p.v("pool", "tensor_tensor", [ysb, C.Bt], [ysb], ysb[:], ysb[:], C.Bt[:], ALU.add)
    p.dma("sp", out_rows, ysb[:], [ysb], [])


def ffn_phase(p, C, T, x_in, x_out, wg_d, wu_d, wd_d, g_d, b_d):
    p.push()
    Wg = p.sb([128, NCH, DFF], BF16, "Wg")
    Wu = p.sb([128, NCH, DFF], BF16, "Wu")
    Wd = p.sb([128, NFC, D], BF16, "Wd")
    for c in range(NCH):
        p.dma("pool", Wg[:, c, :], wg_d[c * 128:(c + 1) * 128, :], [], [Wg.sub(c)])
    for c in range(NCH):
        p.dma("pool", Wu[:, c, :], wu_d[c * 128:(c + 1) * 128, :], [], [Wu.sub(c)])
    for f in range(NFC):
        p.dma("pool", Wd[:, f, :], wd_d[f * 128:(f + 1) * 128, :], [], [Wd.sub(f)])
    load_ln(p, C, g_d, b_d)
    xs = [[p.sb([128, D], F32, f"xs{a}{j}") for j in range(2)] for a in range(2)]
    xb = [p.sb([128, D], BF16, f"xb{j}") for j in range(2)]
    xT = [p.sb([128, NCH, 256], BF16, f"xT{a}") for a in range(2)]
    h = p.sb([128, NFC, 256], BF16, "h")
    sg = [p.sb([128, 256], F32, f"sg{a}") for a in range(2)]
    ptr = [p.ps([128, NCH, 128], BF16, f"ptr{a}") for a in range(2)]
    pgu = [p.ps([128, 2, 256], F32, f"pgu{a}") for a in range(2)]
    py = [p.ps([128, 512], F32, f"py{a}") for a in range(4)]
    NT = T // 256

    def prep(t):
        a = t % 2
        for j in range(2):
            r0 = t * 256 + j * 128
            p.dma("sp", xs[a][j][:], x_in[r0:r0 + 128, :], [], [xs[a][j]])
            p.act(xb[j][:], xs[a][j][:], AF.Copy, [xs[a][j]], [xb[j]])
            for c in range(NCH):
                p.tr(ptr[j][:, c, :], xb[j][:, c * 128:(c + 1) * 128], C.ident[:], [xb[j], C.ident], [ptr[j]])
            p.v("dve", "tensor_copy", [ptr[j]], [xT[a]], xT[a][:, :, j * 128:(j + 1) * 128], ptr[j][:])

    prep(0)
    for t in range(NT):
        a = t % 2
        for f in range(NFC):
            k = f % 2
            for c in range(NCH):
                p.mm(pgu[k][:, 0, :], Wg[:, c, f * 128:(f + 1) * 128], xT[a][:, c, :], c == 0, c == NCH - 1,
                     [Wg.sub(c), xT[a]], [pgu[k]])
            for c in range(NCH):
                p.mm(pgu[k][:, 1, :], Wu[:, c, f * 128:(f + 1) * 128], xT[a][:, c, :], c == 0, c == NCH - 1,
                     [Wu.sub(c), xT[a]], [pgu[k]])
            p.act(sg[k][:], pgu[k][:, 0, :], AF.Silu, [pgu[k]], [sg[k]])
            p.v("dve", "tensor_tensor", [sg[k], pgu[k]], [h], h[:, f, :], sg[k][:], pgu[k][:, 1, :], ALU.mult)
        if t + 1 < NT:
            prep(t + 1)
        for j in range(2):
            for hf in range(2):
                pb = py[j * 2 + hf]
                for f in range(NFC):
                    p.mm(pb[:], h[:, f, j * 128:(j + 1) * 128], Wd[:, f, hf * 512:(hf + 1) * 512], f == 0,
                         f == NFC - 1, [h, Wd.sub(f)], [pb])
            r0 = t * 256 + j * 128
            ln_epilogue(p, C, xs[a][j], py[j * 2], py[j * 2 + 1], 0.5, x_out[r0:r0 + 128, :])
    p.barrier()
    p.pop()


def _t5_bucket_np(dist):
    n = np.maximum(dist, 0)
    nf = np.maximum(n, 1).astype(np.float32)
    large = 16 + (np.log(nf / np.float32(16)) / np.float32(math.log(128 / 16)) * np.float32(16)).astype(np.int32)
    return np.where(n < 16, n, np.minimum(large, 31))


SWA_HPERM = [g * 8 + 2 * m + par for g in range(2) for par in range(2) for m in range(4)]


def swa_consts(rel_bias, sinks):
    s = np.arange(128)[:, None]
    t = np.arange(128)[None, :]
    out = np.empty((2, 128, 16, 128), np.float32)
    for jj in range(2):
        d = t - s + 128 * jj
        valid = (d >= 0) & (d < 128)
        tab = rel_bias[_t5_bucket_np(d)]
        tab = np.where(valid[:, :, None], tab, np.float32(NEG))
        out[jj] = np.transpose(tab[:, :, SWA_HPERM], (0, 2, 1))
    return out, np.ascontiguousarray(sinks[SWA_HPERM])


def load_x_tile(p, C, x_rows, xs, xb, ptr, xT_dst):
    p.dma("sp", xs[:], x_rows, [], [xs])
    p.act(xb[:], xs[:], AF.Copy, [xs], [xb])
    for c in range(NCH):
        p.tr(ptr[:, c, :], xb[:, c * 128:(c + 1) * 128], C.ident[:], [xb, C.ident], [ptr])
    p.v("dve", "tensor_copy", [ptr], [xT_dst], xT_dst[:], ptr[:])


def out_proj_ln(p, C, o_tok, ptr, oT, Wo, py0, py1, xs, out_rows):
    for c in range(NCH):
        p.tr(ptr[:, c, :], o_tok[:, c * 128:(c + 1) * 128], C.ident[:], [o_tok, C.ident], [ptr])
    p.v("dve", "tensor_copy", [ptr], [oT], oT[:], ptr[:])
    for hf, pb in enumerate((py0, py1)):
        for c in range(NCH):
            p.mm(pb[:], oT[:, c, :], Wo[:, c, hf * 512:(hf + 1) * 512], c == 0, c == NCH - 1, [oT, Wo], [pb])
    ln_epilogue(p, C, xs, py0, py1, 1.0, out_rows)


def load_w_chunks(p, W, w_d, col0, ncols, nsplit=2):
    src = w_d.rearrange("(c p) n -> p c n", p=128)
    step = NCH // nsplit
    for s in range(nsplit):
        p.dma("pool", W[:, s * step:(s + 1) * step, 0:ncols], src[:, s * step:(s + 1) * step, col0:col0 + ncols], [], [W])


def swa_phase(p, C, T, x_in, x_out, w_in_d, w_out_d, bias_d, sinks_d, g_d, b_d):
    nc = p.nc
    p.push()
    Wq = p.sb([128, NCH, 1024], BF16, "Wq")
    Wk2 = p.sb([128, NCH, 2, 2, 64], BF16, "Wk2")
    Wv = p.sb([128, NCH, 128], BF16, "Wv")
    Wo = p.sb([128, NCH, 1024], BF16, "Wo")
    load_w_chunks(p, Wq, w_in_d, 0, 1024, 4)
    src = w_in_d.rearrange("(c p) n -> p c n", p=128)
    for g in range(2):
        for dup in range(2):
            p.dma("pool", Wk2[:, :, g, dup, :], src[:, :, 1024 + g * 64:1024 + (g + 1) * 64], [], [Wk2])
    load_w_chunks(p, Wv, w_in_d, 1152, 128, 1)
    load_w_chunks(p, Wo, w_out_d, 0, 1024, 4)
    load_ln(p, C, g_d, b_d)
    BT = p.sb([128, 2, 16, 128], F32, "BT")
    for jj in range(2):
        p.dma("sp", BT[:, jj, :, :], bias_d[jj], [], [BT])
    ES = p.sb([128, 16], F32, "ES")
    p.dma("sp", ES[:], sinks_d.rearrange("(o n) -> o n", o=1).to_broadcast([128, 16]), [], [ES])
    p.act(ES[:], ES[:], AF.Exp, [ES], [ES])

    xs = [p.sb([128, D], F32, f"xs{a}") for a in range(2)]
    xb = [p.sb([128, D], BF16, f"xb{a}") for a in range(2)]
    xT = [p.sb([128, NCH, 128], BF16, f"xT{a}") for a in range(2)]
    qT = [p.sb([128, 8, 128], BF16, f"qT{a}") for a in range(2)]
    kbuf = [p.sb([128, 2, 128], BF16, f"kb{a}") for a in range(2)]
    vbuf = [p.sb([128, 2, 65], BF16, f"vb{a}") for a in range(2)]
    for a in range(2):
        p.v("dve", "memset", [], [vbuf[a]], vbuf[a][:, :, 64:65], 1.0)
    sc = [p.sb([128, 4, 128], F32, f"sc{a}") for a in range(2)]
    e = [p.sb([128, 4, 128], BF16, f"e{a}") for a in range(4)]
    den = [p.sb([128, 8], F32, f"den{a}") for a in range(2)]
    o_tok = [p.sb([128, D], BF16, f"otok{a}") for a in range(2)]
    oT = p.sb([128, NCH, 128], BF16, "oT")
    ptr = [p.ps([128, NCH, 128], BF16, f"ptr{a}") for a in range(1)]
    pq = [p.ps([128, 512], F32, f"pq{a}") for a in range(2)]
    pkv = p.ps([128, 512], F32, "pkv")
    ps = [p.ps([128, 4, 128], F32, f"ps{a}") for a in range(2)]
    po = [p.ps([128, 512], F32, f"po{a}") for a in range(2)]
    ps = ps + [Tile4(pq[0]), Tile4(pq[1])]
    NT = T // 128
    nsc = 0
    for i in range(NT):
        a = i % 2
        load_x_tile(p, C, x_in[i * 128:(i + 1) * 128, :], xs[a], xb[a], ptr[0], xT[a])
        for m in range(8):
            pb = pq[m // 4]
            for c in range(NCH):
                p.mm(pb[:, (m % 4) * 128:(m % 4 + 1) * 128], Wq[:, c, m * 128:(m + 1) * 128], xT[a][:, c, :], c == 0,
                     c == NCH - 1, [Wq, xT[a]], [pb])
        for g in range(2):
            for c in range(NCH):
                p.mm(pkv[:, g * 128:(g + 1) * 128], Wk2[:, c, g, :, :], xT[a][:, c, :], c == 0, c == NCH - 1,
                     [Wk2, xT[a]], [pkv])
        for c in range(NCH):
            p.mm(pkv[:, 256:384], xT[a][:, c, :], Wv[:, c, :], c == 0, c == NCH - 1, [Wv, xT[a]], [pkv])
        p.act(qT[a][:, 0:4, :], pq[0][:], AF.Copy, [pq[0]], [qT[a]])
        p.act(qT[a][:, 4:8, :], pq[1][:], AF.Copy, [pq[1]], [qT[a]])
        p.v("dve", "tensor_copy", [pkv], [kbuf[a]], kbuf[a][:], pkv[:, 0:256])
        p.v("dve", "tensor_copy", [pkv], [vbuf[a]], vbuf[a][:, :, 0:64], pkv[:, 256:384])
        kts = ([i - 1] if i > 0 else []) + [i]
        ot4 = o_tok[a][:].rearrange("p (m r d) -> p m r d", m=8, r=2, d=64)
        items = [(g, par, kt, kt == kts[0], kt == kts[-1]) for g in range(2) for par in range(2) for kt in kts]
        LA = 2

        def stage_a(n):
            g, par, kt, first, last = items[n]
            b = g * 2 + par
            jj = i - kt
            k = (nsc + n) % 4
            p.mm(ps[k][:], kbuf[kt % 2][par * 64:(par + 1) * 64, g, :],
                 qT[a][par * 64:(par + 1) * 64, g * 4:(g + 1) * 4, :], True, True, [kbuf[kt % 2], qT[a]], [ps[k]])
            p.v("dve", "scalar_tensor_tensor", [ps[k], BT], [sc[k % 2]], out=sc[k % 2][:], in0=ps[k][:], scalar=0.125,
                in1=BT[:, jj, b * 4:(b + 1) * 4, :], op0=ALU.mult, op1=ALU.add)
            p.act(e[k][:], sc[k % 2][:], AF.Exp, [sc[k % 2]], [e[k]])

        def stage_c(n):
            g, par, kt, first, last = items[n]
            b = g * 2 + par
            k = (nsc + n) % 4
            pob = po[b % 2]
            pov = pob[:, 0:260].rearrange("p (m d) -> p m d", m=4, d=65)
            for m in range(4):
                p.mm(pov[:, m, :], e[k][:, m, :], vbuf[kt % 2][:, g, :], first and m == 0, last,
                     [e[k], vbuf[kt % 2]], [pob], skip=True)
            if last:
                dn = den[b % 2]
                p.v("dve", "tensor_tensor", [pob, ES], [dn], dn[:, 0:4], pov[:, :, 64], ES[:, b * 4:(b + 1) * 4], ALU.add)
                p.v("dve", "reciprocal", [dn], [dn], dn[:, 4:8], dn[:, 0:4])
                p.v("dve", "tensor_tensor", [pob, dn], [o_tok[a]], ot4[:, g * 4:(g + 1) * 4, par, :], pov[:, :, 0:64],
                    dn[:, 4:8].unsqueeze(2).to_broadcast([128, 4, 64]), ALU.mult)

        NI = len(items)
        for n in range(NI + LA):
            if n < NI:
                stage_a(n)
            if n - LA >= 0:
                stage_c(n - LA)
        nsc += NI
        out_proj_ln(p, C, o_tok[a], ptr[0], oT, Wo, pq[0], pq[1], xs[a], x_out[i * 128:(i + 1) * 128, :])
    p.barrier()
    p.pop()


def hgrn_consts():
    s = np.arange(128)[:, None]
    t = np.arange(128)[None, :]
    same = (s // 64) == (t // 64)
    U = (same & (s <= t)).astype(np.float32)
    Emid = (same & ((s % 64) <= 31)).astype(np.float32)
    urhs = np.zeros((128, 134), np.float32)
    urhs[:, 0:128] = U - Emid
    for c in range(2):
        urhs[:, 128 + c] = ((s[:, 0] // 64 == c) & ((s[:, 0] % 64) <= 31)).astype(np.float32)
        urhs[:, 130 + c] = (s[:, 0] // 64 == c).astype(np.float32)
    urhs[:, 132] = urhs[:, 63]
    urhs[:, 133] = urhs[:, 127]
    mneg = -(U - Emid)
    return np.concatenate([urhs, mneg, U], axis=1).astype(np.float32)


def hgrn_phase(p, C, T, x_in, x_out, layer, w_in_d, w_out_d, gain_d, lb_d, hc_d, g_d, b_d):
    nc = p.nc
    p.push()
    W = [p.sb([128, NCH, 1024], BF16, f"Wh{j}") for j in range(4)]
    Wo = p.sb([128, NCH, 1024], BF16, "Wo")
    for j in range(4):
        load_w_chunks(p, W[j], w_in_d, j * 1024, 1024, 4)
    load_w_chunks(p, Wo, w_out_d, 0, 1024, 4)
    load_ln(p, C, g_d, b_d)
    HC = p.sb([128, 390], F32, "HC")
    p.dma("sp", HC[:], hc_d, [], [HC])
    Urhs, Mneg, Mbd = HC[:, 0:134], HC[:, 134:262], HC[:, 262:390]
    Gn = p.sb([128, 128], F32, "Gn")
    p.dma("sp", Gn[:], gain_d.rearrange("(o n) -> o n", o=1).to_broadcast([128, 128]), [], [Gn])
    LBb = p.sb([128, D], F32, "LBb")
    OMLb = p.sb([128, D], F32, "OMLb")
    lbT = p.sb([128, 16], F32, "lbT")
    p.push()
    L4 = p.sb([128, 4, D], F32, "L4")
    for j in range(4):
        p.dma("sp", L4[:, j, :], lb_d[j].rearrange("(o n) -> o n", o=1).to_broadcast([128, D]), [], [L4])
    p.act(L4[:], L4[:], AF.Exp, [L4], [L4])
    p.v("dve", "tensor_tensor", [L4], [OMLb], OMLb[:], L4[:, 0, :], L4[:, 1, :], ALU.add)
    p.v("dve", "tensor_tensor", [L4, OMLb], [OMLb], OMLb[:], OMLb[:], L4[:, 2, :], ALU.add)
    p.v("dve", "tensor_tensor", [L4, OMLb], [OMLb], OMLb[:], OMLb[:], L4[:, 3, :], ALU.add)
    p.v("dve", "reciprocal", [OMLb], [OMLb], OMLb[:], OMLb[:])
    p.v("dve", "memset", [], [LBb], LBb[:], 0.0)
    for j in range(1, layer + 1):
        p.v("dve", "tensor_tensor", [L4, LBb], [LBb], LBb[:], LBb[:], L4[:, j, :], ALU.add)
    p.v("dve", "tensor_tensor", [LBb, OMLb], [LBb], LBb[:], LBb[:], OMLb[:], ALU.mult)
    p.v("dve", "tensor_scalar", [LBb], [OMLb], OMLb[:], LBb[:], -1.0, 1.0, ALU.mult, ALU.add)
    plb = p.ps([128, 8, 128], F32, "plb")
    for h in range(8):
        p.op("pe", lambda h=h: nc.tensor.transpose(plb[:, h, :], LBb[:, h * 128:(h + 1) * 128], C.identf[:]), [LBb, C.identf], [plb])
    p.v("dve", "tensor_copy", [plb], [lbT], lbT[:, 0:8], plb[:, :, 0])
    p.v("dve", "tensor_scalar", [lbT], [lbT], lbT[:, 8:16], lbT[:, 0:8], -1.0, 1.0, ALU.mult, ALU.add)
    p.barrier()
    p.pop()

    xs = [p.sb([128, D], F32, f"xs{a}") for a in range(2)]
    xb = [p.sb([128, D], BF16, f"xb{a}") for a in range(2)]
    xT = [p.sb([128, NCH, 128], BF16, f"xT{a}") for a in range(2)]
    qT = p.sb([128, 8, 128], F32, "qT")
    smT = p.sb([128, 8, 128], F32, "smT")
    fs = p.sb([128, D], F32, "fs")
    logf = p.sb([128, D], F32, "logf")
    kk = p.sb([128, D], F32, "kk")
    vb = p.sb([128, D], BF16, "vb")
    GG = p.sb([128, D], F32, "GG")
    eD = [p.sb([128, 128], F32, f"eD{a}") for a in range(2)]
    eDn = [p.sb([128, 128], F32, f"eDn{a}") for a in range(2)]
    eDp = [p.sb([128, 128], F32, f"eDp{a}") for a in range(2)]
    ex = [p.sb([128, 8], F32, f"ex{a}") for a in range(2)]
    qz = [p.sb([128, 2, 128], BF16, f"qz{a}") for a in range(2)]
    kz = [p.sb([128, 2, 128], BF16, f"kz{a}") for a in range(2)]
    for a in range(2):
        p.v("dve", "memset", [], [qz[a]], qz[a][:], 0.0)
        p.v("dve", "memset", [], [kz[a]], kz[a][:], 0.0)
    kTt = [p.sb([128, 128], BF16, f"kTt{a}") for a in range(2)]
    aT = [p.sb([128, 128], BF16, f"aT{a}") for a in range(2)]
    Sbf = [p.sb([128, 128], BF16, f"Sbf{a}") for a in range(2)]
    T1 = p.sb([128, 128], F32, "T1")
    S = p.sb([128, 8, 128], F32, "S")
    p.v("dve", "memset", [], [S], S[:], 0.0)
    sq = p.sb([128, 4, 128], F32, "sq")
    t1 = p.sb([128, 4, 128], F32, "t1")
    ss = p.sb([128, 16], F32, "ss")
    o_tok = [p.sb([128, D], BF16, f"otok{a}") for a in range(2)]
    oT = p.sb([128, NCH, 128], BF16, "oT")
    ptr = p.ps([128, NCH, 128], BF16, "ptr")
    PA = [p.ps([128, 512], F32, f"PA{a}") for a in range(2)]
    PB = [p.ps([128, 512], F32, f"PB{a}") for a in range(2)]
    Dk = p.ps([128, 512], F32, "Dk")
    Mi = p.ps([128, 512], F32, "Mi")
    po = p.ps([128, 4, 128], F32, "po")
    NT = T // 128
    hcnt = 0
    for i in range(NT):
        a = i % 2
        load_x_tile(p, C, x_in[i * 128:(i + 1) * 128, :], xs[a], xb[a], ptr, xT[a])

        def proj_fm(P2, Wm):
            for h in range(8):
                pb = P2[h // 4]
                for c in range(NCH):
                    p.mm(pb[:, (h % 4) * 128:(h % 4 + 1) * 128], Wm[:, c, h * 128:(h + 1) * 128], xT[a][:, c, :],
                         c == 0, c == NCH - 1, [Wm, xT[a]], [pb])

        def proj_tm(P2, Wm):
            for hf in range(2):
                for c in range(NCH):
                    p.mm(P2[hf][:], xT[a][:, c, :], Wm[:, c, hf * 512:(hf + 1) * 512], c == 0, c == NCH - 1,
                         [Wm, xT[a]], [P2[hf]])

        proj_fm(PA, W[0])
        proj_fm(PB, W[1])
        for hf in range(2):
            p.act(qT[:, hf * 4:(hf + 1) * 4, :], PA[hf][:], AF.Silu, [PA[hf]], [qT])
            p.act(smT[:, hf * 4:(hf + 1) * 4, :], PB[hf][:], AF.Sigmoid, [PB[hf]], [smT], scale=-1.0)
        proj_tm(PA, W[1])
        proj_tm(PB, W[2])
        for hf in range(2):
            p.act(fs[:, hf * 512:(hf + 1) * 512], PA[hf][:], AF.Sigmoid, [PA[hf]], [fs])
            p.v("dve", "tensor_copy", [PB[hf]], [vb], vb[:, hf * 512:(hf + 1) * 512], PB[hf][:])
        proj_tm(PA, W[3])
        p.v("dve", "tensor_tensor", [fs, OMLb], [fs], fs[:], fs[:], OMLb[:], ALU.mult)
        p.v("dve", "tensor_tensor", [fs, LBb], [fs], fs[:], fs[:], LBb[:], ALU.add)
        p.act(logf[:], fs[:], AF.Ln, [fs], [logf])
        p.v("pool", "tensor_scalar", [fs], [kk], kk[:], fs[:], -1.0, 1.0, ALU.mult, ALU.add)
        for hf in range(2):
            p.act(GG[:, hf * 512:(hf + 1) * 512], PA[hf][:], AF.Silu, [PA[hf]], [GG])
        p.v("pool", "tensor_tensor", [GG, Gn], [GG], GG[:].rearrange("p (h v) -> p h v", h=8), GG[:].rearrange("p (h v) -> p h v", h=8),
            Gn[:].unsqueeze(1).to_broadcast([128, 8, 128]), ALU.mult)
        XB = [Dk, PA[0]]
        YB = [Mi, PA[1]]

        def hs1(h):
            k = h % 2
            hs = slice(h * 128, (h + 1) * 128)
            X = XB[k]
            p.mm(X[:, 0:134], logf[:, hs], Urhs, True, True, [logf, HC], [X])
            p.mm(X[:, 256:384], Mneg, logf[:, hs], True, True, [logf, HC], [X])
            p.act(eD[k][:], X[:, 0:128], AF.Exp, [X], [eD[k]])
            p.act(eDn[k][:], X[:, 0:128], AF.Exp, [X], [eDn[k]], scale=-1.0)
            p.act(ex[k][:, 0:6], X[:, 128:134], AF.Exp, [X], [ex[k]])
            p.act(eDp[k][:], X[:, 256:384], AF.Exp, [X], [eDp[k]])
            for c in range(2):
                cs = slice(c * 64, (c + 1) * 64)
                p.v("dve", "tensor_tensor", [qT, eD[k]], [qz[k]], qz[k][:, c, cs], qT[:, h, cs], eD[k][:, cs], ALU.mult)
                p.v("dve", "tensor_tensor", [kk, eDp[k]], [kz[k]], kz[k][cs, c, :], kk[cs, hs], eDp[k][cs, :], ALU.mult)
            p.v("dve", "scalar_tensor_tensor", [smT, lbT, eDn[k]], [kTt[k]], out=kTt[k][:], in0=smT[:, h, :],
                scalar=lbT[:, 8 + h:9 + h], in1=eDn[k][:], op0=ALU.mult, op1=ALU.mult)

        def hs2(h):
            k = h % 2
            Y = YB[k]
            for c in range(2):
                cs = slice(c * 64, (c + 1) * 64)
                p.mm(Y[:, c * 64:(c + 1) * 64], kTt[k][:], qz[k][:, c, cs], True, True, [kTt[k], qz[k]], [Y])
            p.v("dve", "tensor_tensor", [Y, HC], [aT[k]], aT[k][:], Y[:, 0:128], Mbd, ALU.mult)

        def hs3(h):
            k = h % 2
            hh = h % 4
            hs = slice(h * 128, (h + 1) * 128)
            Y = YB[k]
            for c in range(2):
                wsl = slice(128 + c * 128, 256 + c * 128)
                p.mm(Y[:, wsl], kz[k][:, c, :], vb[:, hs], True, True, [kz[k], vb], [Y])
                sb_ = Sbf[c]
                p.v("act", "mul", [S, ex[k]], [sb_], sb_[:], S[:, h, :], ex[k][:, c:c + 1])
                p.mm(po[:, hh, :], qz[k][:, c, :], sb_[:], c == 0, False, [qz[k], sb_], [po], skip=True)
                p.v("dve", "tensor_scalar", [S, ex[k]], [T1], T1[:], S[:, h, :], ex[k][:, 2 + c:3 + c], None, ALU.mult)
                p.v("dve", "scalar_tensor_tensor", [Y, ex[k], T1], [S], out=S[:, h, :], in0=Y[:, wsl],
                    scalar=ex[k][:, 4 + c:5 + c], in1=T1[:], op0=ALU.mult, op1=ALU.add)
            p.mm(po[:, hh, :], aT[k][:], vb[:, hs], False, True, [aT[k], vb], [po], skip=True)
            if hh == 3:
                g4 = h // 4
                p.act(sq[:], po[:], AF.Square, [po], [sq])
                p.v("dve", "reduce_sum", [sq], [ss], ss[:, 0:4], sq[:], AX.X)
                p.v("dve", "tensor_scalar", [ss], [ss], ss[:, 4:8], ss[:, 0:4], 1.0 / 128, 1e-6, ALU.mult, ALU.add)
                p.act(ss[:, 4:8], ss[:, 4:8], AF.Sqrt, [ss], [ss])
                p.v("dve", "reciprocal", [ss], [ss], ss[:, 8:12], ss[:, 4:8])
                p.v("dve", "tensor_tensor", [po, ss], [t1], t1[:], po[:], ss[:, 8:12].unsqueeze(2).to_broadcast([128, 4, 128]), ALU.mult)
                p.v("dve", "tensor_tensor", [t1, GG], [o_tok[a]], o_tok[a][:, g4 * 512:(g4 + 1) * 512],
                    t1[:].rearrange("p h v -> p (h v)"), GG[:, g4 * 512:(g4 + 1) * 512], ALU.mult)

        hs1(0)
        for h in range(8):
            if h + 1 < 8:
                hs1(h + 1)
            hs2(h)
            hs3(h)
        out_proj_ln(p, C, o_tok[a], ptr, oT, Wo, PB[0], PB[1], xs[a], x_out[i * 128:(i + 1) * 128, :])
    p.barrier()
    p.pop()


def nsa_consts(rel_bias):
    s = np.arange(128)[:, None]
    t = np.arange(128)[None, :]
    bw = np.empty((2, 128, 16, 128), np.float32)
    for jj in range(2):
        d = t - s + 128 * jj
        tab = rel_bias[_t5_bucket_np(d)]
        if jj == 0:
            tab = np.where((d >= 0)[:, :, None], tab, np.float32(NEG))
        bw[jj] = np.transpose(tab, (0, 2, 1))
    tq = np.arange(128)[:, None]
    mq = 14 - np.arange(15)[None, :]
    dc = tq + 16 * mq - 127
    gc = rel_bias[_t5_bucket_np(dc)]
    gc = np.where((dc >= 0)[:, :, None], gc, np.float32(NEG))
    gc = np.ascontiguousarray(np.transpose(gc, (0, 2, 1)))
    ch = np.ascontiguousarray(rel_bias[31])
    mask4 = np.where(s > t, np.float32(0.0), np.float32(NEG)).astype(np.float32)
    return bw, gc, ch, mask4


def nsa_eexp(T):
    return (np.arange(T)[None, :] // 64 == np.arange(64)[:, None]).astype(np.float32)


def nsa_phase(p, C, T, x_in, x_out, w_in_d, w_out_d, pos_d, w1_d, w2_d, bw_d, gc_d, ch_d, m4_d, ee_d, g_d, b_d):
    nc = p.nc
    SKIP = set(os.environ.get("NSA_SKIP", "").split(","))
    p.push()
    NT = T // 128
    srcw = w_in_d.rearrange("(c p) n -> p c n", p=128)
    Wqp = p.sb([128, NCH, 8, 2, 64], BF16, "Wqp")
    for j in range(8):
        A = (j // 4) * 8 + j % 4
        for hf, hd in enumerate((A, A + 4)):
            p.dma("pool", Wqp[:, :, j, hf, :], srcw[:, :, hd * 64:(hd + 1) * 64], [], [Wqp])
    Wr = p.sb([128, NCH, 1584], BF16, "Wr")
    for sgm in range(4):
        p.dma("pool", Wr[:, sgm * 2:(sgm + 1) * 2, :], srcw[:, sgm * 2:(sgm + 1) * 2, 1024:2608], [], [Wr])
    OKC, OVC, OKS, OVS, OKW, OVW, OGT = 0, 256, 512, 768, 1024, 1280, 1536
    Wo = p.sb([128, NCH, 1024], BF16, "Wo")
    load_w_chunks(p, Wo, w_out_d, 0, 1024, 4)
    w1 = [p.sb([128, 32, 128], BF16, f"w1_{kv}") for kv in range(2)]
    w2k = p.sb([128, 2, 64], BF16, "w2k")
    w2v = p.sb([128, 64], BF16, "w2v")
    posT = p.sb([128, 2, 32], BF16, "posT")
    for kv in range(2):
        for hf in range(2):
            p.dma("pool", w1[kv][hf * 64:(hf + 1) * 64, :, :], w1_d[kv].rearrange("l d e -> d l e"), [], [w1[kv]])
            p.dma("pool", posT[hf * 64:(hf + 1) * 64, kv, :], pos_d[kv].rearrange("l d -> d l"), [], [posT],
                  allow_slow_non_contiguous=True)
    for dup in range(2):
        p.dma("pool", w2k[:, dup, :], w2_d[0], [], [w2k])
    p.dma("pool", w2v[:], w2_d[1], [], [w2v])
    load_ln(p, C, g_d, b_d, nysb=1)
    BW = p.sb([128, 2, 16, 128], F32, "BW")
    for jj in range(2):
        p.dma("sp", BW[:, jj, :, :], bw_d[jj], [], [BW])
    G8 = p.sb([128, 16, 15], F32, "G8")
    p.dma("sp", G8[:], gc_d, [], [G8])
    CHb = p.sb([128, 16], F32, "CHb")
    p.dma("sp", CHb[:], ch_d.rearrange("(o n) -> o n", o=1).to_broadcast([128, 16]), [], [CHb])
    M4 = p.sb([128, 128], F32, "M4")
    p.dma("sp", M4[:], m4_d, [], [M4])
    for jj in range(2):
        p.v("dve", "tensor_tensor", [BW, CHb], [BW], BW[:, jj, :, :], BW[:, jj, :, :],
            CHb[:].unsqueeze(2).to_broadcast([128, 16, 128]), ALU.subtract)
    p.v("dve", "tensor_tensor", [G8, CHb], [G8], G8[:], G8[:], CHb[:].unsqueeze(2).to_broadcast([128, 16, 15]), ALU.subtract)
    p.v("dve", "tensor_scalar", [G8], [G8], G8[:], G8[:], 8.0, None, ALU.mult)

    KSE = [p.sb([128, T], BF16, f"KSE{g}") for g in range(4)]
    for g in range(4):
        oh = 1 - g % 2
        p.dma("pool", KSE[g][oh * 64:(oh + 1) * 64, :], ee_d, [], [KSE[g]])
    VS = p.sb([128, NT, 4, 65], BF16, "VS")
    KW = p.sb([128, 5, 2, 128], BF16, "KW")
    VW = p.sb([128, 5, 4, 65], BF16, "VW")
    KC = p.sb([128, 2, 256], BF16, "KC")
    VC = p.sb([128, 2, 4, 64], BF16, "VC")
    p.v("pool", "memset", [], [VS], VS[:, :, :, 64:65], 1.0)
    p.v("pool", "memset", [], [VW], VW[:, :, :, 64:65], 1.0)
    p.v("pool", "memset", [], [VC], VC[:], 0.0)
    p.v("pool", "memset", [], [KC], KC[:], 0.0)
    rawr = [p.sb([64, 16, 9, 4], BF16, f"rawr{kv}") for kv in range(2)]
    for kv in range(2):
        p.v("pool", "memset", [], [rawr[kv]], rawr[kv][:], 0.0)
    cb = p.sb([128, 2], F32, "cb")
    hid = [p.sb([128, 32], BF16, f"hid{kv}") for kv in range(2)]
    for kv in range(2):
        p.v("pool", "memset", [], [hid[kv]], hid[kv][:], 0.0)
    vtmp = p.sb([8, 4, 64], BF16, "vtmp")
    xs2 = [p.sb([128, D], F32, f"xs{a_}") for a_ in range(2)]
    xb = p.sb([128, D], BF16, "xb")
    xT = p.sb([128, NCH, 128], BF16, "xT")
    QM = [p.sb([128, 4, 128], BF16, f"QM{g}") for g in range(4)]
    for g in range(4):
        p.v("pool", "memset", [], [QM[g]], QM[g][:], 0.0)
    gt = p.sb([128, 48], F32, "gt")
    gt3 = gt[:].rearrange("p (h b) -> p h b", b=3)
    sc = [p.sb([128, 4, 128], F32, f"sc{a}") for a in range(2)]
    e = [p.sb([128, 4, 128], BF16, f"e{a}") for a in range(4)]
    MTs = p.sb([128, 2, 128], BF16, "MTs")
    ef = p.sb([128, 4, 256], F32, "ef")
    eb = p.sb([128, 4, 256], BF16, "eb")
    p.v("pool", "memset", [], [eb], eb[:], 0.0)
    ebT = p.sb([128, 8, 128], BF16, "ebT")
    rs = p.sb([128, 16], F32, "rs")
    p.v("pool", "memset", [], [rs], rs[:], 0.0)
    fac = [p.sb([128, 8], F32, f"fac{a}") for a in range(2)]
    Pg = p.sb([128, 256], F32, "Pg")
    impb = p.sb([128, 4, 64], F32, "impb")
    impw = p.sb([128, 64], F32, "impw")
    mx = p.sb([128, 16], F32, "mx")
    selm = p.sb([128, 4, 64], BF16, "selm")
    oacc = p.sb([128, 16, 64], F32, "oacc")
    otmp = [p.sb([128, 4, 64], F32, f"otmp{a}") for a in range(2)]
    o_tok = p.sb([128, D], BF16, "otok")
    oT = p.sb([128, NCH, 128], BF16, "oT")
    ptr = p.ps([128, NCH, 128], BF16, "ptr")
    PA = [p.ps([128, 512], F32, f"PA{a}") for a in range(2)]
    PB = p.ps([128, 512], F32, "PB")
    psb = [p.ps([128, 512], F32, f"ps{a}") for a in range(2)]
    pacc = [p.ps([128, 512], F32, f"pacc{a}") for a in range(2)]
    cnt = {"ps": 0, "acc": 0, "sc": 0}
    print("nsa sbuf remaining:", nc.sbuf_bytes_remaining)

    p.pe_serial = True
    for kv in range(2):
        for l in range(32):
            p.mm(PB[:, kv:kv + 1], w1[kv][0:64, l, :], posT[0:64, kv, l:l + 1], l == 0, l == 31, [w1[kv], posT], [PB])
    p.pe_serial = False
    p.v("dve", "tensor_copy", [PB], [cb], cb[:], PB[:, 0:2])

    def next_ps():
        k = cnt["ps"] % 2
        cnt["ps"] += 1
        return k

    def next_acc():
        k = cnt["acc"] % 2
        cnt["acc"] += 1
        return pacc[k], k

    def finish_branch(pob, k, g, br, first):
        pov = pob[:, 0:260].rearrange("p (m d) -> p m d", m=4, d=65)
        fc = fac[k]
        p.v("dve", "reciprocal", [pob], [fc], fc[:, 0:4], pov[:, :, 64])
        p.v("dve", "tensor_tensor", [fc, gt], [fc], fc[:, 4:8], fc[:, 0:4], gt3[:, 4 * g:4 * g + 4, br], ALU.mult)
        dst = oacc[:, 4 * g:4 * g + 4, :] if first else otmp[k][:]
        p.v("dve", "tensor_tensor", [pob, fc], [oacc if first else otmp[k]], dst, pov[:, :, 0:64],
            fc[:, 4:8].unsqueeze(2).to_broadcast([128, 4, 64]), ALU.mult)
        if not first:
            p.v("pool", "tensor_tensor", [oacc, otmp[k]], [oacc], oacc[:, 4 * g:4 * g + 4, :], oacc[:, 4 * g:4 * g + 4, :],
                otmp[k][:], ALU.add)

    pending = None
    for i in range(NT):
        tsl = slice(i * 128, (i + 1) * 128)
        xs = xs2[i % 2]
        load_x_tile(p, C, x_in[tsl, :], xs, xb, ptr, xT)
        for j in range(8):
            pb = PA[j // 4]
            for c in range(NCH):
                p.mm(pb[:, (j % 4) * 128:(j % 4 + 1) * 128], Wqp[:, c, j, :, :], xT[:, c, :], c == 0, c == NCH - 1,
                     [Wqp, xT], [pb])
        for hf in range(2):
            for par in range(2):
                rs_ = slice(par * 64, par * 64 + 64)
                p.act(QM[2 * hf + par][rs_, :, :], PA[hf][rs_, :].rearrange("p (m n) -> p m n", m=4), AF.Copy, [PA[hf]],
                      [QM[2 * hf + par]])
        def fm(pt, off):
            for pr in range(2):
                for c in range(NCH):
                    p.mm(pt[:, pr * 128:(pr + 1) * 128], Wr[:, c, off + pr * 128:off + (pr + 1) * 128], xT[:, c, :],
                         c == 0, c == NCH - 1, [Wr, xT], [pt])
        for kv, off in ((0, OKC), (1, OVC)):
            pt = PB if kv == 0 else psb[0]
            p.v("dve", "tensor_copy", [rawr[kv]], [rawr[kv]], rawr[kv][:, :, 0, :], rawr[kv][:, :, 8, :])
            for g in range(4):
                for c in range(NCH):
                    p.mm(pt[0:64, g * 128:(g + 1) * 128], Wr[:, c, off + g * 64:off + (g + 1) * 64], xT[:, c, :],
                         c == 0, c == NCH - 1, [Wr, xT], [pt])
            p.v("dve", "tensor_copy", [pt], [rawr[kv]], rawr[kv][:, :, 1:9, :],
                pt[0:64, :].rearrange("p (g k b) -> p b k g", g=4, k=8, b=16))
        fm(psb[1], OKS)
        for g in range(4):
            rs_ = slice((g % 2) * 64, (g % 2) * 64 + 64)
            p.act(KSE[g][rs_, tsl], psb[1][rs_, (g // 2) * 128:(g // 2 + 1) * 128], AF.Copy, [psb[1]], [KSE[g]])
        fm(PB, OKW)
        p.act(KW[:, i % 5, :, :], PB[:, 0:256], AF.Copy, [PB], [KW])
        for c in range(NCH):
            p.mm(PA[0][:, 0:256], xT[:, c, :], Wr[:, c, OVS:OVS + 256], c == 0, c == NCH - 1, [Wr, xT], [PA[0]])
        p.v("dve", "tensor_copy", [PA[0]], [VS], VS[:, i, :, 0:64], PA[0][:, 0:256])
        for c in range(NCH):
            p.mm(PA[1][:, 0:256], xT[:, c, :], Wr[:, c, OVW:OVW + 256], c == 0, c == NCH - 1, [Wr, xT], [PA[1]])
        p.v("dve", "tensor_copy", [PA[1]], [VW], VW[:, i % 5, :, 0:64], PA[1][:, 0:256])
        for c in range(NCH):
            p.mm(PA[0][:, 256:304], xT[:, c, :], Wr[:, c, OGT:OGT + 48], c == 0, c == NCH - 1, [Wr, xT], [PA[0]])
        p.act(gt[:], PA[0][:, 256:304], AF.Sigmoid, [PA[0]], [gt])
        j0 = 1 if i == 0 else 0
        n_lo = 8 * i - 1 + j0
        n_hi = 8 * i + 6
        cn = n_hi - n_lo + 1
        if "cmp" not in SKIP:
            for l in range(32):
                a_, b_ = divmod(l, 16)
                for kv in range(2):
                    p.mm(PB[:, kv * 32:(kv + 1) * 32], w1[kv][0:64, l, :],
                         rawr[kv][:, b_, a_:a_ + 8, :].rearrange("p k g -> p (k g)"), l == 0 and kv == 0, l == 31,
                         [w1[kv], rawr[kv]], [PB], skip=True)
            for kv in range(2):
                p.act(hid[kv][:], PB[:, kv * 32:(kv + 1) * 32], AF.Gelu_apprx_tanh, [PB, cb], [hid[kv]], bias=cb[:, kv:kv + 1])
        p.mm(PB[:, 64:96], w2k[:], hid[0][:], True, True, [w2k, hid[0]], [PB])
        pk2 = PB[:, 64:96].rearrange("p (j g) -> p g j", g=4)
        for hf in range(2):
            hs = slice(hf * 64, hf * 64 + 64)
            p.v("dve", "tensor_copy", [PB], [KC], KC[hs, :, n_lo:n_hi + 1], pk2[hs, hf::2, j0:8])
        pv2 = PB[:, 128:384].rearrange("p (g d) -> p g d", g=4)
        p.pe_serial = True
        for g in range(4):
            p.mm(pv2[0:cn, g, :], hid[1][:, j0 * 4 + g:32:4], w2v[:], True, True, [w2v, hid[1]], [PB])
        p.pe_serial = False
        p.v("dve", "tensor_copy", [PB], [vtmp], vtmp[0:cn, :, :], pv2[0:cn, :, :])
        na = max(0, min(n_hi, 127) - n_lo + 1) if n_lo < 128 else 0
        if na > 0:
            p.dma("sp", VC[n_lo:n_lo + na, 0, :, :], vtmp[0:na, :, :], [vtmp], [VC])
        if cn - na > 0:
            st0 = n_lo + na - 128
            p.dma("sp", VC[st0:st0 + cn - na, 1, :, :], vtmp[na:cn, :, :], [vtmp], [VC])
        if pending is not None:
            pending()
            pending = None
        nv = 8 * i + 7
        nb0 = max(0, nv - 15)
        q0 = 15 - (nv - nb0)
        ntl = 1 if nv <= 128 else 2
        sel_on = i >= 8
        if sel_on:
            p.v("pool", "memset", [], [impb], impb[:], -1e9)
        for g in range(4 if "cattn" not in SKIP else 0):
            hs = slice((g % 2) * 64, (g % 2) * 64 + 64)
            pr = g // 2
            p.v("pool", "memset", [], [rs], rs[:, 0:4], 0.0)
            for hp in range(2):
                k = next_ps()
                Sc = psb[k][:].rearrange("p (m n) -> p m n", m=2)
                for m2 in range(2):
                    m = hp * 2 + m2
                    h = 4 * g + m
                    p.mm(Sc[:, m2, 0:nv], QM[g][hs, m, :], KC[hs, pr, 0:nv], True, True, [QM[g], KC], [psb[k]])
                    p.v("dve", "tensor_tensor", [psb[k], G8], [psb[k]], Sc[:, m2, nb0:nv], Sc[:, m2, nb0:nv], G8[:, h, q0:15], ALU.add)
                    p.act(ef[:, m, 0:nv], Sc[:, m2, 0:nv], AF.Exp, [psb[k]], [ef, rs], scale=0.125, accum_out=rs[:, m:m + 1])
            p.v("dve", "tensor_scalar", [rs], [rs], rs[:, 8:12], rs[:, 0:4], 1e-30, None, ALU.add)
            p.v("dve", "reciprocal", [rs], [rs], rs[:, 4:8], rs[:, 8:12])
            p.v("pool", "tensor_copy", [ef], [eb], eb[:, :, 0:nv], ef[:, :, 0:nv])
            if sel_on:
                p.v("dve", "tensor_scalar", [ef, rs], [Pg], Pg[:, 0:nv], ef[:, 0, 0:nv], rs[:, 4:5], None, ALU.mult)
                for m in range(1, 4):
                    p.v("dve", "scalar_tensor_tensor", [ef, rs, Pg], [Pg], out=Pg[:, 0:nv], in0=ef[:, m, 0:nv],
                        scalar=rs[:, 4 + m:5 + m], in1=Pg[:, 0:nv], op0=ALU.mult, op1=ALU.add)
                nj = 2 * i
                P4 = Pg[:, 0:4 * nj].rearrange("p (j r) -> p j r", r=4)
                p.v("dve", "reduce_sum", [Pg], [impb], impb[:, g, 0:nj], P4, AX.X)
                p.v("dve", "tensor_tensor", [Pg, impb], [impb], impb[:, g, 1:nj], impb[:, g, 1:nj], P4[:, 0:nj - 1, 3], ALU.add)
            for nt in range(ntl):
                for m in range(4):
                    p.tr(ptr[:, nt * 4 + m, :], eb[:, m, nt * 128:(nt + 1) * 128], C.ident[:], [eb, C.ident], [ptr])
            p.v("dve", "tensor_copy", [ptr], [ebT], ebT[:, 0:4 * ntl, :], ptr[:, 0:4 * ntl, :])
            pob, ka = next_acc()
            pov = pob[:, 0:260].rearrange("p (m d) -> p m d", m=4, d=65)
            for m in range(4):
                for nt in range(ntl):
                    p.mm(pov[:, m, 0:64], ebT[:, nt * 4 + m, :], VC[:, nt, g, :], m == 0 and nt == 0, nt == ntl - 1,
                         [ebT, VC], [pob], skip=True)
            fc = fac[ka]
            p.v("dve", "tensor_tensor", [rs, gt], [fc], fc[:, 4:8], rs[:, 4:8], gt3[:, 4 * g:4 * g + 4, 0], ALU.mult)
            p.v("dve", "tensor_tensor", [pob, fc], [oacc], oacc[:, 4 * g:4 * g + 4, :], pov[:, :, 0:64],
                fc[:, 4:8].unsqueeze(2).to_broadcast([128, 4, 64]), ALU.mult)
        if sel_on:
            p.v("pool", "memset", [impb], [impb], impb[:, :, 0:1], 1e9)
            p.v("pool", "memset", [impb], [impb], impb[0:64, :, 2 * i - 1:2 * i], 2e9)
            p.v("pool", "memset", [impb], [impb], impb[0:64, :, 2 * i:2 * i + 1], 3e9)
            p.v("pool", "memset", [impb], [impb], impb[0:64, :, 2 * i + 1:2 * i + 2], -1e9)
            p.v("pool", "memset", [impb], [impb], impb[64:128, :, 2 * i:2 * i + 1], 2e9)
            p.v("pool", "memset", [impb], [impb], impb[64:128, :, 2 * i + 1:2 * i + 2], 3e9)
            for g in range(4):
                p.v("dve", "max", [impb], [mx], mx[:, 0:8], impb[:, g, :])
                p.v("dve", "match_replace", [mx, impb], [impw], out=impw[:], in_to_replace=mx[:, 0:8], in_values=impb[:, g, :],
                    imm_value=-3e9)
                p.v("dve", "max", [impw], [mx], mx[:, 8:16], impw[:])
                p.v("dve", "tensor_scalar", [impb, mx], [selm], selm[:, g ^ 1, :], impb[:, g, :], mx[:, 15:16], None, ALU.is_ge)
            for j in range(2):
                p.tr(ptr[:, j, :], selm[:, 2 * j:2 * j + 2, :].rearrange("p g b -> p (g b)"), C.ident[:], [selm, C.ident], [ptr])
            p.v("dve", "tensor_copy", [ptr], [MTs], MTs[:], ptr[:, 0:2, :])
            for g in range(4):
                oh = 1 - g % 2
                rs_ = slice(oh * 64, oh * 64 + 64)
                p.v("pool", "tensor_scalar", [MTs], [QM[g]], QM[g][rs_, :, :],
                    MTs[rs_, g // 2, :].unsqueeze(1).to_broadcast([64, 4, 128]), 1.0, 30000.0, ALU.subtract, ALU.mult)
        items = []
        for g in range(4):
            for br in (1, 2):
                if ("sel" in SKIP and br == 1) or ("win" in SKIP and br == 2):
                    continue
                kts = list(range(0, i + 1)) if br == 1 else list(range(max(0, i - 4), i + 1))
                for idx, kt in enumerate(kts):
                    items.append((g, br, kt, idx == 0, idx == len(kts) - 1))
        SBK = [psb[0], psb[1], PA[0], PA[1]]
        NB = 4
        LA = int(os.environ.get("NSA_LA", "2"))

        def kv_of(g, br, kt):
            hs = slice((g % 2) * 64, (g % 2) * 64 + 64)
            pr = g // 2
            if br == 1:
                return KSE[g][:, kt * 128:(kt + 1) * 128], VS[:, kt, g, :], KSE[g], VS
            return KW[hs, kt % 5, pr, :], VW[:, kt % 5, g, :], KW, VW

        def stage_a(n):
            g, br, kt, first, last = items[n]
            k = n % NB
            hs = slice((g % 2) * 64, (g % 2) * 64 + 64)
            pr = g // 2
            kk_, vv_, kbuf, vbuf = kv_of(g, br, kt)
            S4 = SBK[k][:].rearrange("p (m n) -> p m n", m=4)
            if br == 1:
                p.mm(S4, kk_, QM[g][:, :, :], True, True, [kbuf, QM[g]], [SBK[k]])
            else:
                p.mm(S4, kk_, QM[g][hs, :, :], True, True, [kbuf, QM[g]], [SBK[k]])
            jj = i - kt
            if jj <= 1 or (br == 2 and jj == 4):
                q = cnt["sc"] % 2
                cnt["sc"] += 1
                in1 = BW[:, jj, 4 * g:4 * g + 4, :] if jj <= 1 else M4[:].unsqueeze(1).to_broadcast([128, 4, 128])
                p.v("dve", "scalar_tensor_tensor", [SBK[k], BW, M4], [sc[q]], out=sc[q][:], in0=S4, scalar=0.125,
                    in1=in1, op0=ALU.mult, op1=ALU.add)
                p.act(e[k][:], sc[q][:], AF.Exp, [sc[q]], [e[k]])
            elif "exp" not in SKIP:
                p.act(e[k][:], S4, AF.Exp, [SBK[k]], [e[k]], scale=0.125)

        def stage_c(n):
            g, br, kt, first, last = items[n]
            k = n % NB
            kk_, vv_, kbuf, vbuf = kv_of(g, br, kt)
            qacc = (g * 2 + br) % 2
            pob = pacc[qacc]
            pov = pob[:, 0:260].rearrange("p (m d) -> p m d", m=4, d=65)
            src = e[k]
            for m in range(4 if "pv" not in SKIP else 1):
                p.mm(pov[:, m, :], src[:, m, :], vv_, first and m == 0, last, [src, vbuf], [pob], skip=True)
            if last:
                finish_branch(pob, qacc, g, br, False)

        NI = len(items)
        for n in range(NI + LA):
            if n < NI:
                stage_a(n)
            if n - LA >= 0:
                stage_c(n - LA)
        def tail(xs=xs, tsl=tsl):
            p.act(o_tok[:], oacc[:].rearrange("p h d -> p (h d)"), AF.Copy, [oacc], [o_tok])
            out_proj_ln(p, C, o_tok, ptr, oT, Wo, PA[0], PA[1], xs, x_out[tsl, :])

        pending = tail
    pending()
    p.barrier()
    p.pop()


def build(T, plan, extra_inputs):
    nc = bass.Bass("TRN2", target_bir_lowering=False)
    dr = {}

    def din(name, shape):
        dr[name] = nc.dram_tensor(name, list(shape), F32, kind="ExternalInput").ap()
        return dr[name]

    x = din("x", [T, D])
    for name, shape in extra_inputs:
        din(name, shape)
    y = nc.dram_tensor("y", [T, D], F32, kind="ExternalOutput").ap()
    bufs = [nc.dram_tensor(f"act{i}", [T, D], F32, kind="Internal").ap() for i in range(2)]
    with ExitStack() as st:
        p = Prog(nc, st)
        C = setup_common(p)
        cur = x
        for i, ph in enumerate(plan):
            dst = y if i == len(plan) - 1 else bufs[i % 2]
            if ph["kind"] == "ffn":
                L, w = ph["layer"], ph["which"]
                ffn_phase(p, C, T, cur, dst, dr[f"ffn{w}_w_gate"][L], dr[f"ffn{w}_w_up"][L], dr[f"ffn{w}_w_down"][L],
                          dr["ln_gain"][L, 0 if w == 1 else 2], dr["ln_bias"][L, 0 if w == 1 else 2])
            elif ph["kind"] == "swa":
                L = ph["layer"]
                swa_phase(p, C, T, cur, dst, dr["swa_w_in"][0], dr["swa_w_out"][0], dr["swa_bias"], dr["swa_sinks_p"],
                          dr["ln_gain"][L, 1], dr["ln_bias"][L, 1])
            elif ph["kind"] == "hgrn":
                L = ph["layer"]
                hgrn_phase(p, C, T, cur, dst, L, dr["hgrn_w_in"][0], dr["hgrn_w_out"][0], dr["hgrn_norm_gain"][0], dr["hgrn_lb"],
                           dr["hgrn_c"], dr["ln_gain"][L, 1], dr["ln_bias"][L, 1])
            elif ph["kind"] == "nsa":
                L, sl = ph["layer"], ph["slot"]
                nsa_phase(p, C, T, cur, dst, dr["nsa_w_in"][sl], dr["nsa_w_out"][sl], dr["nsa_cmp_pos"][sl], dr["nsa_cmp_w1"][sl],
                          dr["nsa_cmp_w2"][sl], dr["nsa_bw"], dr["nsa_gc"], dr["nsa_ch"], dr["nsa_m4"], dr["nsa_ee"],
                          dr["ln_gain"][L, 1], dr["ln_bias"][L, 1])
            elif ph["kind"] == "dummy_dve":
                for _ in range(ph["n"]):
                    p.v("dve", "memset", [], [C.mv[0]], C.mv[0][:, 0:1], 0.0)
                p.barrier()
            elif ph["kind"] == "dummy":
                for _ in range(ph["n"]):
                    p.dma("sp", dst, cur, [], [])
                p.barrier()
            else:
                raise ValueError(ph)
            cur = dst
        p.barrier()
        print("instructions:", p.nins, {k: p.cnt[k] for k in p.cnt}, "waits:", p.nwait)
    return nc


SEQ = 4096
BATCH = 8
_NC_CACHE = {}


def full_plan():
    plan = []
    for L in range(DEPTH):
        plan.append({"kind": "ffn", "layer": L, "which": 1})
        kind = L % 3
        if kind == 0:
            plan.append({"kind": "nsa", "layer": L, "slot": L // 3})
        elif kind == 1:
            plan.append({"kind": "hgrn", "layer": L})
        else:
            plan.append({"kind": "swa", "layer": L})
        plan.append({"kind": "ffn", "layer": L, "which": 2})
    return plan


def kernel(**inputs):
    inp = {k: np.ascontiguousarray(np.asarray(v, dtype=np.float32)) for k, v in inputs.items()}
    x = inp.pop("x")
    B, T, _ = x.shape
    swa_bias, swa_sp = swa_consts(inp["rel_bias"], inp["swa_sinks"][0])
    bw, gc, ch, m4 = nsa_consts(inp["rel_bias"])
    shared = dict(inp)
    shared.pop("rel_bias")
    shared.pop("swa_sinks")
    shared.update({"swa_bias": swa_bias, "swa_sinks_p": swa_sp, "nsa_bw": bw, "nsa_gc": gc, "nsa_ch": ch, "nsa_m4": m4, "nsa_ee": nsa_eexp(T),
                   "hgrn_c": hgrn_consts()})
    extra = [(k, v.shape) for k, v in shared.items()]
    key = (T,)
    if key not in _NC_CACHE:
        _NC_CACHE[key] = build(T, full_plan(), extra)
    nc = _NC_CACHE[key]
    in_maps = [dict(shared, x=np.ascontiguousarray(x[b])) for b in range(B)]
    res = run_bass_kernel_spmd(nc, in_maps, core_ids=list(range(B)))
    return np.stack([np.asarray(r["y"], dtype=np.float32) for r in res.results], axis=0)
```

```python
import math
import numpy as np
import ml_dtypes
from contextlib import ExitStack
import concourse.bass as bass
import concourse.mybir as mybir
from concourse.bass_utils import run_bass_kernel_spmd

F32 = mybir.dt.float32
BF16 = mybir.dt.bfloat16
AF = mybir.ActivationFunctionType
ALU = mybir.AluOpType
AX = mybir.AxisListType

D = 1024
DFF = 2816
DEPTH = 4
NCH = D // 128
NFC = DFF // 128
ALPHA = (2.0 * DEPTH) ** 0.25
LN_EPS = 1e-5
NDS = 24
NEG = -30000.0
EMBED_WAITS = True
import os
EMBED_ENG = set(os.environ.get('EMBED_ENG', 'pe,dve,act,pool').split(','))


class Buf:
    __slots__ = ("name", "w", "r")

    def __init__(self, name):
        self.name = name
        self.w = None
        self.r = {}


class Tile:
    def __init__(self, t, name):
        self.t = t
        self.b = Buf(name)
        self.subs = {}

    def __getitem__(self, idx):
        return self.t[idx]

    def sub(self, key):
        if key not in self.subs:
            self.subs[key] = Buf(f"{self.b.name}.{key}")
        return self.subs[key]


def _bufs(xs):
    out = []
    for x in xs:
        if x is None:
            continue
        if isinstance(x, Buf):
            out.append(x)
        else:
            out.append(x.b)
            out.extend(x.subs.values())
    return out


class Prog:
    def __init__(self, nc, stack):
        self.nc = nc
        self.stacks = [stack]
        self.E = {"pe": nc.tensor, "dve": nc.vector, "act": nc.scalar, "pool": nc.gpsimd, "sp": nc.sync}
        self.sem = {k: stack.enter_context(nc.semaphore("s_" + k)) for k in self.E}
        self.cnt = {k: 0 for k in self.E}
        self.known = {k: {} for k in self.E}
        self.dsems = [stack.enter_context(nc.semaphore(f"dq{i}")) for i in range(NDS)]
        self.dcnt = [0] * NDS
        self.dnext = 0
        self.dnext2 = 0
        self.nwait = {}
        self.pe_serial = False
        self.nins = 0
        self.uid = 0

    def push(self):
        st = ExitStack()
        st.__enter__()
        self.stacks.append(st)

    def pop(self):
        st = self.stacks.pop()
        st.__exit__(None, None, None)

    def sb(self, shape, dt, name):
        self.uid += 1
        nm = f"{name}_{self.uid}"
        t = self.stacks[-1].enter_context(self.nc.sbuf_tensor(nm, list(shape), dt))
        return Tile(t, nm)

    def ps(self, shape, dt, name):
        self.uid += 1
        nm = f"{name}_{self.uid}"
        t = self.stacks[-1].enter_context(self.nc.psum_tensor(nm, list(shape), dt))
        return Tile(t, nm)

    def _wait(self, e, tok):
        sem, v, key = tok
        if self.known[e].get(key, 0) >= v:
            return
        self.E[e].wait_ge(sem, v)
        self.known[e][key] = v
        self.nins += 1
        self.nwait[e] = self.nwait.get(e, 0) + 1

    def _need(self, e, reads, writes):
        reads = _bufs(reads)
        writes = _bufs(writes)
        need = {}

        def add(tok):
            sem, v, key = tok
            if self.known[e].get(key, 0) >= v:
                return
            if key not in need or need[key][1] < v:
                need[key] = tok

        for b in reads:
            if b.w is not None:
                add(b.w)
        for b in writes:
            if b.w is not None and not (e == "pe" and b.w[2] == "pe" and not self.pe_serial):
                add(b.w)
            for k, t in b.r.items():
                if not (e == "pe" and k == "pe"):
                    add(t)
        return reads, writes, list(need.values())

    def _issue(self, e, fn, need):
        for tok in need[:-1]:
            self._wait(e, tok)
        ins = fn()
        if need:
            sem, v, key = need[-1]
            if EMBED_WAITS:
                ins._wait_ge(sem, v)
                self.known[e][key] = v
            else:
                raise RuntimeError
        return ins

    def _post(self, tok, reads, writes):
        key = tok[2]
        for b in reads:
            b.r[key] = tok
        for b in writes:
            b.w = tok
            b.r = {}

    def op(self, e, fn, reads, writes):
        reads, writes, need = self._need(e, reads, writes)
        if e not in EMBED_ENG:
            for tok in need:
                self._wait(e, tok)
            need = []
        ins = self._issue(e, fn, need)
        self.cnt[e] += 1
        ins.then_inc(self.sem[e], 1)
        self._post((self.sem[e], self.cnt[e], e), reads, writes)
        self.nins += 1
        return ins

    def dma(self, q, out, in_, reads, writes, **kw):
        if q == "sp":
            i = self.dnext
            self.dnext = (i + 1) % 16
        else:
            i = 16 + self.dnext2
            self.dnext2 = (self.dnext2 + 1) % (NDS - 16)
        key = ("d", i)
        if self.dcnt[i] > 0:
            self._wait(q, (self.dsems[i], self.dcnt[i], key))
        reads, writes, need = self._need(q, reads, writes)
        for tok in need:
            self._wait(q, tok)
        ins = self.E[q].dma_start(out=out, in_=in_, **kw)
        self.dcnt[i] += 16
        ins.then_inc(self.dsems[i], 16)
        self._post((self.dsems[i], self.dcnt[i], key), reads, writes)
        self.nins += 1

    def barrier(self, engines=None):
        for e in (engines or self.E):
            for f in self.E:
                if f != e and self.cnt[f] > 0:
                    self._wait(e, (self.sem[f], self.cnt[f], f))
            for i in range(NDS):
                if self.dcnt[i] > 0:
                    self._wait(e, (self.dsems[i], self.dcnt[i], ("d", i)))

    def mm(self, out, lhsT, rhs, start, stop, reads, writes, skip=False):
        if skip:
            return self.op("pe", lambda: self.nc.tensor.matmul(out, lhsT, rhs, start=start, stop=stop,
                                                               skip_group_check=True), reads, writes)
        return self.op("pe", lambda: self.nc.tensor.matmul(out, lhsT, rhs, start=start, stop=stop), reads, writes)

    def tr(self, out, in_, ident, reads, writes):
        return self.op("pe", lambda: self.nc.tensor.transpose(out, in_, ident), reads, writes)

    def act(self, out, in_, func, reads, writes, **kw):
        return self.op("act", lambda: self.nc.scalar.activation(out, in_, func, **kw), reads, writes)

    def v(self, e, name, reads, writes, *a, **kw):
        eng = self.E[e]
        return self.op(e, lambda: getattr(eng, name)(*a, **kw), reads, writes)


class Ctx:
    pass


class Tile4:
    def __init__(self, tile):
        self.t = tile
        self.b = tile.b
        self.subs = tile.subs

    def __getitem__(self, idx):
        v = self.t[:].rearrange("p (m n) -> p m n", m=4)
        return v[idx]


def setup_common(p):
    nc = p.nc
    C = Ctx()
    C.identf = p.sb([128, 128], F32, "identf")
    C.ident = p.sb([128, 128], BF16, "ident")
    p.v("pool", "memset", [], [C.identf], C.identf[:], 0.0)
    p.op("pool", lambda: nc.gpsimd.affine_select(out=C.identf[:], in_=C.identf[:], pattern=[[-1, 128]],
                                                 compare_op=ALU.not_equal, fill=1.0, base=0, channel_multiplier=1),
         [C.identf], [C.identf])
    p.v("dve", "tensor_copy", [C.identf], [C.ident], C.ident[:], C.identf[:])
    C.G = p.sb([128, D], F32, "lnG")
    C.Bt = p.sb([128, D], F32, "lnB")
    C.st6 = [p.sb([128, 2, 6], F32, f"st6{i}") for i in range(2)]
    C.mv = [p.sb([128, 4], F32, f"mv{i}") for i in range(2)]
    C.ysb = None
    C.epi = 0
    return C


def load_ln(p, C, g_d, b_d, nysb=2):
    C.ysb = [p.sb([128, D], F32, f"ysb{i}") for i in range(nysb)]
    p.dma("sp", C.G[:], g_d.rearrange("(o n) -> o n", o=1).to_broadcast([128, D]), [], [C.G])
    p.dma("sp", C.Bt[:], b_d.rearrange("(o n) -> o n", o=1).to_broadcast([128, D]), [], [C.Bt])


def ln_epilogue(p, C, xs, py0, py1, yscale, out_rows):
    k = C.epi % 2
    C.epi += 1
    ysb, st6, mv = C.ysb[k % len(C.ysb)], C.st6[k], C.mv[k]
    p.act(ysb[:, 0:512], py0[:], AF.Copy, [py0], [ysb], scale=yscale)
    p.act(ysb[:, 512:1024], py1[:], AF.Copy, [py1], [ysb], scale=yscale)
    p.v("dve", "scalar_tensor_tensor", [xs, ysb], [ysb], out=ysb[:], in0=xs[:], scalar=ALPHA, in1=ysb[:],
        op0=ALU.mult, op1=ALU.add)
    for c in range(2):
        p.v("dve", "bn_stats", [ysb], [st6], st6[:, c, :], ysb[:, c * 512:(c + 1) * 512])
    p.v("dve", "bn_aggr", [st6], [mv], mv[:, 0:2], st6[:])
    p.v("dve", "tensor_scalar", [mv], [mv], mv[:, 3:4], mv[:, 1:2], LN_EPS, None, ALU.add)
    p.act(mv[:, 3:4], mv[:, 3:4], AF.Sqrt, [mv], [mv])
    p.v("dve", "reciprocal", [mv], [mv], mv[:, 2:3], mv[:, 3:4])
    p.v("dve", "tensor_scalar", [ysb, mv], [ysb], ysb[:], ysb[:], mv[:, 0:1], mv[:, 2:3], ALU.subtract, ALU.mult)
    p.v("pool", "tensor_tensor", [ysb, C.G], [ysb], ysb[:], ysb[:], C.G[:], ALU.mult)
    p.v("pool", "tensor_tensor", [ysb, C.Bt], [ysb], ysb[:], ysb[:], C.Bt[:], ALU.add)
    p.dma("sp", out_rows, ysb[:], [ysb], [])


def ffn_phase(p, C, T, x_in, x_out, wg_d, wu_d, wd_d, g_d, b_d):
    p.push()
    Wg = p.sb([128, NCH, DFF], BF16, "Wg")
    Wu = p.sb([128, NCH, DFF], BF16, "Wu")
    Wd = p.sb([128, NFC, D], BF16, "Wd")
    for c in range(NCH):
        p.dma("pool", Wg[:, c, :], wg_d[c * 128:(c + 1) * 128, :], [], [Wg.sub(c)])
    for c in range(NCH):
        p.dma("pool", Wu[:, c, :], wu_d[c * 128:(c + 1) * 128, :], [], [Wu.sub(c)])
    for f in range(NFC):
        p.dma("pool", Wd[:, f, :], wd_d[f * 128:(f + 1) * 128, :], [], [Wd.sub(f)])
    load_ln(p, C, g_d, b_d)
    xs = [[p.sb([128, D], F32, f"xs{a}{j}") for j in range(2)] for a in range(2)]
    xb = [p.sb([128, D], BF16, f"xb{j}") for j in range(2)]
    xT = [p.sb([128, NCH, 256], BF16, f"xT{a}") for a in range(2)]
    h = p.sb([128, NFC, 256], BF16, "h")
    sg = [p.sb([128, 256], F32, f"sg{a}") for a in range(2)]
    ptr = [p.ps([128, NCH, 128], BF16, f"ptr{a}") for a in range(2)]
    pgu = [p.ps([128, 2, 256], F32, f"pgu{a}") for a in range(2)]
    py = [p.ps([128, 512], F32, f"py{a}") for a in range(4)]
    NT = T // 256

    def prep(t):
        a = t % 2
        for j in range(2):
            r0 = t * 256 + j * 128
            p.dma("sp", xs[a][j][:], x_in[r0:r0 + 128, :], [], [xs[a][j]])
            p.act(xb[j][:], xs[a][j][:], AF.Copy, [xs[a][j]], [xb[j]])
            for c in range(NCH):
                p.tr(ptr[j][:, c, :], xb[j][:, c * 128:(c + 1) * 128], C.ident[:], [xb[j], C.ident], [ptr[j]])
            p.v("dve", "tensor_copy", [ptr[j]], [xT[a]], xT[a][:, :, j * 128:(j + 1) * 128], ptr[j][:])

    prep(0)
    for t in range(NT):
        a = t % 2
        for f in range(NFC):
            k = f % 2
            for c in range(NCH):
                p.mm(pgu[k][:, 0, :], Wg[:, c, f * 128:(f + 1) * 128], xT[a][:, c, :], c == 0, c == NCH - 1,
                     [Wg.sub(c), xT[a]], [pgu[k]])
            for c in range(NCH):
                p.mm(pgu[k][:, 1, :], Wu[:, c, f * 128:(f + 1) * 128], xT[a][:, c, :], c == 0, c == NCH - 1,
                     [Wu.sub(c), xT[a]], [pgu[k]])
            p.act(sg[k][:], pgu[k][:, 0, :], AF.Silu, [pgu[k]], [sg[k]])
            p.v("dve", "tensor_tensor", [sg[k], pgu[k]], [h], h[:, f, :], sg[k][:], pgu[k][:, 1, :], ALU.mult)
        if t + 1 < NT:
            prep(t + 1)
        for j in range(2):
            for hf in range(2):
                pb = py[j * 2 + hf]
                for f in range(NFC):
                    p.mm(pb[:], h[:, f, j * 128:(j + 1) * 128], Wd[:, f, hf * 512:(hf + 1) * 512], f == 0,
                         f == NFC - 1, [h, Wd.sub(f)], [pb])
            r0 = t * 256 + j * 128
            ln_epilogue(p, C, xs[a][j], py[j * 2], py[j * 2 + 1], 0.5, x_out[r0:r0 + 128, :])
    p.barrier()
    p.pop()


def _t5_bucket_np(dist):
    n = np.maximum(dist, 0)
    nf = np.maximum(n, 1).astype(np.float32)
    large = 16 + (np.log(nf / np.float32(16)) / np.float32(math.log(128 / 16)) * np.float32(16)).astype(np.int32)
    return np.where(n < 16, n, np.minimum(large, 31))


SWA_HPERM = [g * 8 + 2 * m + par for g in range(2) for par in range(2) for m in range(4)]


def swa_consts(rel_bias, sinks):
    s = np.arange(128)[:, None]
    t = np.arange(128)[None, :]
    out = np.empty((2, 128, 16, 128), np.float32)
    for jj in range(2):
        d = t - s + 128 * jj
        valid = (d >= 0) & (d < 128)
        tab = rel_bias[_t5_bucket_np(d)]
        tab = np.where(valid[:, :, None], tab, np.float32(NEG))
        out[jj] = np.transpose(tab[:, :, SWA_HPERM], (0, 2, 1))
    return out, np.ascontiguousarray(sinks[SWA_HPERM])


def load_x_tile(p, C, x_rows, xs, xb, ptr, xT_dst):
    p.dma("sp", xs[:], x_rows, [], [xs])
    p.act(xb[:], xs[:], AF.Copy, [xs], [xb])
    for c in range(NCH):
        p.tr(ptr[:, c, :], xb[:, c * 128:(c + 1) * 128], C.ident[:], [xb, C.ident], [ptr])
    p.v("dve", "tensor_copy", [ptr], [xT_dst], xT_dst[:], ptr[:])


def out_proj_ln(p, C, o_tok, ptr, oT, Wo, py0, py1, xs, out_rows):
    for c in range(NCH):
        p.tr(ptr[:, c, :], o_tok[:, c * 128:(c + 1) * 128], C.ident[:], [o_tok, C.ident], [ptr])
    p.v("dve", "tensor_copy", [ptr], [oT], oT[:], ptr[:])
    for hf, pb in enumerate((py0, py1)):
        for c in range(NCH):
            p.mm(pb[:], oT[:, c, :], Wo[:, c, hf * 512:(hf + 1) * 512], c == 0, c == NCH - 1, [oT, Wo], [pb])
    ln_epilogue(p, C, xs, py0, py1, 1.0, out_rows)


def load_w_chunks(p, W, w_d, col0, ncols, nsplit=2):
    src = w_d.rearrange("(c p) n -> p c n", p=128)
    step = NCH // nsplit
    for s in range(nsplit):
        p.dma("pool", W[:, s * step:(s + 1) * step, 0:ncols], src[:, s * step:(s + 1) * step, col0:col0 + ncols], [], [W])


def swa_phase(p, C, T, x_in, x_out, w_in_d, w_out_d, bias_d, sinks_d, g_d, b_d):
    nc = p.nc
    p.push()
    Wq = p.sb([128, NCH, 1024], BF16, "Wq")
    Wk2 = p.sb([128, NCH, 2, 2, 64], BF16, "Wk2")
    Wv = p.sb([128, NCH, 128], BF16, "Wv")
    Wo = p.sb([128, NCH, 1024], BF16, "Wo")
    load_w_chunks(p, Wq, w_in_d, 0, 1024, 4)
    src = w_in_d.rearrange("(c p) n -> p c n", p=128)
    for g in range(2):
        for dup in range(2):
            p.dma("pool", Wk2[:, :, g, dup, :], src[:, :, 1024 + g * 64:1024 + (g + 1) * 64], [], [Wk2])
    load_w_chunks(p, Wv, w_in_d, 1152, 128, 1)
    load_w_chunks(p, Wo, w_out_d, 0, 1024, 4)
    load_ln(p, C, g_d, b_d)
    BT = p.sb([128, 2, 16, 128], F32, "BT")
    for jj in range(2):
        p.dma("sp", BT[:, jj, :, :], bias_d[jj], [], [BT])
    ES = p.sb([128, 16], F32, "ES")
    p.dma("sp", ES[:], sinks_d.rearrange("(o n) -> o n", o=1).to_broadcast([128, 16]), [], [ES])
    p.act(ES[:], ES[:], AF.Exp, [ES], [ES])

    xs = [p.sb([128, D], F32, f"xs{a}") for a in range(2)]
    xb = [p.sb([128, D], BF16, f"xb{a}") for a in range(2)]
    xT = [p.sb([128, NCH, 128], BF16, f"xT{a}") for a in range(2)]
    qT = [p.sb([128, 8, 128], BF16, f"qT{a}") for a in range(2)]
    kbuf = [p.sb([128, 2, 128], BF16, f"kb{a}") for a in range(2)]
    vbuf = [p.sb([128, 2, 65], BF16, f"vb{a}") for a in range(2)]
    for a in range(2):
        p.v("dve", "memset", [], [vbuf[a]], vbuf[a][:, :, 64:65], 1.0)
    sc = [p.sb([128, 4, 128], F32, f"sc{a}") for a in range(2)]
    e = [p.sb([128, 4, 128], BF16, f"e{a}") for a in range(4)]
    den = [p.sb([128, 8], F32, f"den{a}") for a in range(2)]
    o_tok = [p.sb([128, D], BF16, f"otok{a}") for a in range(2)]
    oT = p.sb([128, NCH, 128], BF16, "oT")
    ptr = [p.ps([128, NCH, 128], BF16, f"ptr{a}") for a in range(1)]
    pq = [p.ps([128, 512], F32, f"pq{a}") for a in range(2)]
    pkv = p.ps([128, 512], F32, "pkv")
    ps = [p.ps([128, 4, 128], F32, f"ps{a}") for a in range(2)]
    po = [p.ps([128, 512], F32, f"po{a}") for a in range(2)]
    ps = ps + [Tile4(pq[0]), Tile4(pq[1])]
    NT = T // 128
    nsc = 0
    pending = None
    for i in range(NT):
        a = i % 2
        load_x_tile(p, C, x_in[i * 128:(i + 1) * 128, :], xs[a], xb[a], ptr[0], xT[a])
        for m in range(8):
            pb = pq[m // 4]
            for c in range(NCH):
                p.mm(pb[:, (m % 4) * 128:(m % 4 + 1) * 128], Wq[:, c, m * 128:(m + 1) * 128], xT[a][:, c, :], c == 0,
                     c == NCH - 1, [Wq, xT[a]], [pb])
        for g in range(2):
            for c in range(NCH):
                p.mm(pkv[:, g * 128:(g + 1) * 128], Wk2[:, c, g, :, :], xT[a][:, c, :], c == 0, c == NCH - 1,
                     [Wk2, xT[a]], [pkv])
        for c in range(NCH):
            p.mm(pkv[:, 256:384], xT[a][:, c, :], Wv[:, c, :], c == 0, c == NCH - 1, [Wv, xT[a]], [pkv])
        p.act(qT[a][:, 0:4, :], pq[0][:], AF.Copy, [pq[0]], [qT[a]])
        p.act(qT[a][:, 4:8, :], pq[1][:], AF.Copy, [pq[1]], [qT[a]])
        p.v("dve", "tensor_copy", [pkv], [kbuf[a]], kbuf[a][:], pkv[:, 0:256])
        p.v("dve", "tensor_copy", [pkv], [vbuf[a]], vbuf[a][:, :, 0:64], pkv[:, 256:384])
        if pending is not None:
            pending()
            pending = None
        kts = ([i - 1] if i > 0 else []) + [i]
        ot4 = o_tok[a][:].rearrange("p (m r d) -> p m r d", m=8, r=2, d=64)
        items = [(g, par, kt, kt == kts[0], kt == kts[-1]) for g in range(2) for par in range(2) for kt in kts]
        LA = 2

        def stage_a(n):
            g, par, kt, first, last = items[n]
            b = g * 2 + par
            jj = i - kt
            k = (nsc + n) % 4
            p.mm(ps[k][:], kbuf[kt % 2][par * 64:(par + 1) * 64, g, :],
                 qT[a][par * 64:(par + 1) * 64, g * 4:(g + 1) * 4, :], True, True, [kbuf[kt % 2], qT[a]], [ps[k]])
            p.v("dve", "scalar_tensor_tensor", [ps[k], BT], [sc[k % 2]], out=sc[k % 2][:], in0=ps[k][:], scalar=0.125,
                in1=BT[:, jj, b * 4:(b + 1) * 4, :], op0=ALU.mult, op1=ALU.add)
            p.act(e[k][:], sc[k % 2][:], AF.Exp, [sc[k % 2]], [e[k]])

        def stage_c(n):
            g, par, kt, first, last = items[n]
            b = g * 2 + par
            k = (nsc + n) % 4
            pob = po[b % 2]
            pov = pob[:, 0:260].rearrange("p (m d) -> p m d", m=4, d=65)
            for m in range(4):
                p.mm(pov[:, m, :], e[k][:, m, :], vbuf[kt % 2][:, g, :], first and m == 0, last,
                     [e[k], vbuf[kt % 2]], [pob], skip=True)
            if last:
                dn = den[b % 2]
                p.v("dve", "tensor_tensor", [pob, ES], [dn], dn[:, 0:4], pov[:, :, 64], ES[:, b * 4:(b + 1) * 4], ALU.add)
                p.v("dve", "reciprocal", [dn], [dn], dn[:, 4:8], dn[:, 0:4])
                p.v("dve", "tensor_tensor", [pob, dn], [o_tok[a]], ot4[:, g * 4:(g + 1) * 4, par, :], pov[:, :, 0:64],
                    dn[:, 4:8].unsqueeze(2).to_broadcast([128, 4, 64]), ALU.mult)

        NI = len(items)
        for n in range(NI + LA):
            if n < NI:
                stage_a(n)
            if n - LA >= 0:
                stage_c(n - LA)
        nsc += NI
        def tail(a=a, i=i):
            out_proj_ln(p, C, o_tok[a], ptr[0], oT, Wo, pq[0], pq[1], xs[a], x_out[i * 128:(i + 1) * 128, :])

        pending = tail
    pending()
    p.barrier()
    p.pop()


def hgrn_consts():
    s = np.arange(128)[:, None]
    t = np.arange(128)[None, :]
    same = (s // 64) == (t // 64)
    U = (same & (s <= t)).astype(np.float32)
    Emid = (same & ((s % 64) <= 31)).astype(np.float32)
    urhs = np.zeros((128, 134), np.float32)
    urhs[:, 0:128] = U - Emid
    for c in range(2):
        urhs[:, 128 + c] = ((s[:, 0] // 64 == c) & ((s[:, 0] % 64) <= 31)).astype(np.float32)
        urhs[:, 130 + c] = (s[:, 0] // 64 == c).astype(np.float32)
    urhs[:, 132] = urhs[:, 63]
    urhs[:, 133] = urhs[:, 127]
    mneg = -(U - Emid)
    return np.concatenate([urhs, mneg, U], axis=1).astype(np.float32)


def hgrn_phase(p, C, T, x_in, x_out, layer, w_in_d, w_out_d, gain_d, lb_d, hc_d, g_d, b_d):
    nc = p.nc
    p.push()
    W = [p.sb([128, NCH, 1024], BF16, f"Wh{j}") for j in range(4)]
    Wo = p.sb([128, NCH, 1024], BF16, "Wo")
    for j in range(4):
        load_w_chunks(p, W[j], w_in_d, j * 1024, 1024, 4)
    load_w_chunks(p, Wo, w_out_d, 0, 1024, 4)
    load_ln(p, C, g_d, b_d)
    HC = p.sb([128, 390], F32, "HC")
    p.dma("sp", HC[:], hc_d, [], [HC])
    Urhs, Mneg, Mbd = HC[:, 0:134], HC[:, 134:262], HC[:, 262:390]
    Gn = p.sb([128, 128], F32, "Gn")
    p.dma("sp", Gn[:], gain_d.rearrange("(o n) -> o n", o=1).to_broadcast([128, 128]), [], [Gn])
    LBb = p.sb([128, D], F32, "LBb")
    OMLb = p.sb([128, D], F32, "OMLb")
    lbT = p.sb([128, 16], F32, "lbT")
    p.push()
    L4 = p.sb([128, 4, D], F32, "L4")
    for j in range(4):
        p.dma("sp", L4[:, j, :], lb_d[j].rearrange("(o n) -> o n", o=1).to_broadcast([128, D]), [], [L4])
    p.act(L4[:], L4[:], AF.Exp, [L4], [L4])
    p.v("dve", "tensor_tensor", [L4], [OMLb], OMLb[:], L4[:, 0, :], L4[:, 1, :], ALU.add)
    p.v("dve", "tensor_tensor", [L4, OMLb], [OMLb], OMLb[:], OMLb[:], L4[:, 2, :], ALU.add)
    p.v("dve", "tensor_tensor", [L4, OMLb], [OMLb], OMLb[:], OMLb[:], L4[:, 3, :], ALU.add)
    p.v("dve", "reciprocal", [OMLb], [OMLb], OMLb[:], OMLb[:])
    p.v("dve", "memset", [], [LBb], LBb[:], 0.0)
    for j in range(1, layer + 1):
        p.v("dve", "tensor_tensor", [L4, LBb], [LBb], LBb[:], LBb[:], L4[:, j, :], ALU.add)
    p.v("dve", "tensor_tensor", [LBb, OMLb], [LBb], LBb[:], LBb[:], OMLb[:], ALU.mult)
    p.v("dve", "tensor_scalar", [LBb], [OMLb], OMLb[:], LBb[:], -1.0, 1.0, ALU.mult, ALU.add)
    plb = p.ps([128, 8, 128], F32, "plb")
    for h in range(8):
        p.op("pe", lambda h=h: nc.tensor.transpose(plb[:, h, :], LBb[:, h * 128:(h + 1) * 128], C.identf[:]), [LBb, C.identf], [plb])
    p.v("dve", "tensor_copy", [plb], [lbT], lbT[:, 0:8], plb[:, :, 0])
    p.v("dve", "tensor_scalar", [lbT], [lbT], lbT[:, 8:16], lbT[:, 0:8], -1.0, 1.0, ALU.mult, ALU.add)
    p.barrier()
    p.pop()

    xs = [p.sb([128, D], F32, f"xs{a}") for a in range(2)]
    xb = [p.sb([128, D], BF16, f"xb{a}") for a in range(2)]
    xT = [p.sb([128, NCH, 128], BF16, f"xT{a}") for a in range(2)]
    qT = p.sb([128, 8, 128], F32, "qT")
    smT = p.sb([128, 8, 128], F32, "smT")
    fs = p.sb([128, D], F32, "fs")
    logf = p.sb([128, D], F32, "logf")
    kk = p.sb([128, D], F32, "kk")
    vb = p.sb([128, D], BF16, "vb")
    GG = p.sb([128, D], F32, "GG")
    eD = [p.sb([128, 128], F32, f"eD{a}") for a in range(2)]
    eDn = [p.sb([128, 128], F32, f"eDn{a}") for a in range(2)]
    eDp = [p.sb([128, 128], F32, f"eDp{a}") for a in range(2)]
    ex = [p.sb([128, 8], F32, f"ex{a}") for a in range(2)]
    qz = [p.sb([128, 2, 128], BF16, f"qz{a}") for a in range(2)]
    kz = [p.sb([128, 2, 128], BF16, f"kz{a}") for a in range(2)]
    for a in range(2):
        p.v("dve", "memset", [], [qz[a]], qz[a][:], 0.0)
        p.v("dve", "memset", [], [kz[a]], kz[a][:], 0.0)
    kTt = [p.sb([128, 128], BF16, f"kTt{a}") for a in range(2)]
    aT = [p.sb([128, 128], BF16, f"aT{a}") for a in range(2)]
    Sbf = [p.sb([128, 128], BF16, f"Sbf{a}") for a in range(2)]
    T1 = p.sb([128, 128], F32, "T1")
    S = p.sb([128, 8, 128], F32, "S")
    p.v("dve", "memset", [], [S], S[:], 0.0)
    sq = p.sb([128, 4, 128], F32, "sq")
    t1 = p.sb([128, 4, 128], F32, "t1")
    ss = p.sb([128, 16], F32, "ss")
    o_tok = [p.sb([128, D], BF16, f"otok{a}") for a in range(2)]
    oT = p.sb([128, NCH, 128], BF16, "oT")
    ptr = p.ps([128, NCH, 128], BF16, "ptr")
    PA = [p.ps([128, 512], F32, f"PA{a}") for a in range(2)]
    PB = [p.ps([128, 512], F32, f"PB{a}") for a in range(2)]
    Dk = p.ps([128, 512], F32, "Dk")
    Mi = p.ps([128, 512], F32, "Mi")
    po = p.ps([128, 4, 128], F32, "po")
    NT = T // 128
    hcnt = 0
    pending = None
    for i in range(NT):
        a = i % 2
        load_x_tile(p, C, x_in[i * 128:(i + 1) * 128, :], xs[a], xb[a], ptr, xT[a])

        def proj_fm(P2, Wm):
            for h in range(8):
                pb = P2[h // 4]
                for c in range(NCH):
                    p.mm(pb[:, (h % 4) * 128:(h % 4 + 1) * 128], Wm[:, c, h * 128:(h + 1) * 128], xT[a][:, c, :],
                         c == 0, c == NCH - 1, [Wm, xT[a]], [pb])

        def proj_tm(P2, Wm):
            for hf in range(2):
                for c in range(NCH):
                    p.mm(P2[hf][:], xT[a][:, c, :], Wm[:, c, hf * 512:(hf + 1) * 512], c == 0, c == NCH - 1,
                         [Wm, xT[a]], [P2[hf]])

        proj_fm(PA, W[0])
        proj_fm(PB, W[1])
        for hf in range(2):
            p.act(qT[:, hf * 4:(hf + 1) * 4, :], PA[hf][:], AF.Silu, [PA[hf]], [qT])
            p.act(smT[:, hf * 4:(hf + 1) * 4, :], PB[hf][:], AF.Sigmoid, [PB[hf]], [smT], scale=-1.0)
        proj_tm(PA, W[1])
        proj_tm(PB, W[2])
        for hf in range(2):
            p.act(fs[:, hf * 512:(hf + 1) * 512], PA[hf][:], AF.Sigmoid, [PA[hf]], [fs])
            p.v("dve", "tensor_copy", [PB[hf]], [vb], vb[:, hf * 512:(hf + 1) * 512], PB[hf][:])
        proj_tm(PA, W[3])
        p.v("dve", "tensor_tensor", [fs, OMLb], [fs], fs[:], fs[:], OMLb[:], ALU.mult)
        p.v("dve", "tensor_tensor", [fs, LBb], [fs], fs[:], fs[:], LBb[:], ALU.add)
        p.act(logf[:], fs[:], AF.Ln, [fs], [logf])
        p.v("pool", "tensor_scalar", [fs], [kk], kk[:], fs[:], -1.0, 1.0, ALU.mult, ALU.add)
        if pending is not None:
            pending()
            pending = None
        for hf in range(2):
            p.act(GG[:, hf * 512:(hf + 1) * 512], PA[hf][:], AF.Silu, [PA[hf]], [GG])
        p.v("pool", "tensor_tensor", [GG, Gn], [GG], GG[:].rearrange("p (h v) -> p h v", h=8), GG[:].rearrange("p (h v) -> p h v", h=8),
            Gn[:].unsqueeze(1).to_broadcast([128, 8, 128]), ALU.mult)
        XB = [Dk, PA[0]]
        YB = [Mi, PA[1]]

        def hs1(h):
            k = h % 2
            hs = slice(h * 128, (h + 1) * 128)
            X = XB[k]
            p.mm(X[:, 0:134], logf[:, hs], Urhs, True, True, [logf, HC], [X])
            p.mm(X[:, 256:384], Mneg, logf[:, hs], True, True, [logf, HC], [X])
            p.act(eD[k][:], X[:, 0:128], AF.Exp, [X], [eD[k]])
            p.act(eDn[k][:], X[:, 0:128], AF.Exp, [X], [eDn[k]], scale=-1.0)
            p.act(ex[k][:, 0:6], X[:, 128:134], AF.Exp, [X], [ex[k]])
            p.act(eDp[k][:], X[:, 256:384], AF.Exp, [X], [eDp[k]])
            for c in range(2):
                cs = slice(c * 64, (c + 1) * 64)
                p.v("dve", "tensor_tensor", [qT, eD[k]], [qz[k]], qz[k][:, c, cs], qT[:, h, cs], eD[k][:, cs], ALU.mult)
                p.v("dve", "tensor_tensor", [kk, eDp[k]], [kz[k]], kz[k][cs, c, :], kk[cs, hs], eDp[k][cs, :], ALU.mult)
            p.v("dve", "scalar_tensor_tensor", [smT, lbT, eDn[k]], [kTt[k]], out=kTt[k][:], in0=smT[:, h, :],
                scalar=lbT[:, 8 + h:9 + h], in1=eDn[k][:], op0=ALU.mult, op1=ALU.mult)

        def hs2(h):
            k = h % 2
            Y = YB[k]
            for c in range(2):
                cs = slice(c * 64, (c + 1) * 64)
                p.mm(Y[:, c * 64:(c + 1) * 64], kTt[k][:], qz[k][:, c, cs], True, True, [kTt[k], qz[k]], [Y])
            p.v("dve", "tensor_tensor", [Y, HC], [aT[k]], aT[k][:], Y[:, 0:128], Mbd, ALU.mult)

        def hs3(h):
            k = h % 2
            hh = h % 4
            hs = slice(h * 128, (h + 1) * 128)
            Y = YB[k]
            for c in range(2):
                wsl = slice(128 + c * 128, 256 + c * 128)
                p.mm(Y[:, wsl], kz[k][:, c, :], vb[:, hs], True, True, [kz[k], vb], [Y])
                sb_ = Sbf[c]
                p.v("act", "mul", [S, ex[k]], [sb_], sb_[:], S[:, h, :], ex[k][:, c:c + 1])
                p.mm(po[:, hh, :], qz[k][:, c, :], sb_[:], c == 0, False, [qz[k], sb_], [po], skip=True)
                p.v("dve", "tensor_scalar", [S, ex[k]], [T1], T1[:], S[:, h, :], ex[k][:, 2 + c:3 + c], None, ALU.mult)
                p.v("dve", "scalar_tensor_tensor", [Y, ex[k], T1], [S], out=S[:, h, :], in0=Y[:, wsl],
                    scalar=ex[k][:, 4 + c:5 + c], in1=T1[:], op0=ALU.mult, op1=ALU.add)
            p.mm(po[:, hh, :], aT[k][:], vb[:, hs], False, True, [aT[k], vb], [po], skip=True)
            if hh == 3:
                g4 = h // 4
                p.act(sq[:], po[:], AF.Square, [po], [sq])
                p.v("dve", "reduce_sum", [sq], [ss], ss[:, 0:4], sq[:], AX.X)
                p.v("dve", "tensor_scalar", [ss], [ss], ss[:, 4:8], ss[:, 0:4], 1.0 / 128, 1e-6, ALU.mult, ALU.add)
                p.act(ss[:, 4:8], ss[:, 4:8], AF.Sqrt, [ss], [ss])
                p.v("dve", "reciprocal", [ss], [ss], ss[:, 8:12], ss[:, 4:8])
                p.v("dve", "tensor_tensor", [po, ss], [t1], t1[:], po[:], ss[:, 8:12].unsqueeze(2).to_broadcast([128, 4, 128]), ALU.mult)
                p.v("dve", "tensor_tensor", [t1, GG], [o_tok[a]], o_tok[a][:, g4 * 512:(g4 + 1) * 512],
                    t1[:].rearrange("p h v -> p (h v)"), GG[:, g4 * 512:(g4 + 1) * 512], ALU.mult)

        hs1(0)
        for h in range(8):
            if h + 1 < 8:
                hs1(h + 1)
            hs2(h)
            hs3(h)
        def tail(a=a, i=i):
            out_proj_ln(p, C, o_tok[a], ptr, oT, Wo, PB[0], PB[1], xs[a], x_out[i * 128:(i + 1) * 128, :])

        pending = tail
    pending()
    p.barrier()
    p.pop()


def nsa_consts(rel_bias):
    s = np.arange(128)[:, None]
    t = np.arange(128)[None, :]
    bw = np.empty((2, 128, 16, 128), np.float32)
    for jj in range(2):
        d = t - s + 128 * jj
        tab = rel_bias[_t5_bucket_np(d)]
        if jj == 0:
            tab = np.where((d >= 0)[:, :, None], tab, np.float32(NEG))
        bw[jj] = np.transpose(tab, (0, 2, 1))
    tq = np.arange(128)[:, None]
    mq = 14 - np.arange(15)[None, :]
    dc = tq + 16 * mq - 127
    gc = rel_bias[_t5_bucket_np(dc)]
    gc = np.where((dc >= 0)[:, :, None], gc, np.float32(NEG))
    gc = np.ascontiguousarray(np.transpose(gc, (0, 2, 1)))
    ch = np.ascontiguousarray(rel_bias[31])
    mask4 = np.where(s > t, np.float32(0.0), np.float32(NEG)).astype(np.float32)
    return bw, gc, ch, mask4


def nsa_eexp(T):
    return (np.arange(T)[None, :] // 64 == np.arange(64)[:, None]).astype(np.float32)


def nsa_phase(p, C, T, x_in, x_out, w_in_d, w_out_d, pos_d, w1_d, w2_d, bw_d, gc_d, ch_d, m4_d, ee_d, g_d, b_d):
    nc = p.nc
    SKIP = set(os.environ.get("NSA_SKIP", "").split(","))
    p.push()
    NT = T // 128
    srcw = w_in_d.rearrange("(c p) n -> p c n", p=128)
    Wqp = p.sb([128, NCH, 8, 2, 64], BF16, "Wqp")
    for j in range(8):
        A = (j // 4) * 8 + j % 4
        for hf, hd in enumerate((A, A + 4)):
            p.dma("pool", Wqp[:, :, j, hf, :], srcw[:, :, hd * 64:(hd + 1) * 64], [], [Wqp])
    Wr = p.sb([128, NCH, 1584], BF16, "Wr")
    for sgm in range(4):
        p.dma("pool", Wr[:, sgm * 2:(sgm + 1) * 2, :], srcw[:, sgm * 2:(sgm + 1) * 2, 1024:2608], [], [Wr])
    OKC, OVC, OKS, OVS, OKW, OVW, OGT = 0, 256, 512, 768, 1024, 1280, 1536
    Wo = p.sb([128, NCH, 1024], BF16, "Wo")
    load_w_chunks(p, Wo, w_out_d, 0, 1024, 4)
    w1 = [p.sb([128, 32, 128], BF16, f"w1_{kv}") for kv in range(2)]
    w2k = p.sb([128, 2, 64], BF16, "w2k")
    w2v = p.sb([128, 64], BF16, "w2v")
    posT = p.sb([128, 2, 32], BF16, "posT")
    for kv in range(2):
        for hf in range(2):
            p.dma("pool", w1[kv][hf * 64:(hf + 1) * 64, :, :], w1_d[kv].rearrange("l d e -> d l e"), [], [w1[kv]])
            p.dma("pool", posT[hf * 64:(hf + 1) * 64, kv, :], pos_d[kv].rearrange("l d -> d l"), [], [posT],
                  allow_slow_non_contiguous=True)
    for dup in range(2):
        p.dma("pool", w2k[:, dup, :], w2_d[0], [], [w2k])
    p.dma("pool", w2v[:], w2_d[1], [], [w2v])
    load_ln(p, C, g_d, b_d, nysb=1)
    BW = p.sb([128, 2, 16, 128], F32, "BW")
    for jj in range(2):
        p.dma("sp", BW[:, jj, :, :], bw_d[jj], [], [BW])
    G8 = p.sb([128, 16, 15], F32, "G8")
    p.dma("sp", G8[:], gc_d, [], [G8])
    CHb = p.sb([128, 16], F32, "CHb")
    p.dma("sp", CHb[:], ch_d.rearrange("(o n) -> o n", o=1).to_broadcast([128, 16]), [], [CHb])
    M4 = p.sb([128, 128], F32, "M4")
    p.dma("sp", M4[:], m4_d, [], [M4])
    for jj in range(2):
        p.v("dve", "tensor_tensor", [BW, CHb], [BW], BW[:, jj, :, :], BW[:, jj, :, :],
            CHb[:].unsqueeze(2).to_broadcast([128, 16, 128]), ALU.subtract)
    p.v("dve", "tensor_tensor", [G8, CHb], [G8], G8[:], G8[:], CHb[:].unsqueeze(2).to_broadcast([128, 16, 15]), ALU.subtract)
    p.v("dve", "tensor_scalar", [G8], [G8], G8[:], G8[:], 8.0, None, ALU.mult)

    KSE = [p.sb([128, T], BF16, f"KSE{g}") for g in range(4)]
    for g in range(4):
        oh = 1 - g % 2
        p.dma("pool", KSE[g][oh * 64:(oh + 1) * 64, :], ee_d, [], [KSE[g]])
    VS = p.sb([128, NT, 4, 65], BF16, "VS")
    KW = p.sb([128, 5, 2, 128], BF16, "KW")
    VW = p.sb([128, 5, 4, 65], BF16, "VW")
    KC = p.sb([128, 2, 256], BF16, "KC")
    VC = p.sb([128, 2, 4, 64], BF16, "VC")
    p.v("pool", "memset", [], [VS], VS[:, :, :, 64:65], 1.0)
    p.v("pool", "memset", [], [VW], VW[:, :, :, 64:65], 1.0)
    p.v("pool", "memset", [], [VC], VC[:], 0.0)
    p.v("pool", "memset", [], [KC], KC[:], 0.0)
    rawr = [p.sb([64, 16, 9, 4], BF16, f"rawr{kv}") for kv in range(2)]
    for kv in range(2):
        p.v("pool", "memset", [], [rawr[kv]], rawr[kv][:], 0.0)
    cb = p.sb([128, 2], F32, "cb")
    hid = [p.sb([128, 32], BF16, f"hid{kv}") for kv in range(2)]
    for kv in range(2):
        p.v("pool", "memset", [], [hid[kv]], hid[kv][:], 0.0)
    vtmp = p.sb([8, 4, 64], BF16, "vtmp")
    xs2 = [p.sb([128, D], F32, f"xs{a_}") for a_ in range(2)]
    xb = p.sb([128, D], BF16, "xb")
    xT = p.sb([128, NCH, 128], BF16, "xT")
    QM = [p.sb([128, 4, 128], BF16, f"QM{g}") for g in range(4)]
    for g in range(4):
        p.v("pool", "memset", [], [QM[g]], QM[g][:], 0.0)
    gt = p.sb([128, 48], F32, "gt")
    gt3 = gt[:].rearrange("p (h b) -> p h b", b=3)
    sc = [p.sb([128, 4, 128], F32, f"sc{a}") for a in range(2)]
    e = [p.sb([128, 4, 128], BF16, f"e{a}") for a in range(4)]
    MTs = p.sb([128, 2, 128], BF16, "MTs")
    ef = p.sb([128, 4, 256], F32, "ef")
    eb = p.sb([128, 4, 256], BF16, "eb")
    p.v("pool", "memset", [], [eb], eb[:], 0.0)
    ebT = p.sb([128, 8, 128], BF16, "ebT")
    rs = p.sb([128, 16], F32, "rs")
    p.v("pool", "memset", [], [rs], rs[:], 0.0)
    fac = [p.sb([128, 8], F32, f"fac{a}") for a in range(2)]
    Pg = p.sb([128, 256], F32, "Pg")
    impb = p.sb([128, 4, 64], F32, "impb")
    impw = p.sb([128, 64], F32, "impw")
    mx = p.sb([128, 16], F32, "mx")
    selm = p.sb([128, 4, 64], BF16, "selm")
    oacc = p.sb([128, 16, 64], F32, "oacc")
    otmp = [p.sb([128, 4, 64], F32, f"otmp{a}") for a in range(2)]
    o_tok = p.sb([128, D], BF16, "otok")
    oT = p.sb([128, NCH, 128], BF16, "oT")
    ptr = p.ps([128, NCH, 128], BF16, "ptr")
    PA = [p.ps([128, 512], F32, f"PA{a}") for a in range(2)]
    PB = p.ps([128, 512], F32, "PB")
    psb = [p.ps([128, 512], F32, f"ps{a}") for a in range(2)]
    pacc = [p.ps([128, 512], F32, f"pacc{a}") for a in range(2)]
    cnt = {"ps": 0, "acc": 0, "sc": 0}
    print("nsa sbuf remaining:", nc.sbuf_bytes_remaining)

    p.pe_serial = True
    for kv in range(2):
        for l in range(32):
            p.mm(PB[:, kv:kv + 1], w1[kv][0:64, l, :], posT[0:64, kv, l:l + 1], l == 0, l == 31, [w1[kv], posT], [PB])
    p.pe_serial = False
    p.v("dve", "tensor_copy", [PB], [cb], cb[:], PB[:, 0:2])

    def next_ps():
        k = cnt["ps"] % 2
        cnt["ps"] += 1
        return k

    def next_acc():
        k = cnt["acc"] % 2
        cnt["acc"] += 1
        return pacc[k], k

    def finish_branch(pob, k, g, br, first):
        pov = pob[:, 0:260].rearrange("p (m d) -> p m d", m=4, d=65)
        fc = fac[k]
        p.v("dve", "reciprocal", [pob], [fc], fc[:, 0:4], pov[:, :, 64])
        p.v("dve", "tensor_tensor", [fc, gt], [fc], fc[:, 4:8], fc[:, 0:4], gt3[:, 4 * g:4 * g + 4, br], ALU.mult)
        dst = oacc[:, 4 * g:4 * g + 4, :] if first else otmp[k][:]
        p.v("dve", "tensor_tensor", [pob, fc], [oacc if first else otmp[k]], dst, pov[:, :, 0:64],
            fc[:, 4:8].unsqueeze(2).to_broadcast([128, 4, 64]), ALU.mult)
        if not first:
            p.v("pool", "tensor_tensor", [oacc, otmp[k]], [oacc], oacc[:, 4 * g:4 * g + 4, :], oacc[:, 4 * g:4 * g + 4, :],
                otmp[k][:], ALU.add)

    pending = None
    for i in range(NT):
        tsl = slice(i * 128, (i + 1) * 128)
        xs = xs2[i % 2]
        load_x_tile(p, C, x_in[tsl, :], xs, xb, ptr, xT)
        for j in range(8):
            pb = PA[j // 4]
            for c in range(NCH):
                p.mm(pb[:, (j % 4) * 128:(j % 4 + 1) * 128], Wqp[:, c, j, :, :], xT[:, c, :], c == 0, c == NCH - 1,
                     [Wqp, xT], [pb])
        for hf in range(2):
            for par in range(2):
                rs_ = slice(par * 64, par * 64 + 64)
                p.act(QM[2 * hf + par][rs_, :, :], PA[hf][rs_, :].rearrange("p (m n) -> p m n", m=4), AF.Copy, [PA[hf]],
                      [QM[2 * hf + par]])
        def fm(pt, off):
            for pr in range(2):
                for c in range(NCH):
                    p.mm(pt[:, pr * 128:(pr + 1) * 128], Wr[:, c, off + pr * 128:off + (pr + 1) * 128], xT[:, c, :],
                         c == 0, c == NCH - 1, [Wr, xT], [pt])
        for kv, off in ((0, OKC), (1, OVC)):
            pt = PB if kv == 0 else psb[0]
            p.v("dve", "tensor_copy", [rawr[kv]], [rawr[kv]], rawr[kv][:, :, 0, :], rawr[kv][:, :, 8, :])
            for g in range(4):
                for c in range(NCH):
                    p.mm(pt[0:64, g * 128:(g + 1) * 128], Wr[:, c, off + g * 64:off + (g + 1) * 64], xT[:, c, :],
                         c == 0, c == NCH - 1, [Wr, xT], [pt])
            p.v("dve", "tensor_copy", [pt], [rawr[kv]], rawr[kv][:, :, 1:9, :],
                pt[0:64, :].rearrange("p (g k b) -> p b k g", g=4, k=8, b=16))
        fm(psb[1], OKS)
        for g in range(4):
            rs_ = slice((g % 2) * 64, (g % 2) * 64 + 64)
            p.act(KSE[g][rs_, tsl], psb[1][rs_, (g // 2) * 128:(g // 2 + 1) * 128], AF.Copy, [psb[1]], [KSE[g]])
        fm(PB, OKW)
        p.act(KW[:, i % 5, :, :], PB[:, 0:256], AF.Copy, [PB], [KW])
        for c in range(NCH):
            p.mm(PA[0][:, 0:256], xT[:, c, :], Wr[:, c, OVS:OVS + 256], c == 0, c == NCH - 1, [Wr, xT], [PA[0]])
        p.v("dve", "tensor_copy", [PA[0]], [VS], VS[:, i, :, 0:64], PA[0][:, 0:256])
        for c in range(NCH):
            p.mm(PA[1][:, 0:256], xT[:, c, :], Wr[:, c, OVW:OVW + 256], c == 0, c == NCH - 1, [Wr, xT], [PA[1]])
        p.v("dve", "tensor_copy", [PA[1]], [VW], VW[:, i % 5, :, 0:64], PA[1][:, 0:256])
        for c in range(NCH):
            p.mm(PA[0][:, 256:304], xT[:, c, :], Wr[:, c, OGT:OGT + 48], c == 0, c == NCH - 1, [Wr, xT], [PA[0]])
        p.act(gt[:], PA[0][:, 256:304], AF.Sigmoid, [PA[0]], [gt])
        j0 = 1 if i == 0 else 0
        n_lo = 8 * i - 1 + j0
        n_hi = 8 * i + 6
        cn = n_hi - n_lo + 1
        if "cmp" not in SKIP:
            for l in range(32):
                a_, b_ = divmod(l, 16)
                for kv in range(2):
                    p.mm(PB[:, kv * 32:(kv + 1) * 32], w1[kv][0:64, l, :],
                         rawr[kv][:, b_, a_:a_ + 8, :].rearrange("p k g -> p (k g)"), l == 0 and kv == 0, l == 31,
                         [w1[kv], rawr[kv]], [PB], skip=True)
            for kv in range(2):
                p.act(hid[kv][:], PB[:, kv * 32:(kv + 1) * 32], AF.Gelu_apprx_tanh, [PB, cb], [hid[kv]], bias=cb[:, kv:kv + 1])
        p.mm(PB[:, 64:96], w2k[:], hid[0][:], True, True, [w2k, hid[0]], [PB])
        pk2 = PB[:, 64:96].rearrange("p (j g) -> p g j", g=4)
        for hf in range(2):
            hs = slice(hf * 64, hf * 64 + 64)
            p.v("dve", "tensor_copy", [PB], [KC], KC[hs, :, n_lo:n_hi + 1], pk2[hs, hf::2, j0:8])
        pv2 = PB[:, 128:384].rearrange("p (g d) -> p g d", g=4)
        p.pe_serial = True
        for g in range(4):
            p.mm(pv2[0:cn, g, :], hid[1][:, j0 * 4 + g:32:4], w2v[:], True, True, [w2v, hid[1]], [PB])
        p.pe_serial = False
        p.v("dve", "tensor_copy", [PB], [vtmp], vtmp[0:cn, :, :], pv2[0:cn, :, :])
        na = max(0, min(n_hi, 127) - n_lo + 1) if n_lo < 128 else 0
        if na > 0:
            p.dma("sp", VC[n_lo:n_lo + na, 0, :, :], vtmp[0:na, :, :], [vtmp], [VC])
        if cn - na > 0:
            st0 = n_lo + na - 128
            p.dma("sp", VC[st0:st0 + cn - na, 1, :, :], vtmp[na:cn, :, :], [vtmp], [VC])
        if pending is not None:
            pending()
            pending = None
        nv = 8 * i + 7
        nb0 = max(0, nv - 15)
        q0 = 15 - (nv - nb0)
        ntl = 1 if nv <= 128 else 2
        sel_on = i >= 8
        if sel_on:
            p.v("pool", "memset", [], [impb], impb[:], -1e9)
        for g in range(4 if "cattn" not in SKIP else 0):
            hs = slice((g % 2) * 64, (g % 2) * 64 + 64)
            pr = g // 2
            p.v("pool", "memset", [], [rs], rs[:, 0:4], 0.0)
            for hp in range(2):
                k = next_ps()
                Sc = psb[k][:].rearrange("p (m n) -> p m n", m=2)
                for m2 in range(2):
                    m = hp * 2 + m2
                    h = 4 * g + m
                    p.mm(Sc[:, m2, 0:nv], QM[g][hs, m, :], KC[hs, pr, 0:nv], True, True, [QM[g], KC], [psb[k]])
                    p.v("dve", "tensor_tensor", [psb[k], G8], [psb[k]], Sc[:, m2, nb0:nv], Sc[:, m2, nb0:nv], G8[:, h, q0:15], ALU.add)
                    p.act(ef[:, m, 0:nv], Sc[:, m2, 0:nv], AF.Exp, [psb[k]], [ef, rs], scale=0.125, accum_out=rs[:, m:m + 1])
            p.v("dve", "tensor_scalar", [rs], [rs], rs[:, 8:12], rs[:, 0:4], 1e-30, None, ALU.add)
            p.v("dve", "reciprocal", [rs], [rs], rs[:, 4:8], rs[:, 8:12])
            p.v("pool", "tensor_copy", [ef], [eb], eb[:, :, 0:nv], ef[:, :, 0:nv])
            if sel_on:
                p.v("dve", "tensor_scalar", [ef, rs], [Pg], Pg[:, 0:nv], ef[:, 0, 0:nv], rs[:, 4:5], None, ALU.mult)
                for m in range(1, 4):
                    p.v("dve", "scalar_tensor_tensor", [ef, rs, Pg], [Pg], out=Pg[:, 0:nv], in0=ef[:, m, 0:nv],
                        scalar=rs[:, 4 + m:5 + m], in1=Pg[:, 0:nv], op0=ALU.mult, op1=ALU.add)
                nj = 2 * i
                P4 = Pg[:, 0:4 * nj].rearrange("p (j r) -> p j r", r=4)
                p.v("dve", "reduce_sum", [Pg], [impb], impb[:, g, 0:nj], P4, AX.X)
                p.v("dve", "tensor_tensor", [Pg, impb], [impb], impb[:, g, 1:nj], impb[:, g, 1:nj], P4[:, 0:nj - 1, 3], ALU.add)
            for nt in range(ntl):
                for m in range(4):
                    p.tr(ptr[:, nt * 4 + m, :], eb[:, m, nt * 128:(nt + 1) * 128], C.ident[:], [eb, C.ident], [ptr])
            p.v("dve", "tensor_copy", [ptr], [ebT], ebT[:, 0:4 * ntl, :], ptr[:, 0:4 * ntl, :])
            pob, ka = next_acc()
            pov = pob[:, 0:260].rearrange("p (m d) -> p m d", m=4, d=65)
            for m in range(4):
                for nt in range(ntl):
                    p.mm(pov[:, m, 0:64], ebT[:, nt * 4 + m, :], VC[:, nt, g, :], m == 0 and nt == 0, nt == ntl - 1,
                         [ebT, VC], [pob], skip=True)
            fc = fac[ka]
            p.v("dve", "tensor_tensor", [rs, gt], [fc], fc[:, 4:8], rs[:, 4:8], gt3[:, 4 * g:4 * g + 4, 0], ALU.mult)
            p.v("dve", "tensor_tensor", [pob, fc], [oacc], oacc[:, 4 * g:4 * g + 4, :], pov[:, :, 0:64],
                fc[:, 4:8].unsqueeze(2).to_broadcast([128, 4, 64]), ALU.mult)
        if sel_on:
            p.v("pool", "memset", [impb], [impb], impb[:, :, 0:1], 1e9)
            p.v("pool", "memset", [impb], [impb], impb[0:64, :, 2 * i - 1:2 * i], 2e9)
            p.v("pool", "memset", [impb], [impb], impb[0:64, :, 2 * i:2 * i + 1], 3e9)
            p.v("pool", "memset", [impb], [impb], impb[0:64, :, 2 * i + 1:2 * i + 2], -1e9)
            p.v("pool", "memset", [impb], [impb], impb[64:128, :, 2 * i:2 * i + 1], 2e9)
            p.v("pool", "memset", [impb], [impb], impb[64:128, :, 2 * i + 1:2 * i + 2], 3e9)
            for g in range(4):
                p.v("dve", "max", [impb], [mx], mx[:, 0:8], impb[:, g, :])
                p.v("dve", "match_replace", [mx, impb], [impw], out=impw[:], in_to_replace=mx[:, 0:8], in_values=impb[:, g, :],
                    imm_value=-3e9)
                p.v("dve", "max", [impw], [mx], mx[:, 8:16], impw[:])
                p.v("dve", "tensor_scalar", [impb, mx], [selm], selm[:, g ^ 1, :], impb[:, g, :], mx[:, 15:16], None, ALU.is_ge)
            for j in range(2):
                p.tr(ptr[:, j, :], selm[:, 2 * j:2 * j + 2, :].rearrange("p g b -> p (g b)"), C.ident[:], [selm, C.ident], [ptr])
            p.v("dve", "tensor_copy", [ptr], [MTs], MTs[:], ptr[:, 0:2, :])
            for g in range(4):
                oh = 1 - g % 2
                rs_ = slice(oh * 64, oh * 64 + 64)
                p.v("pool", "tensor_scalar", [MTs], [QM[g]], QM[g][rs_, :, :],
                    MTs[rs_, g // 2, :].unsqueeze(1).to_broadcast([64, 4, 128]), 1.0, 30000.0, ALU.subtract, ALU.mult)
        items = []
        for g in range(4):
            for br in (1, 2):
                if ("sel" in SKIP and br == 1) or ("win" in SKIP and br == 2):
                    continue
                kts = list(range(0, i + 1)) if br == 1 else list(range(max(0, i - 4), i + 1))
                for idx, kt in enumerate(kts):
                    items.append((g, br, kt, idx == 0, idx == len(kts) - 1))
        SBK = [psb[0], psb[1], PA[0], PA[1]]
        NB = 4
        LA = int(os.environ.get("NSA_LA", "2"))

        def kv_of(g, br, kt):
            hs = slice((g % 2) * 64, (g % 2) * 64 + 64)
            pr = g // 2
            if br == 1:
                return KSE[g][:, kt * 128:(kt + 1) * 128], VS[:, kt, g, :], KSE[g], VS
            return KW[hs, kt % 5, pr, :], VW[:, kt % 5, g, :], KW, VW

        def stage_a(n):
            g, br, kt, first, last = items[n]
            k = n % NB
            hs = slice((g % 2) * 64, (g % 2) * 64 + 64)
            pr = g // 2
            kk_, vv_, kbuf, vbuf = kv_of(g, br, kt)
            S4 = SBK[k][:].rearrange("p (m n) -> p m n", m=4)
            if br == 1:
                p.mm(S4, kk_, QM[g][:, :, :], True, True, [kbuf, QM[g]], [SBK[k]])
            else:
                p.mm(S4, kk_, QM[g][hs, :, :], True, True, [kbuf, QM[g]], [SBK[k]])
            jj = i - kt
            if jj <= 1 or (br == 2 and jj == 4):
                q = cnt["sc"] % 2
                cnt["sc"] += 1
                in1 = BW[:, jj, 4 * g:4 * g + 4, :] if jj <= 1 else M4[:].unsqueeze(1).to_broadcast([128, 4, 128])
                p.v("dve", "scalar_tensor_tensor", [SBK[k], BW, M4], [sc[q]], out=sc[q][:], in0=S4, scalar=0.125,
                    in1=in1, op0=ALU.mult, op1=ALU.add)
                p.act(e[k][:], sc[q][:], AF.Exp, [sc[q]], [e[k]])
            elif "exp" not in SKIP:
                p.act(e[k][:], S4, AF.Exp, [SBK[k]], [e[k]], scale=0.125)

        def stage_c(n):
            g, br, kt, first, last = items[n]
            k = n % NB
            kk_, vv_, kbuf, vbuf = kv_of(g, br, kt)
            qacc = (g * 2 + br) % 2
            pob = pacc[qacc]
            pov = pob[:, 0:260].rearrange("p (m d) -> p m d", m=4, d=65)
            src = e[k]
            for m in range(4 if "pv" not in SKIP else 1):
                p.mm(pov[:, m, :], src[:, m, :], vv_, first and m == 0, last, [src, vbuf], [pob], skip=True)
            if last:
                finish_branch(pob, qacc, g, br, False)

        NI = len(items)
        for n in range(NI + LA):
            if n < NI:
                stage_a(n)
            if n - LA >= 0:
                stage_c(n - LA)
        def tail(xs=xs, tsl=tsl):
            p.act(o_tok[:], oacc[:].rearrange("p h d -> p (h d)"), AF.Copy, [oacc], [o_tok])
            out_proj_ln(p, C, o_tok, ptr, oT, Wo, PA[0], PA[1], xs, x_out[tsl, :])

        pending = tail
    pending()
    p.barrier()
    p.pop()


def build(T, plan, extra_inputs):
    nc = bass.Bass("TRN2", target_bir_lowering=False)
    dr = {}

    def din(name, shape):
        dr[name] = nc.dram_tensor(name, list(shape), F32, kind="ExternalInput").ap()
        return dr[name]

    x = din("x", [T, D])
    for name, shape in extra_inputs:
        din(name, shape)
    y = nc.dram_tensor("y", [T, D], F32, kind="ExternalOutput").ap()
    bufs = [nc.dram_tensor(f"act{i}", [T, D], F32, kind="Internal").ap() for i in range(2)]
    with ExitStack() as st:
        p = Prog(nc, st)
        C = setup_common(p)
        cur = x
        for i, ph in enumerate(plan):
            dst = y if i == len(plan) - 1 else bufs[i % 2]
            if ph["kind"] == "ffn":
                L, w = ph["layer"], ph["which"]
                ffn_phase(p, C, T, cur, dst, dr[f"ffn{w}_w_gate"][L], dr[f"ffn{w}_w_up"][L], dr[f"ffn{w}_w_down"][L],
                          dr["ln_gain"][L, 0 if w == 1 else 2], dr["ln_bias"][L, 0 if w == 1 else 2])
            elif ph["kind"] == "swa":
                L = ph["layer"]
                swa_phase(p, C, T, cur, dst, dr["swa_w_in"][0], dr["swa_w_out"][0], dr["swa_bias"], dr["swa_sinks_p"],
                          dr["ln_gain"][L, 1], dr["ln_bias"][L, 1])
            elif ph["kind"] == "hgrn":
                L = ph["layer"]
                hgrn_phase(p, C, T, cur, dst, L, dr["hgrn_w_in"][0], dr["hgrn_w_out"][0], dr["hgrn_norm_gain"][0], dr["hgrn_lb"],
                           dr["hgrn_c"], dr["ln_gain"][L, 1], dr["ln_bias"][L, 1])
            elif ph["kind"] == "nsa":
                L, sl = ph["layer"], ph["slot"]
                nsa_phase(p, C, T, cur, dst, dr["nsa_w_in"][sl], dr["nsa_w_out"][sl], dr["nsa_cmp_pos"][sl], dr["nsa_cmp_w1"][sl],
                          dr["nsa_cmp_w2"][sl], dr["nsa_bw"], dr["nsa_gc"], dr["nsa_ch"], dr["nsa_m4"], dr["nsa_ee"],
                          dr["ln_gain"][L, 1], dr["ln_bias"][L, 1])
            elif ph["kind"] == "dummy_dve":
                for _ in range(ph["n"]):
                    p.v("dve", "memset", [], [C.mv[0]], C.mv[0][:, 0:1], 0.0)
                p.barrier()
            elif ph["kind"] == "dummy":
                for _ in range(ph["n"]):
                    p.dma("sp", dst, cur, [], [])
                p.barrier()
            else:
                raise ValueError(ph)
            cur = dst
        p.barrier()
        print("instructions:", p.nins, {k: p.cnt[k] for k in p.cnt}, "waits:", p.nwait)
    return nc


SEQ = 4096
BATCH = 8
_NC_CACHE = {}


def full_plan():
    plan = []
    for L in range(DEPTH):
        plan.append({"kind": "ffn", "layer": L, "which": 1})
        kind = L % 3
        if kind == 0:
            plan.append({"kind": "nsa", "layer": L, "slot": L // 3})
        elif kind == 1:
            plan.append({"kind": "hgrn", "layer": L})
        else:
            plan.append({"kind": "swa", "layer": L})
        plan.append({"kind": "ffn", "layer": L, "which": 2})
    return plan


def kernel(**inputs):
    inp = {k: np.ascontiguousarray(np.asarray(v, dtype=np.float32)) for k, v in inputs.items()}
    x = inp.pop("x")
    B, T, _ = x.shape
    swa_bias, swa_sp = swa_consts(inp["rel_bias"], inp["swa_sinks"][0])
    bw, gc, ch, m4 = nsa_consts(inp["rel_bias"])
    shared = dict(inp)
    shared.pop("rel_bias")
    shared.pop("swa_sinks")
    shared.update({"swa_bias": swa_bias, "swa_sinks_p": swa_sp, "nsa_bw": bw, "nsa_gc": gc, "nsa_ch": ch, "nsa_m4": m4, "nsa_ee": nsa_eexp(T),
                   "hgrn_c": hgrn_consts()})
    extra = [(k, v.shape) for k, v in shared.items()]
    key = (T,)
    if key not in _NC_CACHE:
        _NC_CACHE[key] = build(T, full_plan(), extra)
    nc = _NC_CACHE[key]
    in_maps = [dict(shared, x=np.ascontiguousarray(x[b])) for b in range(B)]
    res = run_bass_kernel_spmd(nc, in_maps, core_ids=list(range(B)))
    return np.stack([np.asarray(r["y"], dtype=np.float32) for r in res.results], axis=0)
```

```python
import math
import numpy as np
import ml_dtypes
from contextlib import ExitStack
import concourse.bass as bass
import concourse.mybir as mybir
from concourse.bass_utils import run_bass_kernel_spmd

F32 = mybir.dt.float32
BF16 = mybir.dt.bfloat16
AF = mybir.ActivationFunctionType
ALU = mybir.AluOpType
AX = mybir.AxisListType

D = 1024
DFF = 2816
DEPTH = 4
NCH = D // 128
NFC = DFF // 128
ALPHA = (2.0 * DEPTH) ** 0.25
LN_EPS = 1e-5
NDS = 24
NEG = -30000.0
EMBED_WAITS = True
import os
EMBED_ENG = set(os.environ.get('EMBED_ENG', 'pe,dve,act,pool').split(','))


class Buf:
    __slots__ = ("name", "w", "r")

    def __init__(self, name):
        self.name = name
        self.w = None
        self.r = {}


class Tile:
    def __init__(self, t, name):
        self.t = t
        self.b = Buf(name)
        self.subs = {}

    def __getitem__(self, idx):
        return self.t[idx]

    def sub(self, key):
        if key not in self.subs:
            self.subs[key] = Buf(f"{self.b.name}.{key}")
        return self.subs[key]


def _bufs(xs):
    out = []
    for x in xs:
        if x is None:
            continue
        if isinstance(x, Buf):
            out.append(x)
        else:
            out.append(x.b)
            out.extend(x.subs.values())
    return out


class Prog:
    def __init__(self, nc, stack):
        self.nc = nc
        self.stacks = [stack]
        self.E = {"pe": nc.tensor, "dve": nc.vector, "act": nc.scalar, "pool": nc.gpsimd, "sp": nc.sync}
        self.sem = {k: stack.enter_context(nc.semaphore("s_" + k)) for k in self.E}
        self.cnt = {k: 0 for k in self.E}
        self.known = {k: {} for k in self.E}
        self.dsems = [stack.enter_context(nc.semaphore(f"dq{i}")) for i in range(NDS)]
        self.dcnt = [0] * NDS
        self.dnext = 0
        self.dnext2 = 0
        self.nwait = {}
        self.pe_serial = False
        self.nins = 0
        self.uid = 0

    def push(self):
        st = ExitStack()
        st.__enter__()
        self.stacks.append(st)

    def pop(self):
        st = self.stacks.pop()
        st.__exit__(None, None, None)

    def sb(self, shape, dt, name):
        self.uid += 1
        nm = f"{name}_{self.uid}"
        t = self.stacks[-1].enter_context(self.nc.sbuf_tensor(nm, list(shape), dt))
        return Tile(t, nm)

    def ps(self, shape, dt, name):
        self.uid += 1
        nm = f"{name}_{self.uid}"
        t = self.stacks[-1].enter_context(self.nc.psum_tensor(nm, list(shape), dt))
        return Tile(t, nm)

    def _wait(self, e, tok):
        sem, v, key = tok
        if self.known[e].get(key, 0) >= v:
            return
        self.E[e].wait_ge(sem, v)
        self.known[e][key] = v
        self.nins += 1
        self.nwait[e] = self.nwait.get(e, 0) + 1

    def _need(self, e, reads, writes):
        reads = _bufs(reads)
        writes = _bufs(writes)
        need = {}

        def add(tok):
            sem, v, key = tok
            if self.known[e].get(key, 0) >= v:
                return
            if key not in need or need[key][1] < v:
                need[key] = tok

        for b in reads:
            if b.w is not None:
                add(b.w)
        for b in writes:
            if b.w is not None and not (e == "pe" and b.w[2] == "pe" and not self.pe_serial):
                add(b.w)
            for k, t in b.r.items():
                if not (e == "pe" and k == "pe"):
                    add(t)
        return reads, writes, list(need.values())

    def _issue(self, e, fn, need):
        for tok in need[:-1]:
            self._wait(e, tok)
        ins = fn()
        if need:
            sem, v, key = need[-1]
            if EMBED_WAITS:
                ins._wait_ge(sem, v)
                self.known[e][key] = v
            else:
                raise RuntimeError
        return ins

    def _post(self, tok, reads, writes):
        key = tok[2]
        for b in reads:
            b.r[key] = tok
        for b in writes:
            b.w = tok
            b.r = {}

    def op(self, e, fn, reads, writes):
        reads, writes, need = self._need(e, reads, writes)
        if e not in EMBED_ENG:
            for tok in need:
                self._wait(e, tok)
            need = []
        ins = self._issue(e, fn, need)
        self.cnt[e] += 1
        ins.then_inc(self.sem[e], 1)
        self._post((self.sem[e], self.cnt[e], e), reads, writes)
        self.nins += 1
        return ins

    def dma(self, q, out, in_, reads, writes, **kw):
        if q == "sp":
            i = self.dnext
            self.dnext = (i + 1) % 16
        else:
            i = 16 + self.dnext2
            self.dnext2 = (self.dnext2 + 1) % (NDS - 16)
        key = ("d", i)
        if self.dcnt[i] > 0:
            self._wait(q, (self.dsems[i], self.dcnt[i], key))
        reads, writes, need = self._need(q, reads, writes)
        for tok in need:
            self._wait(q, tok)
        ins = self.E[q].dma_start(out=out, in_=in_, **kw)
        self.dcnt[i] += 16
        ins.then_inc(self.dsems[i], 16)
        self._post((self.dsems[i], self.dcnt[i], key), reads, writes)
        self.nins += 1

    def barrier(self, engines=None):
        for e in (engines or self.E):
            for f in self.E:
                if f != e and self.cnt[f] > 0:
                    self._wait(e, (self.sem[f], self.cnt[f], f))
            for i in range(NDS):
                if self.dcnt[i] > 0:
                    self._wait(e, (self.dsems[i], self.dcnt[i], ("d", i)))

    def mm(self, out, lhsT, rhs, start, stop, reads, writes, skip=False):
        if skip:
            return self.op("pe", lambda: self.nc.tensor.matmul(out, lhsT, rhs, start=start, stop=stop,
                                                               skip_group_check=True), reads, writes)
        return self.op("pe", lambda: self.nc.tensor.matmul(out, lhsT, rhs, start=start, stop=stop), reads, writes)

    def tr(self, out, in_, ident, reads, writes):
        return self.op("pe", lambda: self.nc.tensor.transpose(out, in_, ident), reads, writes)

    def act(self, out, in_, func, reads, writes, **kw):
        return self.op("act", lambda: self.nc.scalar.activation(out, in_, func, **kw), reads, writes)

    def v(self, e, name, reads, writes, *a, **kw):
        eng = self.E[e]
        return self.op(e, lambda: getattr(eng, name)(*a, **kw), reads, writes)


class Ctx:
    pass


class Tile4:
    def __init__(self, tile):
        self.t = tile
        self.b = tile.b
        self.subs = tile.subs

    def __getitem__(self, idx):
        v = self.t[:].rearrange("p (m n) -> p m n", m=4)
        return v[idx]


def setup_common(p):
    nc = p.nc
    C = Ctx()
    C.identf = p.sb([128, 128], F32, "identf")
    C.ident = p.sb([128, 128], BF16, "ident")
    p.v("pool", "memset", [], [C.identf], C.identf[:], 0.0)
    p.op("pool", lambda: nc.gpsimd.affine_select(out=C.identf[:], in_=C.identf[:], pattern=[[-1, 128]],
                                                 compare_op=ALU.not_equal, fill=1.0, base=0, channel_multiplier=1),
         [C.identf], [C.identf])
    p.v("dve", "tensor_copy", [C.identf], [C.ident], C.ident[:], C.identf[:])
    C.G = p.sb([128, D], F32, "lnG")
    C.Bt = p.sb([128, D], F32, "lnB")
    C.st6 = [p.sb([128, 2, 6], F32, f"st6{i}") for i in range(2)]
    C.mv = [p.sb([128, 4], F32, f"mv{i}") for i in range(2)]
    C.ysb = None
    C.epi = 0
    return C


def load_ln(p, C, g_d, b_d, nysb=2):
    C.ysb = [p.sb([128, D], F32, f"ysb{i}") for i in range(nysb)]
    p.dma("sp", C.G[:], g_d.rearrange("(o n) -> o n", o=1).to_broadcast([128, D]), [], [C.G])
    p.dma("sp", C.Bt[:], b_d.rearrange("(o n) -> o n", o=1).to_broadcast([128, D]), [], [C.Bt])


def ln_epilogue(p, C, xs, py0, py1, yscale, out_rows):
    k = C.epi % 2
    C.epi += 1
    ysb, st6, mv = C.ysb[k % len(C.ysb)], C.st6[k], C.mv[k]
    p.act(ysb[:, 0:512], py0[:], AF.Copy, [py0], [ysb], scale=yscale)
    p.act(ysb[:, 512:1024], py1[:], AF.Copy, [py1], [ysb], scale=yscale)
    p.v("dve", "scalar_tensor_tensor", [xs, ysb], [ysb], out=ysb[:], in0=xs[:], scalar=ALPHA, in1=ysb[:],
        op0=ALU.mult, op1=ALU.add)
    for c in range(2):
        p.v("dve", "bn_stats", [ysb], [st6], st6[:, c, :], ysb[:, c * 512:(c + 1) * 512])
    p.v("dve", "bn_aggr", [st6], [mv], mv[:, 0:2], st6[:])
    p.v("dve", "tensor_scalar", [mv], [mv], mv[:, 3:4], mv[:, 1:2], LN_EPS, None, ALU.add)
    p.act(mv[:, 3:4], mv[:, 3:4], AF.Sqrt, [mv], [mv])
    p.v("dve", "reciprocal", [mv], [mv], mv[:, 2:3], mv[:, 3:4])
    p.v("dve", "tensor_scalar", [ysb, mv], [ysb], ysb[:], ysb[:], mv[:, 0:1], mv[:, 2:3], ALU.subtract, ALU.mult)
    p.v("pool", "tensor_tensor", [ysb, C.G], [ysb], ysb[:], ysb[:], C.G[:], ALU.mult)
    p.v("pool", "tensor_tensor", [ysb, C.Bt], [ysb], ysb[:], ysb[:], C.Bt[:], ALU.add)
    p.dma("sp", out_rows, ysb[:], [ysb], [])


def ffn_phase(p, C, T, x_in, x_out, wg_d, wu_d, wd_d, g_d, b_d):
    p.push()
    Wg = p.sb([128, NCH, DFF], BF16, "Wg")
    Wu = p.sb([128, NCH, DFF], BF16, "Wu")
    Wd = p.sb([128, NFC, D], BF16, "Wd")
    for c in range(NCH):
        p.dma("pool", Wg[:, c, :], wg_d[c * 128:(c + 1) * 128, :], [], [Wg.sub(c)])
    for c in range(NCH):
        p.dma("pool", Wu[:, c, :], wu_d[c * 128:(c + 1) * 128, :], [], [Wu.sub(c)])
    for f in range(NFC):
        p.dma("pool", Wd[:, f, :], wd_d[f * 128:(f + 1) * 128, :], [], [Wd.sub(f)])
    load_ln(p, C, g_d, b_d)
    xs = [[p.sb([128, D], F32, f"xs{a}{j}") for j in range(2)] for a in range(2)]
    xb = [p.sb([128, D], BF16, f"xb{j}") for j in range(2)]
    xT = [p.sb([128, NCH, 256], BF16, f"xT{a}") for a in range(2)]
    h = p.sb([128, NFC, 256], BF16, "h")
    sg = [p.sb([128, 256], F32, f"sg{a}") for a in range(2)]
    ptr = [p.ps([128, NCH, 128], BF16, f"ptr{a}") for a in range(2)]
    pgu = [p.ps([128, 2, 256], F32, f"pgu{a}") for a in range(2)]
    py = [p.ps([128, 512], F32, f"py{a}") for a in range(4)]
    NT = T // 256

    def prep(t):
        a = t % 2
        for j in range(2):
            r0 = t * 256 + j * 128
            p.dma("sp", xs[a][j][:], x_in[r0:r0 + 128, :], [], [xs[a][j]])
            p.act(xb[j][:], xs[a][j][:], AF.Copy, [xs[a][j]], [xb[j]])
            for c in range(NCH):
                p.tr(ptr[j][:, c, :], xb[j][:, c * 128:(c + 1) * 128], C.ident[:], [xb[j], C.ident], [ptr[j]])
            p.v("dve", "tensor_copy", [ptr[j]], [xT[a]], xT[a][:, :, j * 128:(j + 1) * 128], ptr[j][:])

    prep(0)
    for t in range(NT):
        a = t % 2
        for f in range(NFC):
            k = f % 2
            for c in range(NCH):
                p.mm(pgu[k][:, 0, :], Wg[:, c, f * 128:(f + 1) * 128], xT[a][:, c, :], c == 0, c == NCH - 1,
                     [Wg.sub(c), xT[a]], [pgu[k]])
            for c in range(NCH):
                p.mm(pgu[k][:, 1, :], Wu[:, c, f * 128:(f + 1) * 128], xT[a][:, c, :], c == 0, c == NCH - 1,
                     [Wu.sub(c), xT[a]], [pgu[k]])
            p.act(sg[k][:], pgu[k][:, 0, :], AF.Silu, [pgu[k]], [sg[k]])
            p.v("dve", "tensor_tensor", [sg[k], pgu[k]], [h], h[:, f, :], sg[k][:], pgu[k][:, 1, :], ALU.mult)
        if t + 1 < NT:
            prep(t + 1)
        for j in range(2):
            for hf in range(2):
                pb = py[j * 2 + hf]
                for f in range(NFC):
                    p.mm(pb[:], h[:, f, j * 128:(j + 1) * 128], Wd[:, f, hf * 512:(hf + 1) * 512], f == 0,
                         f == NFC - 1, [h, Wd.sub(f)], [pb])
            r0 = t * 256 + j * 128
            ln_epilogue(p, C, xs[a][j], py[j * 2], py[j * 2 + 1], 0.5, x_out[r0:r0 + 128, :])
    p.barrier()
    p.pop()


def _t5_bucket_np(dist):
    n = np.maximum(dist, 0)
    nf = np.maximum(n, 1).astype(np.float32)
    large = 16 + (np.log(nf / np.float32(16)) / np.float32(math.log(128 / 16)) * np.float32(16)).astype(np.int32)
    return np.where(n < 16, n, np.minimum(large, 31))


SWA_HPERM = [g * 8 + 2 * m + par for g in range(2) for par in range(2) for m in range(4)]


def swa_consts(rel_bias, sinks):
    s = np.arange(128)[:, None]
    t = np.arange(128)[None, :]
    out = np.empty((2, 128, 16, 128), np.float32)
    for jj in range(2):
        d = t - s + 128 * jj
        valid = (d >= 0) & (d < 128)
        tab = rel_bias[_t5_bucket_np(d)]
        tab = np.where(valid[:, :, None], tab, np.float32(NEG))
        out[jj] = np.transpose(tab[:, :, SWA_HPERM], (0, 2, 1))
    return out, np.ascontiguousarray(sinks[SWA_HPERM])


def load_x_tile(p, C, x_rows, xs, xb, ptr, xT_dst):
    p.dma("sp", xs[:], x_rows, [], [xs])
    p.act(xb[:], xs[:], AF.Copy, [xs], [xb])
    for c in range(NCH):
        p.tr(ptr[:, c, :], xb[:, c * 128:(c + 1) * 128], C.ident[:], [xb, C.ident], [ptr])
    p.v("dve", "tensor_copy", [ptr], [xT_dst], xT_dst[:], ptr[:])


def out_proj_ln(p, C, o_tok, ptr, oT, Wo, py0, py1, xs, out_rows):
    for c in range(NCH):
        p.tr(ptr[:, c, :], o_tok[:, c * 128:(c + 1) * 128], C.ident[:], [o_tok, C.ident], [ptr])
    p.v("dve", "tensor_copy", [ptr], [oT], oT[:], ptr[:])
    for hf, pb in enumerate((py0, py1)):
        for c in range(NCH):
            p.mm(pb[:], oT[:, c, :], Wo[:, c, hf * 512:(hf + 1) * 512], c == 0, c == NCH - 1, [oT, Wo], [pb])
    ln_epilogue(p, C, xs, py0, py1, 1.0, out_rows)


def load_w_chunks(p, W, w_d, col0, ncols, nsplit=2):
    src = w_d.rearrange("(c p) n -> p c n", p=128)
    step = NCH // nsplit
    for s in range(nsplit):
        p.dma("pool", W[:, s * step:(s + 1) * step, 0:ncols], src[:, s * step:(s + 1) * step, col0:col0 + ncols], [], [W])


def swa_phase(p, C, T, x_in, x_out, w_in_d, w_out_d, bias_d, sinks_d, g_d, b_d):
    nc = p.nc
    p.push()
    Wq = p.sb([128, NCH, 1024], BF16, "Wq")
    Wk2 = p.sb([128, NCH, 2, 2, 64], BF16, "Wk2")
    Wv = p.sb([128, NCH, 128], BF16, "Wv")
    Wo = p.sb([128, NCH, 1024], BF16, "Wo")
    load_w_chunks(p, Wq, w_in_d, 0, 1024, 4)
    src = w_in_d.rearrange("(c p) n -> p c n", p=128)
    for g in range(2):
        for dup in range(2):
            p.dma("pool", Wk2[:, :, g, dup, :], src[:, :, 1024 + g * 64:1024 + (g + 1) * 64], [], [Wk2])
    load_w_chunks(p, Wv, w_in_d, 1152, 128, 1)
    load_w_chunks(p, Wo, w_out_d, 0, 1024, 4)
    load_ln(p, C, g_d, b_d)
    BT = p.sb([128, 2, 16, 128], F32, "BT")
    for jj in range(2):
        p.dma("sp", BT[:, jj, :, :], bias_d[jj], [], [BT])
    ES = p.sb([128, 16], F32, "ES")
    p.dma("sp", ES[:], sinks_d.rearrange("(o n) -> o n", o=1).to_broadcast([128, 16]), [], [ES])
    p.act(ES[:], ES[:], AF.Exp, [ES], [ES])

    xs = [p.sb([128, D], F32, f"xs{a}") for a in range(2)]
    xb = [p.sb([128, D], BF16, f"xb{a}") for a in range(2)]
    xT = [p.sb([128, NCH, 128], BF16, f"xT{a}") for a in range(2)]
    qT = [p.sb([128, 8, 128], BF16, f"qT{a}") for a in range(2)]
    kbuf = [p.sb([128, 2, 128], BF16, f"kb{a}") for a in range(2)]
    vbuf = [p.sb([128, 2, 65], BF16, f"vb{a}") for a in range(2)]
    for a in range(2):
        p.v("dve", "memset", [], [vbuf[a]], vbuf[a][:, :, 64:65], 1.0)
    sc = [p.sb([128, 4, 128], F32, f"sc{a}") for a in range(2)]
    e = [p.sb([128, 4, 128], BF16, f"e{a}") for a in range(4)]
    den = [p.sb([128, 8], F32, f"den{a}") for a in range(2)]
    o_tok = [p.sb([128, D], BF16, f"otok{a}") for a in range(2)]
    oT = p.sb([128, NCH, 128], BF16, "oT")
    ptr = [p.ps([128, NCH, 128], BF16, f"ptr{a}") for a in range(1)]
    pq = [p.ps([128, 512], F32, f"pq{a}") for a in range(2)]
    pkv = p.ps([128, 512], F32, "pkv")
    ps = [p.ps([128, 4, 128], F32, f"ps{a}") for a in range(2)]
    po = [p.ps([128, 512], F32, f"po{a}") for a in range(2)]
    ps = ps + [Tile4(pq[0]), Tile4(pq[1])]
    NT = T // 128
    nsc = 0
    pending = None
    for i in range(NT):
        a = i % 2
        load_x_tile(p, C, x_in[i * 128:(i + 1) * 128, :], xs[a], xb[a], ptr[0], xT[a])
        for m in range(8):
            pb = pq[m // 4]
            for c in range(NCH):
                p.mm(pb[:, (m % 4) * 128:(m % 4 + 1) * 128], Wq[:, c, m * 128:(m + 1) * 128], xT[a][:, c, :], c == 0,
                     c == NCH - 1, [Wq, xT[a]], [pb])
        for g in range(2):
            for c in range(NCH):
                p.mm(pkv[:, g * 128:(g + 1) * 128], Wk2[:, c, g, :, :], xT[a][:, c, :], c == 0, c == NCH - 1,
                     [Wk2, xT[a]], [pkv])
        for c in range(NCH):
            p.mm(pkv[:, 256:384], xT[a][:, c, :], Wv[:, c, :], c == 0, c == NCH - 1, [Wv, xT[a]], [pkv])
        p.act(qT[a][:, 0:4, :], pq[0][:], AF.Copy, [pq[0]], [qT[a]])
        p.act(qT[a][:, 4:8, :], pq[1][:], AF.Copy, [pq[1]], [qT[a]])
        p.v("dve", "tensor_copy", [pkv], [kbuf[a]], kbuf[a][:], pkv[:, 0:256])
        p.v("dve", "tensor_copy", [pkv], [vbuf[a]], vbuf[a][:, :, 0:64], pkv[:, 256:384])
        if pending is not None:
            pending()
            pending = None
        kts = ([i - 1] if i > 0 else []) + [i]
        ot4 = o_tok[a][:].rearrange("p (m r d) -> p m r d", m=8, r=2, d=64)
        items = [(g, par, kt, kt == kts[0], kt == kts[-1]) for g in range(2) for par in range(2) for kt in kts]
        LA = 2

        def stage_a(n):
            g, par, kt, first, last = items[n]
            b = g * 2 + par
            jj = i - kt
            k = (nsc + n) % 4
            p.mm(ps[k][:], kbuf[kt % 2][par * 64:(par + 1) * 64, g, :],
                 qT[a][par * 64:(par + 1) * 64, g * 4:(g + 1) * 4, :], True, True, [kbuf[kt % 2], qT[a]], [ps[k]])
            p.v("dve", "scalar_tensor_tensor", [ps[k], BT], [sc[k % 2]], out=sc[k % 2][:], in0=ps[k][:], scalar=0.125,
                in1=BT[:, jj, b * 4:(b + 1) * 4, :], op0=ALU.mult, op1=ALU.add)
            p.act(e[k][:], sc[k % 2][:], AF.Exp, [sc[k % 2]], [e[k]])

        def stage_c(n):
            g, par, kt, first, last = items[n]
            b = g * 2 + par
            k = (nsc + n) % 4
            pob = po[b % 2]
            pov = pob[:, 0:260].rearrange("p (m d) -> p m d", m=4, d=65)
            for m in range(4):
                p.mm(pov[:, m, :], e[k][:, m, :], vbuf[kt % 2][:, g, :], first and m == 0, last,
                     [e[k], vbuf[kt % 2]], [pob], skip=True)
            if last:
                dn = den[b % 2]
                p.v("dve", "tensor_tensor", [pob, ES], [dn], dn[:, 0:4], pov[:, :, 64], ES[:, b * 4:(b + 1) * 4], ALU.add)
                p.v("dve", "reciprocal", [dn], [dn], dn[:, 4:8], dn[:, 0:4])
                p.v("dve", "tensor_tensor", [pob, dn], [o_tok[a]], ot4[:, g * 4:(g + 1) * 4, par, :], pov[:, :, 0:64],
                    dn[:, 4:8].unsqueeze(2).to_broadcast([128, 4, 64]), ALU.mult)

        NI = len(items)
        for n in range(NI + LA):
            if n < NI:
                stage_a(n)
            if n - LA >= 0:
                stage_c(n - LA)
        nsc += NI
        def tail(a=a, i=i):
            out_proj_ln(p, C, o_tok[a], ptr[0], oT, Wo, pq[0], pq[1], xs[a], x_out[i * 128:(i + 1) * 128, :])

        pending = tail
    pending()
    p.barrier()
    p.pop()


def hgrn_consts():
    s = np.arange(128)[:, None]
    t = np.arange(128)[None, :]
    same = (s // 64) == (t // 64)
    U = (same & (s <= t)).astype(np.float32)
    Emid = (same & ((s % 64) <= 31)).astype(np.float32)
    urhs = np.zeros((128, 134), np.float32)
    urhs[:, 0:128] = U - Emid
    for c in range(2):
        urhs[:, 128 + c] = ((s[:, 0] // 64 == c) & ((s[:, 0] % 64) <= 31)).astype(np.float32)
        urhs[:, 130 + c] = (s[:, 0] // 64 == c).astype(np.float32)
    urhs[:, 132] = urhs[:, 63]
    urhs[:, 133] = urhs[:, 127]
    mneg = -(U - Emid)
    return np.concatenate([urhs, mneg, U], axis=1).astype(np.float32)


def hgrn_phase(p, C, T, x_in, x_out, layer, w_in_d, w_out_d, gain_d, lb_d, hc_d, g_d, b_d):
    nc = p.nc
    p.push()
    W = [p.sb([128, NCH, 1024], BF16, f"Wh{j}") for j in range(4)]
    Wo = p.sb([128, NCH, 1024], BF16, "Wo")
    for j in range(4):
        load_w_chunks(p, W[j], w_in_d, j * 1024, 1024, 4)
    load_w_chunks(p, Wo, w_out_d, 0, 1024, 4)
    load_ln(p, C, g_d, b_d)
    HC = p.sb([128, 390], F32, "HC")
    p.dma("sp", HC[:], hc_d, [], [HC])
    Urhs, Mneg, Mbd = HC[:, 0:134], HC[:, 134:262], HC[:, 262:390]
    Gn = p.sb([128, 128], F32, "Gn")
    p.dma("sp", Gn[:], gain_d.rearrange("(o n) -> o n", o=1).to_broadcast([128, 128]), [], [Gn])
    LBb = p.sb([128, D], F32, "LBb")
    OMLb = p.sb([128, D], F32, "OMLb")
    lbT = p.sb([128, 16], F32, "lbT")
    p.push()
    L4 = p.sb([128, 4, D], F32, "L4")
    for j in range(4):
        p.dma("sp", L4[:, j, :], lb_d[j].rearrange("(o n) -> o n", o=1).to_broadcast([128, D]), [], [L4])
    p.act(L4[:], L4[:], AF.Exp, [L4], [L4])
    p.v("dve", "tensor_tensor", [L4], [OMLb], OMLb[:], L4[:, 0, :], L4[:, 1, :], ALU.add)
    p.v("dve", "tensor_tensor", [L4, OMLb], [OMLb], OMLb[:], OMLb[:], L4[:, 2, :], ALU.add)
    p.v("dve", "tensor_tensor", [L4, OMLb], [OMLb], OMLb[:], OMLb[:], L4[:, 3, :], ALU.add)
    p.v("dve", "reciprocal", [OMLb], [OMLb], OMLb[:], OMLb[:])
    p.v("dve", "memset", [], [LBb], LBb[:], 0.0)
    for j in range(1, layer + 1):
        p.v("dve", "tensor_tensor", [L4, LBb], [LBb], LBb[:], LBb[:], L4[:, j, :], ALU.add)
    p.v("dve", "tensor_tensor", [LBb, OMLb], [LBb], LBb[:], LBb[:], OMLb[:], ALU.mult)
    p.v("dve", "tensor_scalar", [LBb], [OMLb], OMLb[:], LBb[:], -1.0, 1.0, ALU.mult, ALU.add)
    plb = p.ps([128, 8, 128], F32, "plb")
    for h in range(8):
        p.op("pe", lambda h=h: nc.tensor.transpose(plb[:, h, :], LBb[:, h * 128:(h + 1) * 128], C.identf[:]), [LBb, C.identf], [plb])
    p.v("dve", "tensor_copy", [plb], [lbT], lbT[:, 0:8], plb[:, :, 0])
    p.v("dve", "tensor_scalar", [lbT], [lbT], lbT[:, 8:16], lbT[:, 0:8], -1.0, 1.0, ALU.mult, ALU.add)
    p.barrier()
    p.pop()

    xs = [p.sb([128, D], F32, f"xs{a}") for a in range(2)]
    xb = [p.sb([128, D], BF16, f"xb{a}") for a in range(2)]
    xT = [p.sb([128, NCH, 128], BF16, f"xT{a}") for a in range(2)]
    qT = p.sb([128, 8, 128], F32, "qT")
    smT = p.sb([128, 8, 128], F32, "smT")
    fs = p.sb([128, D], F32, "fs")
    logf = p.sb([128, D], F32, "logf")
    kk = p.sb([128, D], F32, "kk")
    vb = p.sb([128, D], BF16, "vb")
    GG = p.sb([128, D], F32, "GG")
    eD = [p.sb([128, 128], F32, f"eD{a}") for a in range(2)]
    eDn = [p.sb([128, 128], F32, f"eDn{a}") for a in range(2)]
    eDp = [p.sb([128, 128], F32, f"eDp{a}") for a in range(2)]
    ex = [p.sb([128, 8], F32, f"ex{a}") for a in range(2)]
    qz = [p.sb([128, 2, 128], BF16, f"qz{a}") for a in range(2)]
    kz = [p.sb([128, 2, 128], BF16, f"kz{a}") for a in range(2)]
    for a in range(2):
        p.v("dve", "memset", [], [qz[a]], qz[a][:], 0.0)
        p.v("dve", "memset", [], [kz[a]], kz[a][:], 0.0)
    kTt = [p.sb([128, 128], BF16, f"kTt{a}") for a in range(2)]
    aT = [p.sb([128, 128], BF16, f"aT{a}") for a in range(2)]
    Sbf = [p.sb([128, 128], BF16, f"Sbf{a}") for a in range(2)]
    T1 = p.sb([128, 128], F32, "T1")
    S = p.sb([128, 8, 128], F32, "S")
    p.v("dve", "memset", [], [S], S[:], 0.0)
    sq = p.sb([128, 4, 128], F32, "sq")
    t1 = p.sb([128, 4, 128], F32, "t1")
    ss = p.sb([128, 16], F32, "ss")
    o_tok = [p.sb([128, D], BF16, f"otok{a}") for a in range(2)]
    oT = p.sb([128, NCH, 128], BF16, "oT")
    ptr = p.ps([128, NCH, 128], BF16, "ptr")
    PA = [p.ps([128, 512], F32, f"PA{a}") for a in range(2)]
    PB = [p.ps([128, 512], F32, f"PB{a}") for a in range(2)]
    Dk = p.ps([128, 512], F32, "Dk")
    Mi = p.ps([128, 512], F32, "Mi")
    po = p.ps([128, 4, 128], F32, "po")
    NT = T // 128
    hcnt = 0
    pending = None
    for i in range(NT):
        a = i % 2
        load_x_tile(p, C, x_in[i * 128:(i + 1) * 128, :], xs[a], xb[a], ptr, xT[a])

        def proj_fm(P2, Wm):
            for h in range(8):
                pb = P2[h // 4]
                for c in range(NCH):
                    p.mm(pb[:, (h % 4) * 128:(h % 4 + 1) * 128], Wm[:, c, h * 128:(h + 1) * 128], xT[a][:, c, :],
                         c == 0, c == NCH - 1, [Wm, xT[a]], [pb])

        def proj_tm(P2, Wm):
            for hf in range(2):
                for c in range(NCH):
                    p.mm(P2[hf][:], xT[a][:, c, :], Wm[:, c, hf * 512:(hf + 1) * 512], c == 0, c == NCH - 1,
                         [Wm, xT[a]], [P2[hf]])

        proj_fm(PA, W[0])
        proj_fm(PB, W[1])
        for hf in range(2):
            p.act(qT[:, hf * 4:(hf + 1) * 4, :], PA[hf][:], AF.Silu, [PA[hf]], [qT])
            p.act(smT[:, hf * 4:(hf + 1) * 4, :], PB[hf][:], AF.Sigmoid, [PB[hf]], [smT], scale=-1.0)
        proj_tm(PA, W[1])
        proj_tm(PB, W[2])
        for hf in range(2):
            p.act(fs[:, hf * 512:(hf + 1) * 512], PA[hf][:], AF.Sigmoid, [PA[hf]], [fs])
            p.v("dve", "tensor_copy", [PB[hf]], [vb], vb[:, hf * 512:(hf + 1) * 512], PB[hf][:])
        proj_tm(PA, W[3])
        p.v("dve", "tensor_tensor", [fs, OMLb], [fs], fs[:], fs[:], OMLb[:], ALU.mult)
        p.v("dve", "tensor_tensor", [fs, LBb], [fs], fs[:], fs[:], LBb[:], ALU.add)
        p.act(logf[:], fs[:], AF.Ln, [fs], [logf])
        p.v("pool", "tensor_scalar", [fs], [kk], kk[:], fs[:], -1.0, 1.0, ALU.mult, ALU.add)
        if pending is not None:
            pending()
            pending = None
        for hf in range(2):
            p.act(GG[:, hf * 512:(hf + 1) * 512], PA[hf][:], AF.Silu, [PA[hf]], [GG])
        p.v("pool", "tensor_tensor", [GG, Gn], [GG], GG[:].rearrange("p (h v) -> p h v", h=8), GG[:].rearrange("p (h v) -> p h v", h=8),
            Gn[:].unsqueeze(1).to_broadcast([128, 8, 128]), ALU.mult)
        XB = [Dk, PA[0]]
        YB = [Mi, PA[1]]

        def hs1(h):
            k = h % 2
            hs = slice(h * 128, (h + 1) * 128)
            X = XB[k]
            p.mm(X[:, 0:134], logf[:, hs], Urhs, True, True, [logf, HC], [X])
            p.mm(X[:, 256:384], Mneg, logf[:, hs], True, True, [logf, HC], [X])
            p.act(eD[k][:], X[:, 0:128], AF.Exp, [X], [eD[k]])
            p.act(eDn[k][:], X[:, 0:128], AF.Exp, [X], [eDn[k]], scale=-1.0)
            p.act(ex[k][:, 0:6], X[:, 128:134], AF.Exp, [X], [ex[k]])
            p.act(eDp[k][:], X[:, 256:384], AF.Exp, [X], [eDp[k]])
            for c in range(2):
                cs = slice(c * 64, (c + 1) * 64)
                p.v("dve", "tensor_tensor", [qT, eD[k]], [qz[k]], qz[k][:, c, cs], qT[:, h, cs], eD[k][:, cs], ALU.mult)
                p.v("dve", "tensor_tensor", [kk, eDp[k]], [kz[k]], kz[k][cs, c, :], kk[cs, hs], eDp[k][cs, :], ALU.mult)
            p.v("dve", "scalar_tensor_tensor", [smT, lbT, eDn[k]], [kTt[k]], out=kTt[k][:], in0=smT[:, h, :],
                scalar=lbT[:, 8 + h:9 + h], in1=eDn[k][:], op0=ALU.mult, op1=ALU.mult)

        def hs2(h):
            k = h % 2
            Y = YB[k]
            for c in range(2):
                cs = slice(c * 64, (c + 1) * 64)
                p.mm(Y[:, c * 64:(c + 1) * 64], kTt[k][:], qz[k][:, c, cs], True, True, [kTt[k], qz[k]], [Y])
            p.v("dve", "tensor_tensor", [Y, HC], [aT[k]], aT[k][:], Y[:, 0:128], Mbd, ALU.mult)

        def hs3(h):
            k = h % 2
            hh = h % 4
            hs = slice(h * 128, (h + 1) * 128)
            Y = YB[k]
            for c in range(2):
                wsl = slice(128 + c * 128, 256 + c * 128)
                p.mm(Y[:, wsl], kz[k][:, c, :], vb[:, hs], True, True, [kz[k], vb], [Y])
                sb_ = Sbf[c]
                p.v("act", "mul", [S, ex[k]], [sb_], sb_[:], S[:, h, :], ex[k][:, c:c + 1])
                p.mm(po[:, hh, :], qz[k][:, c, :], sb_[:], c == 0, False, [qz[k], sb_], [po], skip=True)
                p.v("dve", "tensor_scalar", [S, ex[k]], [T1], T1[:], S[:, h, :], ex[k][:, 2 + c:3 + c], None, ALU.mult)
                p.v("dve", "scalar_tensor_tensor", [Y, ex[k], T1], [S], out=S[:, h, :], in0=Y[:, wsl],
                    scalar=ex[k][:, 4 + c:5 + c], in1=T1[:], op0=ALU.mult, op1=ALU.add)
            p.mm(po[:, hh, :], aT[k][:], vb[:, hs], False, True, [aT[k], vb], [po], skip=True)
            if hh == 3:
                g4 = h // 4
                p.act(sq[:], po[:], AF.Square, [po], [sq])
                p.v("dve", "reduce_sum", [sq], [ss], ss[:, 0:4], sq[:], AX.X)
                p.v("dve", "tensor_scalar", [ss], [ss], ss[:, 4:8], ss[:, 0:4], 1.0 / 128, 1e-6, ALU.mult, ALU.add)
                p.act(ss[:, 4:8], ss[:, 4:8], AF.Sqrt, [ss], [ss])
                p.v("dve", "reciprocal", [ss], [ss], ss[:, 8:12], ss[:, 4:8])
                p.v("dve", "tensor_tensor", [po, ss], [t1], t1[:], po[:], ss[:, 8:12].unsqueeze(2).to_broadcast([128, 4, 128]), ALU.mult)
                p.v("dve", "tensor_tensor", [t1, GG], [o_tok[a]], o_tok[a][:, g4 * 512:(g4 + 1) * 512],
                    t1[:].rearrange("p h v -> p (h v)"), GG[:, g4 * 512:(g4 + 1) * 512], ALU.mult)

        hs1(0)
        for h in range(8):
            if h + 1 < 8:
                hs1(h + 1)
            hs2(h)
            hs3(h)
        def tail(a=a, i=i):
            out_proj_ln(p, C, o_tok[a], ptr, oT, Wo, PB[0], PB[1], xs[a], x_out[i * 128:(i + 1) * 128, :])

        pending = tail
    pending()
    p.barrier()
    p.pop()


def nsa_consts(rel_bias):
    s = np.arange(128)[:, None]
    t = np.arange(128)[None, :]
    bw = np.empty((2, 128, 16, 128), np.float32)
    for jj in range(2):
        d = t - s + 128 * jj
        tab = rel_bias[_t5_bucket_np(d)]
        if jj == 0:
            tab = np.where((d >= 0)[:, :, None], tab, np.float32(NEG))
        bw[jj] = np.transpose(tab, (0, 2, 1))
    tq = np.arange(128)[:, None]
    mq = 14 - np.arange(15)[None, :]
    dc = tq + 16 * mq - 127
    gc = rel_bias[_t5_bucket_np(dc)]
    gc = np.where((dc >= 0)[:, :, None], gc, np.float32(NEG))
    gc = np.ascontiguousarray(np.transpose(gc, (0, 2, 1)))
    ch = np.ascontiguousarray(rel_bias[31])
    mask4 = np.where(s > t, np.float32(0.0), np.float32(NEG)).astype(np.float32)
    return bw, gc, ch, mask4


def nsa_eexp(T):
    return (np.arange(T)[None, :] // 64 == np.arange(64)[:, None]).astype(np.float32)


def nsa_phase(p, C, T, x_in, x_out, w_in_d, w_out_d, pos_d, w1_d, w2_d, bw_d, gc_d, ch_d, m4_d, ee_d, g_d, b_d):
    nc = p.nc
    SKIP = set(os.environ.get("NSA_SKIP", "").split(","))
    p.push()
    NT = T // 128
    srcw = w_in_d.rearrange("(c p) n -> p c n", p=128)
    Wqp = p.sb([128, NCH, 8, 2, 64], BF16, "Wqp")
    for j in range(8):
        A = (j // 4) * 8 + j % 4
        for hf, hd in enumerate((A, A + 4)):
            p.dma("pool", Wqp[:, :, j, hf, :], srcw[:, :, hd * 64:(hd + 1) * 64], [], [Wqp])
    Wr = p.sb([128, NCH, 1584], BF16, "Wr")
    for sgm in range(4):
        p.dma("pool", Wr[:, sgm * 2:(sgm + 1) * 2, :], srcw[:, sgm * 2:(sgm + 1) * 2, 1024:2608], [], [Wr])
    OKC, OVC, OKS, OVS, OKW, OVW, OGT = 0, 256, 512, 768, 1024, 1280, 1536
    Wo = p.sb([128, NCH, 1024], BF16, "Wo")
    load_w_chunks(p, Wo, w_out_d, 0, 1024, 4)
    w1 = [p.sb([128, 32, 128], BF16, f"w1_{kv}") for kv in range(2)]
    w2k = p.sb([128, 2, 64], BF16, "w2k")
    w2v = p.sb([128, 64], BF16, "w2v")
    posT = p.sb([128, 2, 32], BF16, "posT")
    for kv in range(2):
        for hf in range(2):
            p.dma("pool", w1[kv][hf * 64:(hf + 1) * 64, :, :], w1_d[kv].rearrange("l d e -> d l e"), [], [w1[kv]])
            p.dma("pool", posT[hf * 64:(hf + 1) * 64, kv, :], pos_d[kv].rearrange("l d -> d l"), [], [posT],
                  allow_slow_non_contiguous=True)
    for dup in range(2):
        p.dma("pool", w2k[:, dup, :], w2_d[0], [], [w2k])
    p.dma("pool", w2v[:], w2_d[1], [], [w2v])
    load_ln(p, C, g_d, b_d, nysb=1)
    BW = p.sb([128, 2, 16, 128], F32, "BW")
    for jj in range(2):
        p.dma("sp", BW[:, jj, :, :], bw_d[jj], [], [BW])
    G8 = p.sb([128, 16, 15], F32, "G8")
    p.dma("sp", G8[:], gc_d, [], [G8])
    CHb = p.sb([128, 16], F32, "CHb")
    p.dma("sp", CHb[:], ch_d.rearrange("(o n) -> o n", o=1).to_broadcast([128, 16]), [], [CHb])
    M4 = p.sb([128, 128], F32, "M4")
    p.dma("sp", M4[:], m4_d, [], [M4])
    for jj in range(2):
        p.v("dve", "tensor_tensor", [BW, CHb], [BW], BW[:, jj, :, :], BW[:, jj, :, :],
            CHb[:].unsqueeze(2).to_broadcast([128, 16, 128]), ALU.subtract)
    p.v("dve", "tensor_tensor", [G8, CHb], [G8], G8[:], G8[:], CHb[:].unsqueeze(2).to_broadcast([128, 16, 15]), ALU.subtract)
    p.v("dve", "tensor_scalar", [G8], [G8], G8[:], G8[:], 8.0, None, ALU.mult)

    KSE = [p.sb([128, T], BF16, f"KSE{g}") for g in range(4)]
    for g in range(4):
        oh = 1 - g % 2
        p.dma("pool", KSE[g][oh * 64:(oh + 1) * 64, :], ee_d, [], [KSE[g]])
    VS = p.sb([128, NT, 4, 65], BF16, "VS")
    KW = p.sb([128, 5, 2, 128], BF16, "KW")
    VW = p.sb([128, 5, 4, 65], BF16, "VW")
    KC = p.sb([128, 2, 256], BF16, "KC")
    VC = p.sb([128, 2, 4, 64], BF16, "VC")
    p.v("pool", "memset", [], [VS], VS[:, :, :, 64:65], 1.0)
    p.v("pool", "memset", [], [VW], VW[:, :, :, 64:65], 1.0)
    p.v("pool", "memset", [], [VC], VC[:], 0.0)
    p.v("pool", "memset", [], [KC], KC[:], 0.0)
    rawr = [p.sb([64, 16, 9, 4], BF16, f"rawr{kv}") for kv in range(2)]
    for kv in range(2):
        p.v("pool", "memset", [], [rawr[kv]], rawr[kv][:], 0.0)
    cb = p.sb([128, 2], F32, "cb")
    hid = [p.sb([128, 32], BF16, f"hid{kv}") for kv in range(2)]
    for kv in range(2):
        p.v("pool", "memset", [], [hid[kv]], hid[kv][:], 0.0)
    vtmp = p.sb([8, 4, 64], BF16, "vtmp")
    xs2 = [p.sb([128, D], F32, f"xs{a_}") for a_ in range(2)]
    xb = p.sb([128, D], BF16, "xb")
    xT = p.sb([128, NCH, 128], BF16, "xT")
    QM = [p.sb([128, 4, 128], BF16, f"QM{g}") for g in range(4)]
    for g in range(4):
        p.v("pool", "memset", [], [QM[g]], QM[g][:], 0.0)
    gt = p.sb([128, 48], F32, "gt")
    gt3 = gt[:].rearrange("p (h b) -> p h b", b=3)
    sc = [p.sb([128, 4, 128], F32, f"sc{a}") for a in range(2)]
    e = [p.sb([128, 4, 128], BF16, f"e{a}") for a in range(4)]
    MTs = p.sb([128, 2, 128], BF16, "MTs")
    ef = p.sb([128, 4, 256], F32, "ef")
    eb = p.sb([128, 4, 256], BF16, "eb")
    p.v("pool", "memset", [], [eb], eb[:], 0.0)
    ebT = p.sb([128, 8, 128], BF16, "ebT")
    rs = p.sb([128, 16], F32, "rs")
    p.v("pool", "memset", [], [rs], rs[:], 0.0)
    fac = [p.sb([128, 8], F32, f"fac{a}") for a in range(2)]
    Pg = p.sb([128, 256], F32, "Pg")
    impb = p.sb([128, 4, 64], F32, "impb")
    impw = p.sb([128, 64], F32, "impw")
    mx = p.sb([128, 16], F32, "mx")
    selm = p.sb([128, 4, 64], BF16, "selm")
    oacc = p.sb([128, 16, 64], F32, "oacc")
    otmp = [p.sb([128, 4, 64], F32, f"otmp{a}") for a in range(2)]
    o_tok = p.sb([128, D], BF16, "otok")
    oT = p.sb([128, NCH, 128], BF16, "oT")
    ptr = p.ps([128, NCH, 128], BF16, "ptr")
    PA = [p.ps([128, 512], F32, f"PA{a}") for a in range(2)]
    PB = p.ps([128, 512], F32, "PB")
    psb = [p.ps([128, 512], F32, f"ps{a}") for a in range(2)]
    pacc = [p.ps([128, 512], F32, f"pacc{a}") for a in range(2)]
    cnt = {"ps": 0, "acc": 0, "sc": 0}
    print("nsa sbuf remaining:", nc.sbuf_bytes_remaining)

    p.pe_serial = True
    for kv in range(2):
        for l in range(32):
            p.mm(PB[:, kv:kv + 1], w1[kv][0:64, l, :], posT[0:64, kv, l:l + 1], l == 0, l == 31, [w1[kv], posT], [PB])
    p.pe_serial = False
    p.v("dve", "tensor_copy", [PB], [cb], cb[:], PB[:, 0:2])

    def next_ps():
        k = cnt["ps"] % 2
        cnt["ps"] += 1
        return k

    def next_acc():
        k = cnt["acc"] % 2
        cnt["acc"] += 1
        return pacc[k], k

    def finish_branch(pob, k, g, br, first):
        pov = pob[:, 0:260].rearrange("p (m d) -> p m d", m=4, d=65)
        fc = fac[k]
        p.v("dve", "reciprocal", [pob], [fc], fc[:, 0:4], pov[:, :, 64])
        p.v("dve", "tensor_tensor", [fc, gt], [fc], fc[:, 4:8], fc[:, 0:4], gt3[:, 4 * g:4 * g + 4, br], ALU.mult)
        dst = oacc[:, 4 * g:4 * g + 4, :] if first else otmp[k][:]
        p.v("dve", "tensor_tensor", [pob, fc], [oacc if first else otmp[k]], dst, pov[:, :, 0:64],
            fc[:, 4:8].unsqueeze(2).to_broadcast([128, 4, 64]), ALU.mult)
        if not first:
            p.v("dve", "tensor_tensor", [oacc, otmp[k]], [oacc], oacc[:, 4 * g:4 * g + 4, :], oacc[:, 4 * g:4 * g + 4, :],
                otmp[k][:], ALU.add)

    pending = None
    for i in range(NT):
        tsl = slice(i * 128, (i + 1) * 128)
        xs = xs2[i % 2]
        load_x_tile(p, C, x_in[tsl, :], xs, xb, ptr, xT)
        for j in range(8):
            pb = PA[j // 4]
            for c in range(NCH):
                p.mm(pb[:, (j % 4) * 128:(j % 4 + 1) * 128], Wqp[:, c, j, :, :], xT[:, c, :], c == 0, c == NCH - 1,
                     [Wqp, xT], [pb])
        for hf in range(2):
            for par in range(2):
                rs_ = slice(par * 64, par * 64 + 64)
                p.act(QM[2 * hf + par][rs_, :, :], PA[hf][rs_, :].rearrange("p (m n) -> p m n", m=4), AF.Copy, [PA[hf]],
                      [QM[2 * hf + par]])
        def fm(pt, off):
            for pr in range(2):
                for c in range(NCH):
                    p.mm(pt[:, pr * 128:(pr + 1) * 128], Wr[:, c, off + pr * 128:off + (pr + 1) * 128], xT[:, c, :],
                         c == 0, c == NCH - 1, [Wr, xT], [pt])
        for kv, off in ((0, OKC), (1, OVC)):
            pt = PB if kv == 0 else psb[0]
            p.v("dve", "tensor_copy", [rawr[kv]], [rawr[kv]], rawr[kv][:, :, 0, :], rawr[kv][:, :, 8, :])
            for g in range(4):
                for c in range(NCH):
                    p.mm(pt[0:64, g * 128:(g + 1) * 128], Wr[:, c, off + g * 64:off + (g + 1) * 64], xT[:, c, :],
                         c == 0, c == NCH - 1, [Wr, xT], [pt])
            p.v("dve", "tensor_copy", [pt], [rawr[kv]], rawr[kv][:, :, 1:9, :],
                pt[0:64, :].rearrange("p (g k b) -> p b k g", g=4, k=8, b=16))
        fm(psb[1], OKS)
        for g in range(4):
            rs_ = slice((g % 2) * 64, (g % 2) * 64 + 64)
            p.act(KSE[g][rs_, tsl], psb[1][rs_, (g // 2) * 128:(g // 2 + 1) * 128], AF.Copy, [psb[1]], [KSE[g]])
        fm(PB, OKW)
        p.act(KW[:, i % 5, :, :], PB[:, 0:256], AF.Copy, [PB], [KW])
        for c in range(NCH):
            p.mm(PA[0][:, 0:256], xT[:, c, :], Wr[:, c, OVS:OVS + 256], c == 0, c == NCH - 1, [Wr, xT], [PA[0]])
        p.v("dve", "tensor_copy", [PA[0]], [VS], VS[:, i, :, 0:64], PA[0][:, 0:256])
        for c in range(NCH):
            p.mm(PA[1][:, 0:256], xT[:, c, :], Wr[:, c, OVW:OVW + 256], c == 0, c == NCH - 1, [Wr, xT], [PA[1]])
        p.v("dve", "tensor_copy", [PA[1]], [VW], VW[:, i % 5, :, 0:64], PA[1][:, 0:256])
        for c in range(NCH):
            p.mm(PA[0][:, 256:304], xT[:, c, :], Wr[:, c, OGT:OGT + 48], c == 0, c == NCH - 1, [Wr, xT], [PA[0]])
        p.act(gt[:], PA[0][:, 256:304], AF.Sigmoid, [PA[0]], [gt])
        j0 = 1 if i == 0 else 0
        n_lo = 8 * i - 1 + j0
        n_hi = 8 * i + 6
        cn = n_hi - n_lo + 1
        if "cmp" not in SKIP:
            for l in range(32):
                a_, b_ = divmod(l, 16)
                for kv in range(2):
                    p.mm(PB[:, kv * 32:(kv + 1) * 32], w1[kv][0:64, l, :],
                         rawr[kv][:, b_, a_:a_ + 8, :].rearrange("p k g -> p (k g)"), l == 0 and kv == 0, l == 31,
                         [w1[kv], rawr[kv]], [PB], skip=True)
            for kv in range(2):
                p.act(hid[kv][:], PB[:, kv * 32:(kv + 1) * 32], AF.Gelu_apprx_tanh, [PB, cb], [hid[kv]], bias=cb[:, kv:kv + 1])
        p.mm(PB[:, 64:96], w2k[:], hid[0][:], True, True, [w2k, hid[0]], [PB])
        pk2 = PB[:, 64:96].rearrange("p (j g) -> p g j", g=4)
        for hf in range(2):
            hs = slice(hf * 64, hf * 64 + 64)
            p.v("dve", "tensor_copy", [PB], [KC], KC[hs, :, n_lo:n_hi + 1], pk2[hs, hf::2, j0:8])
        pv2 = PB[:, 128:384].rearrange("p (g d) -> p g d", g=4)
        p.pe_serial = True
        for g in range(4):
            p.mm(pv2[0:cn, g, :], hid[1][:, j0 * 4 + g:32:4], w2v[:], True, True, [w2v, hid[1]], [PB])
        p.pe_serial = False
        p.v("dve", "tensor_copy", [PB], [vtmp], vtmp[0:cn, :, :], pv2[0:cn, :, :])
        na = max(0, min(n_hi, 127) - n_lo + 1) if n_lo < 128 else 0
        if na > 0:
            p.dma("sp", VC[n_lo:n_lo + na, 0, :, :], vtmp[0:na, :, :], [vtmp], [VC])
        if cn - na > 0:
            st0 = n_lo + na - 128
            p.dma("sp", VC[st0:st0 + cn - na, 1, :, :], vtmp[na:cn, :, :], [vtmp], [VC])
        if pending is not None:
            pending()
            pending = None
        nv = 8 * i + 7
        nb0 = max(0, nv - 15)
        q0 = 15 - (nv - nb0)
        ntl = 1 if nv <= 128 else 2
        sel_on = i >= 8
        if sel_on:
            p.v("pool", "memset", [], [impb], impb[:], -1e9)
        for g in range(4 if "cattn" not in SKIP else 0):
            hs = slice((g % 2) * 64, (g % 2) * 64 + 64)
            pr = g // 2
            p.v("pool", "memset", [], [rs], rs[:, 0:4], 0.0)
            for hp in range(2):
                k = next_ps()
                Sc = psb[k][:].rearrange("p (m n) -> p m n", m=2)
                for m2 in range(2):
                    m = hp * 2 + m2
                    h = 4 * g + m
                    p.mm(Sc[:, m2, 0:nv], QM[g][hs, m, :], KC[hs, pr, 0:nv], True, True, [QM[g], KC], [psb[k]])
                    p.v("dve", "tensor_tensor", [psb[k], G8], [psb[k]], Sc[:, m2, nb0:nv], Sc[:, m2, nb0:nv], G8[:, h, q0:15], ALU.add)
                    p.act(ef[:, m, 0:nv], Sc[:, m2, 0:nv], AF.Exp, [psb[k]], [ef, rs], scale=0.125, accum_out=rs[:, m:m + 1])
            p.v("dve", "tensor_scalar", [rs], [rs], rs[:, 8:12], rs[:, 0:4], 1e-30, None, ALU.add)
            p.v("dve", "reciprocal", [rs], [rs], rs[:, 4:8], rs[:, 8:12])
            p.v("dve", "tensor_copy", [ef], [eb], eb[:, :, 0:nv], ef[:, :, 0:nv])
            if sel_on:
                p.v("dve", "tensor_scalar", [ef, rs], [Pg], Pg[:, 0:nv], ef[:, 0, 0:nv], rs[:, 4:5], None, ALU.mult)
                for m in range(1, 4):
                    p.v("dve", "scalar_tensor_tensor", [ef, rs, Pg], [Pg], out=Pg[:, 0:nv], in0=ef[:, m, 0:nv],
                        scalar=rs[:, 4 + m:5 + m], in1=Pg[:, 0:nv], op0=ALU.mult, op1=ALU.add)
                nj = 2 * i
                P4 = Pg[:, 0:4 * nj].rearrange("p (j r) -> p j r", r=4)
                p.v("dve", "reduce_sum", [Pg], [impb], impb[:, g, 0:nj], P4, AX.X)
                p.v("dve", "tensor_tensor", [Pg, impb], [impb], impb[:, g, 1:nj], impb[:, g, 1:nj], P4[:, 0:nj - 1, 3], ALU.add)
            for nt in range(ntl):
                for m in range(4):
                    p.tr(ptr[:, nt * 4 + m, :], eb[:, m, nt * 128:(nt + 1) * 128], C.ident[:], [eb, C.ident], [ptr])
            p.v("dve", "tensor_copy", [ptr], [ebT], ebT[:, 0:4 * ntl, :], ptr[:, 0:4 * ntl, :])
            pob, ka = next_acc()
            pov = pob[:, 0:260].rearrange("p (m d) -> p m d", m=4, d=65)
            for m in range(4):
                for nt in range(ntl):
                    p.mm(pov[:, m, 0:64], ebT[:, nt * 4 + m, :], VC[:, nt, g, :], m == 0 and nt == 0, nt == ntl - 1,
                         [ebT, VC], [pob], skip=True)
            fc = fac[ka]
            p.v("dve", "tensor_tensor", [rs, gt], [fc], fc[:, 4:8], rs[:, 4:8], gt3[:, 4 * g:4 * g + 4, 0], ALU.mult)
            p.v("dve", "tensor_tensor", [pob, fc], [oacc], oacc[:, 4 * g:4 * g + 4, :], pov[:, :, 0:64],
                fc[:, 4:8].unsqueeze(2).to_broadcast([128, 4, 64]), ALU.mult)
        if sel_on:
            p.v("pool", "memset", [impb], [impb], impb[:, :, 0:1], 1e9)
            p.v("pool", "memset", [impb], [impb], impb[0:64, :, 2 * i - 1:2 * i], 2e9)
            p.v("pool", "memset", [impb], [impb], impb[0:64, :, 2 * i:2 * i + 1], 3e9)
            p.v("pool", "memset", [impb], [impb], impb[0:64, :, 2 * i + 1:2 * i + 2], -1e9)
            p.v("pool", "memset", [impb], [impb], impb[64:128, :, 2 * i:2 * i + 1], 2e9)
            p.v("pool", "memset", [impb], [impb], impb[64:128, :, 2 * i + 1:2 * i + 2], 3e9)
            for g in range(4):
                p.v("dve", "max", [impb], [mx], mx[:, 0:8], impb[:, g, :])
                p.v("dve", "match_replace", [mx, impb], [impw], out=impw[:], in_to_replace=mx[:, 0:8], in_values=impb[:, g, :],
                    imm_value=-3e9)
                p.v("dve", "max", [impw], [mx], mx[:, 8:16], impw[:])
                p.v("dve", "tensor_scalar", [impb, mx], [selm], selm[:, g ^ 1, :], impb[:, g, :], mx[:, 15:16], None, ALU.is_ge)
            for j in range(2):
                p.tr(ptr[:, j, :], selm[:, 2 * j:2 * j + 2, :].rearrange("p g b -> p (g b)"), C.ident[:], [selm, C.ident], [ptr])
            p.v("dve", "tensor_copy", [ptr], [MTs], MTs[:], ptr[:, 0:2, :])
            for g in range(4):
                oh = 1 - g % 2
                rs_ = slice(oh * 64, oh * 64 + 64)
                p.v("dve", "tensor_scalar", [MTs], [QM[g]], QM[g][rs_, :, :],
                    MTs[rs_, g // 2, :].unsqueeze(1).to_broadcast([64, 4, 128]), 1.0, 30000.0, ALU.subtract, ALU.mult)
        items = []
        for g in range(4):
            for br in (1, 2):
                if ("sel" in SKIP and br == 1) or ("win" in SKIP and br == 2):
                    continue
                kts = list(range(0, i + 1)) if br == 1 else list(range(max(0, i - 4), i + 1))
                for idx, kt in enumerate(kts):
                    items.append((g, br, kt, idx == 0, idx == len(kts) - 1))
        SBK = [psb[0], psb[1], PA[0], PA[1]]
        NB = 4
        LA = int(os.environ.get("NSA_LA", "2"))

        def kv_of(g, br, kt):
            hs = slice((g % 2) * 64, (g % 2) * 64 + 64)
            pr = g // 2
            if br == 1:
                return KSE[g][:, kt * 128:(kt + 1) * 128], VS[:, kt, g, :], KSE[g], VS
            return KW[hs, kt % 5, pr, :], VW[:, kt % 5, g, :], KW, VW

        def stage_a(n):
            g, br, kt, first, last = items[n]
            k = n % NB
            hs = slice((g % 2) * 64, (g % 2) * 64 + 64)
            pr = g // 2
            kk_, vv_, kbuf, vbuf = kv_of(g, br, kt)
            S4 = SBK[k][:].rearrange("p (m n) -> p m n", m=4)
            if br == 1:
                p.mm(S4, kk_, QM[g][:, :, :], True, True, [kbuf, QM[g]], [SBK[k]])
            else:
                p.mm(S4, kk_, QM[g][hs, :, :], True, True, [kbuf, QM[g]], [SBK[k]])
            jj = i - kt
            if jj <= 1 or (br == 2 and jj == 4):
                q = cnt["sc"] % 2
                cnt["sc"] += 1
                in1 = BW[:, jj, 4 * g:4 * g + 4, :] if jj <= 1 else M4[:].unsqueeze(1).to_broadcast([128, 4, 128])
                p.v("dve", "scalar_tensor_tensor", [SBK[k], BW, M4], [sc[q]], out=sc[q][:], in0=S4, scalar=0.125,
                    in1=in1, op0=ALU.mult, op1=ALU.add)
                p.act(e[k][:], sc[q][:], AF.Exp, [sc[q]], [e[k]])
            elif "exp" not in SKIP:
                p.act(e[k][:], S4, AF.Exp, [SBK[k]], [e[k]], scale=0.125)

        def stage_c(n):
            g, br, kt, first, last = items[n]
            k = n % NB
            kk_, vv_, kbuf, vbuf = kv_of(g, br, kt)
            qacc = (g * 2 + br) % 2
            pob = pacc[qacc]
            pov = pob[:, 0:260].rearrange("p (m d) -> p m d", m=4, d=65)
            src = e[k]
            for m in range(4 if "pv" not in SKIP else 1):
                p.mm(pov[:, m, :], src[:, m, :], vv_, first and m == 0, last, [src, vbuf], [pob], skip=True)
            if last:
                finish_branch(pob, qacc, g, br, False)

        NI = len(items)
        for n in range(NI + LA):
            if n < NI:
                stage_a(n)
            if n - LA >= 0:
                stage_c(n - LA)
        def tail(xs=xs, tsl=tsl):
            p.act(o_tok[:], oacc[:].rearrange("p h d -> p (h d)"), AF.Copy, [oacc], [o_tok])
            out_proj_ln(p, C, o_tok, ptr, oT, Wo, PA[0], PA[1], xs, x_out[tsl, :])

        pending = tail
    pending()
    p.barrier()
    p.pop()


def build(T, plan, extra_inputs):
    nc = bass.Bass("TRN2", target_bir_lowering=False)
    dr = {}

    def din(name, shape):
        dr[name] = nc.dram_tensor(name, list(shape), F32, kind="ExternalInput").ap()
        return dr[name]

    x = din("x", [T, D])
    for name, shape in extra_inputs:
        din(name, shape)
    y = nc.dram_tensor("y", [T, D], F32, kind="ExternalOutput").ap()
    bufs = [nc.dram_tensor(f"act{i}", [T, D], F32, kind="Internal").ap() for i in range(2)]
    with ExitStack() as st:
        p = Prog(nc, st)
        C = setup_common(p)
        cur = x
        for i, ph in enumerate(plan):
            dst = y if i == len(plan) - 1 else bufs[i % 2]
            if ph["kind"] == "ffn":
                L, w = ph["layer"], ph["which"]
                ffn_phase(p, C, T, cur, dst, dr[f"ffn{w}_w_gate"][L], dr[f"ffn{w}_w_up"][L], dr[f"ffn{w}_w_down"][L],
                          dr["ln_gain"][L, 0 if w == 1 else 2], dr["ln_bias"][L, 0 if w == 1 else 2])
            elif ph["kind"] == "swa":
                L = ph["layer"]
                swa_phase(p, C, T, cur, dst, dr["swa_w_in"][0], dr["swa_w_out"][0], dr["swa_bias"], dr["swa_sinks_p"],
                          dr["ln_gain"][L, 1], dr["ln_bias"][L, 1])
            elif ph["kind"] == "hgrn":
                L = ph["layer"]
                hgrn_phase(p, C, T, cur, dst, L, dr["hgrn_w_in"][0], dr["hgrn_w_out"][0], dr["hgrn_norm_gain"][0], dr["hgrn_lb"],
                           dr["hgrn_c"], dr["ln_gain"][L, 1], dr["ln_bias"][L, 1])
            elif ph["kind"] == "nsa":
                L, sl = ph["layer"], ph["slot"]
                nsa_phase(p, C, T, cur, dst, dr["nsa_w_in"][sl], dr["nsa_w_out"][sl], dr["nsa_cmp_pos"][sl], dr["nsa_cmp_w1"][sl],
                          dr["nsa_cmp_w2"][sl], dr["nsa_bw"], dr["nsa_gc"], dr["nsa_ch"], dr["nsa_m4"], dr["nsa_ee"],
                          dr["ln_gain"][L, 1], dr["ln_bias"][L, 1])
            elif ph["kind"] == "dummy_dve":
                for _ in range(ph["n"]):
                    p.v("dve", "memset", [], [C.mv[0]], C.mv[0][:, 0:1], 0.0)
                p.barrier()
            elif ph["kind"] == "dummy":
                for _ in range(ph["n"]):
                    p.dma("sp", dst, cur, [], [])
                p.barrier()
            else:
                raise ValueError(ph)
            cur = dst
        p.barrier()
        print("instructions:", p.nins, {k: p.cnt[k] for k in p.cnt}, "waits:", p.nwait)
    return nc


SEQ = 4096
BATCH = 8
_NC_CACHE = {}


def full_plan():
    plan = []
    for L in range(DEPTH):
        plan.append({"kind": "ffn", "layer": L, "which": 1})
        kind = L % 3
        if kind == 0:
            plan.append({"kind": "nsa", "layer": L, "slot": L // 3})
        elif kind == 1:
            plan.append({"kind": "hgrn", "layer": L})
        else:
            plan.append({"kind": "swa", "layer": L})
        plan.append({"kind": "ffn", "layer": L, "which": 2})
    return plan


def kernel(**inputs):
    inp = {k: np.ascontiguousarray(np.asarray(v, dtype=np.float32)) for k, v in inputs.items()}
    x = inp.pop("x")
    B, T, _ = x.shape
    swa_bias, swa_sp = swa_consts(inp["rel_bias"], inp["swa_sinks"][0])
    bw, gc, ch, m4 = nsa_consts(inp["rel_bias"])
    shared = dict(inp)
    shared.pop("rel_bias")
    shared.pop("swa_sinks")
    shared.update({"swa_bias": swa_bias, "swa_sinks_p": swa_sp, "nsa_bw": bw, "nsa_gc": gc, "nsa_ch": ch, "nsa_m4": m4, "nsa_ee": nsa_eexp(T),
                   "hgrn_c": hgrn_consts()})
    extra = [(k, v.shape) for k, v in shared.items()]
    key = (T,)
    if key not in _NC_CACHE:
        _NC_CACHE[key] = build(T, full_plan(), extra)
    nc = _NC_CACHE[key]
    in_maps = [dict(shared, x=np.ascontiguousarray(x[b])) for b in range(B)]
    res = run_bass_kernel_spmd(nc, in_maps, core_ids=list(range(B)))
    return np.stack([np.asarray(r["y"], dtype=np.float32) for r in res.results], axis=0)
```
